# Optimizing a Trainium2 kernel written in Bass

```python
import jax, jax.numpy as jnp
from jax import lax
import numpy as np

D_MODEL = 1024
BATCH = 8
SEQ = 2048
DEPTH = 2

MEM_LEN = 256
POOL_WIDTH = 512
POOL_GROUPS = 4
POOL_GROUP_DIM = POOL_WIDTH // POOL_GROUPS
POOL_WINDOWS = (2, 4, 8, 16)
MEM_HEADS = 4
MEM_HEAD_DIM = 128
MEM_WIDTH = MEM_HEADS * MEM_HEAD_DIM
RWKV_HEAD_DIM = 64
RWKV_WIDTH = D_MODEL
RWKV_HEADS = RWKV_WIDTH // RWKV_HEAD_DIM
DECAY_LORA = 64
ICLR_LORA = 64
VRES_LORA = 32
GATE_LORA = 128
RWKV_COLS = 3 * RWKV_WIDTH + DECAY_LORA + ICLR_LORA + GATE_LORA
N_BRANCH = 3
OFF_Q = POOL_WIDTH
OFF_RWKV = OFF_Q + MEM_WIDTH
OFF_GATE = OFF_RWKV + RWKV_COLS
IN_COLS = OFF_GATE + N_BRANCH * D_MODEL
D_FF = 2816
CONV_WIDTH = 3
NORM_EPS = 1e-6
LNX_EPS = 64e-5
L2_EPS = 1e-12

kernel_name = "hybrid_pool_rwkv7_memxattn_convffn"


def rms_norm(x, g):
    xf = x.astype(jnp.float32)
    y = xf * lax.rsqrt(jnp.mean(xf * xf, axis=-1, keepdims=True) + NORM_EPS)
    return (y * g.astype(jnp.float32)).astype(x.dtype)


def token_shift(z):
    return jnp.pad(z, ((0, 0), (1, 0), (0, 0)))[:, :-1]


def pool_branch(zp, pool_w, pool_b, pool_scale):
    B, S, _ = zp.shape
    p = zp.astype(jnp.float32).reshape(B, S, POOL_GROUPS, POOL_GROUP_DIM)
    cs = jnp.cumsum(p, axis=1)
    t = jnp.arange(S)
    outs = []
    for gi, win in enumerate(POOL_WINDOWS):
        c = cs[:, :, gi]
        c_lag = jnp.pad(c, ((0, 0), (win, 0), (0, 0)))[:, :S]
        cnt = jnp.minimum(t + 1, win).astype(jnp.float32)[None, :, None]
        outs.append((c - c_lag) / cnt - p[:, :, gi])
    pooled = jnp.stack(outs, axis=2).astype(zp.dtype)
    mixed = jnp.einsum('bsgc,gcd->bsgd', pooled, pool_w).reshape(B, S, POOL_WIDTH) + pool_b
    return mixed * pool_scale


def mem_attention(zq, mem_n, w_mem_kv):
    B, S, _ = zq.shape
    M = mem_n.shape[1]
    q = zq.reshape(B, S, MEM_HEADS, MEM_HEAD_DIM)
    kv = mem_n @ w_mem_kv
    k, v = jnp.split(kv, 2, axis=-1)
    k = k.reshape(B, M, MEM_HEADS, MEM_HEAD_DIM)
    v = v.reshape(B, M, MEM_HEADS, MEM_HEAD_DIM)
    s = jnp.einsum('bshd,bmhd->bhsm', q, k).astype(jnp.float32) * (MEM_HEAD_DIM ** -0.5)
    pr = jax.nn.softmax(s, axis=-1).astype(zq.dtype)
    o = jnp.einsum('bhsm,bmhd->bshd', pr, v)
    return o.reshape(B, S, MEM_WIDTH)


def rwkv7_scan(r, w, k, v, kk_neg, kk_a):
    B, _, H, N = r.shape

    def step(state, inp):
        rt, wt, kt, vt, an, bn = inp
        sa = jnp.einsum('bhvk,bhk->bhv', state, an)
        state = state * wt[:, :, None, :] + sa[..., None] * bn[:, :, None, :] + vt[..., None] * kt[:, :, None, :]
        yt = jnp.einsum('bhvk,bhk->bhv', state, rt)
        return state, yt

    xs = tuple(jnp.moveaxis(a, 1, 0) for a in (r, w, k, v, kk_neg, kk_a))
    s0 = jnp.zeros((B, H, N, N), jnp.float32)
    _, ys = lax.scan(step, s0, xs)
    return jnp.moveaxis(ys, 0, 1)


def rwkv_branch(r, k, v, dw, da, dg, w0, w_up_decay, a0, w_up_a, w_up_g, k_k, k_a, r_k, ln_x_w, ln_x_b):
    B, S, _ = r.shape
    dt = r.dtype
    f32 = jnp.float32
    w_log = -jax.nn.softplus(-(w0 + jnp.tanh(dw) @ w_up_decay).astype(f32)) - 0.5
    decay = jnp.exp(-jnp.exp(w_log))
    a = jax.nn.sigmoid((a0 + da @ w_up_a).astype(f32))
    g = jax.nn.sigmoid(dg) @ w_up_g
    hs = (B, S, RWKV_HEADS, RWKV_HEAD_DIM)
    kk = (k * k_k).astype(f32).reshape(hs)
    kk = kk * lax.rsqrt(jnp.sum(kk * kk, axis=-1, keepdims=True) + L2_EPS)
    k_eff = k.astype(f32) * (1.0 + (a - 1.0) * k_a.astype(f32))
    rh = r.astype(f32).reshape(hs)
    kh = k_eff.reshape(hs)
    vh = v.astype(f32).reshape(hs)
    wh = decay.reshape(hs)
    ah = a.reshape(hs)
    y = rwkv7_scan(rh, wh, kh, vh, -kk, kk * ah)
    mu = jnp.mean(y, axis=-1, keepdims=True)
    var = jnp.mean(jnp.square(y - mu), axis=-1, keepdims=True)
    y = ((y - mu) * lax.rsqrt(var + LNX_EPS)).reshape(B, S, RWKV_WIDTH)
    y = y * ln_x_w.astype(f32) + ln_x_b.astype(f32)
    bonus = jnp.sum(rh * kh * r_k.astype(f32), axis=-1, keepdims=True) * vh
    y = y + bonus.reshape(B, S, RWKV_WIDTH)
    return y.astype(dt) * g


def causal_dwconv(u, w, b):
    S = u.shape[1]
    up = jnp.pad(u, ((0, 0), (CONV_WIDTH - 1, 0), (0, 0)))
    out = b
    for j in range(CONV_WIDTH):
        out = out + w[j] * up[:, j:j + S]
    return out


def setup_inputs(seed: int = 0) -> dict:
    key = jax.random.key(seed)
    ks = iter(jax.random.split(key, 40))
    f32 = jnp.float32

    def nrm(shape, scale):
        return jax.random.normal(next(ks), shape, f32) * scale

    def gain(shape):
        return 1.0 + nrm(shape, 0.05)

    L, D = DEPTH, D_MODEL
    return {
        "x": nrm((BATCH, SEQ, D), 1.0),
        "mem": nrm((BATCH, MEM_LEN, D), 1.0),
        "mem_norm": gain((D,)),
        "norm_mix_pre": gain((L, D)),
        "norm_mix_post": gain((L, D)),
        "w_in": nrm((L, D, IN_COLS), D ** -0.5),
        "mu_shift": jax.random.uniform(next(ks), (L, RWKV_COLS), f32, 0.1, 0.9),
        "pool_w": nrm((L, POOL_GROUPS, POOL_GROUP_DIM, POOL_GROUP_DIM), POOL_GROUP_DIM ** -0.5),
        "pool_b": nrm((L, POOL_WIDTH), 0.01),
        "pool_scale": 1.0 + nrm((L, POOL_WIDTH), 0.1),
        "w_proj_pool": nrm((L, POOL_WIDTH, D), POOL_WIDTH ** -0.5),
        "w_mem_kv": nrm((L, D, 2 * MEM_WIDTH), D ** -0.5),
        "w_proj_mem": nrm((L, MEM_WIDTH, D), MEM_WIDTH ** -0.5),
        "w0": jax.random.uniform(next(ks), (L, RWKV_WIDTH), f32, -5.0, 1.0),
        "w_up_decay": nrm((L, DECAY_LORA, RWKV_WIDTH), DECAY_LORA ** -0.5),
        "a0": nrm((L, RWKV_WIDTH), 0.1),
        "w_up_a": nrm((L, ICLR_LORA, RWKV_WIDTH), ICLR_LORA ** -0.5),
        "w_up_g": nrm((L, GATE_LORA, RWKV_WIDTH), GATE_LORA ** -0.5),
        "k_k": 0.85 + nrm((L, RWKV_WIDTH), 0.05),
        "k_a": 1.0 + nrm((L, RWKV_WIDTH), 0.05),
        "r_k": nrm((L, RWKV_HEADS, RWKV_HEAD_DIM), 0.1),
        "ln_x_w": gain((L, RWKV_WIDTH)),
        "ln_x_b": nrm((L, RWKV_WIDTH), 0.01),
        "v0": nrm((L - 1, RWKV_WIDTH), 0.1),
        "w_down_v": nrm((L - 1, RWKV_WIDTH, VRES_LORA), RWKV_WIDTH ** -0.5),
        "w_up_v": nrm((L - 1, VRES_LORA, RWKV_WIDTH), VRES_LORA ** -0.5),
        "w_proj_rwkv": nrm((L, RWKV_WIDTH, D), RWKV_WIDTH ** -0.5),
        "gate_b": nrm((L, N_BRANCH, D), 0.01),
        "w_o": nrm((L, D, D), D ** -0.5),
        "norm_ffn_pre": gain((L, D)),
        "norm_ffn_post": gain((L, D)),
        "w_ffn_up": nrm((L, D, 2 * D_FF), D ** -0.5),
        "conv_w": nrm((L, CONV_WIDTH, 2 * D_FF), CONV_WIDTH ** -0.5),
        "conv_b": nrm((L, 2 * D_FF), 0.01),
        "w_ffn_down": nrm((L, D_FF, D), D_FF ** -0.5),
    }


def reference(x, mem, mem_norm, norm_mix_pre, norm_mix_post, w_in, mu_shift, pool_w, pool_b,
              pool_scale, w_proj_pool, w_mem_kv, w_proj_mem, w0, w_up_decay, a0, w_up_a, w_up_g,
              k_k, k_a, r_k, ln_x_w, ln_x_b, v0, w_down_v, w_up_v, w_proj_rwkv, gate_b, w_o,
              norm_ffn_pre, norm_ffn_post, w_ffn_up, conv_w, conv_b, w_ffn_down):
    B, S, D = x.shape
    mem_n = rms_norm(mem, mem_norm)
    v_first = None
    for l in range(DEPTH):
        h = rms_norm(x, norm_mix_pre[l])
        z = h @ w_in[l]
        z_pool, z_q, z_rwkv, z_gate = jnp.split(z, [OFF_Q, OFF_RWKV, OFF_GATE], axis=-1)

        y_pool = pool_branch(z_pool, pool_w[l], pool_b[l], pool_scale[l]) @ w_proj_pool[l]

        y_mem = mem_attention(z_q, mem_n, w_mem_kv[l]) @ w_proj_mem[l]

        zr = z_rwkv + (token_shift(z_rwkv) - z_rwkv) * mu_shift[l]
        r, k, v, dw, da, dg = jnp.split(
            zr, [RWKV_WIDTH, 2 * RWKV_WIDTH, 3 * RWKV_WIDTH, 3 * RWKV_WIDTH + DECAY_LORA,
                 3 * RWKV_WIDTH + DECAY_LORA + ICLR_LORA], axis=-1)
        if l == 0:
            v_first = v
        else:
            vg = jax.nn.sigmoid(v0[l - 1] + (v @ w_down_v[l - 1]) @ w_up_v[l - 1])
            v = v + (v_first - v) * vg
        y_rwkv = rwkv_branch(r, k, v, dw, da, dg, w0[l], w_up_decay[l], a0[l], w_up_a[l],
                             w_up_g[l], k_k[l], k_a[l], r_k[l], ln_x_w[l], ln_x_b[l]) @ w_proj_rwkv[l]

        gates = jax.nn.sigmoid(z_gate.reshape(B, S, N_BRANCH, D) + gate_b[l])
        merged = gates[:, :, 0] * y_pool + gates[:, :, 1] * y_rwkv + gates[:, :, 2] * y_mem
        x = x + rms_norm(merged @ w_o[l], norm_mix_post[l])

        h = rms_norm(x, norm_ffn_pre[l])
        u = causal_dwconv(h @ w_ffn_up[l], conv_w[l], conv_b[l])
        u_gate, u_val = jnp.split(u, 2, axis=-1)
        f = (jax.nn.gelu(u_gate, approximate=True) * u_val) @ w_ffn_down[l]
        x = x + rms_norm(f, norm_ffn_post[l])
    return x
```

```python
import contextlib
import numpy as np
import concourse.bass as bass
import concourse.mybir as mybir
from concourse.bass_utils import run_bass_kernel_spmd

F32 = mybir.dt.float32
BF16 = mybir.dt.bfloat16
U8 = mybir.dt.uint8
AF = mybir.ActivationFunctionType
ALU = mybir.AluOpType

EPOCH = 8192
NDMA = {"sp": 8, "act": 4, "pool": 8}
BUCKET = 1024
ENGMAP = {"pe": "tensor", "act": "scalar", "dve": "vector", "pool": "gpsimd", "sp": "sync"}


def _esize(dt):
    return mybir.dt.size(dt)


class Prog:
    def __init__(self, nc, sbuf_bytes=212736):
        self.nc = nc
        self.ops = []
        self.stack = contextlib.ExitStack()
        self.sems = {}
        self.dma_cnt = {q: 0 for q in NDMA}
        self.pstride = {}
        self.track = {}
        self.bank_last = {}
        self.arena = self.stack.enter_context(nc.sbuf_tensor("arena", [128, sbuf_bytes], U8))
        self.pstride["arena"] = sbuf_bytes
        self.sbuf_bytes = sbuf_bytes
        self.top = 0
        self.peak = 0
        self.psum = []
        for i in range(8):
            t = self.stack.enter_context(nc.psum_tensor("pb%d" % i, [128, 512], F32))
            self.pstride["pb%d" % i] = 2048
            self.psum.append(t)

    def mark(self):
        return self.top

    def release(self, m):
        self.top = m

    def tile(self, shape, dt):
        nb = int(np.prod(shape[1:])) * _esize(dt)
        off = self.top
        self.top += (nb + 63) // 64 * 64
        self.peak = max(self.peak, self.top)
        assert self.top <= self.sbuf_bytes, ("SBUF overflow", self.top)
        v = self.arena[:, off:off + nb].bitcast(dt)
        if len(shape) > 2:
            names = " ".join("d%d" % i for i in range(1, len(shape)))
            kw = {"d%d" % i: shape[i] for i in range(1, len(shape) - 1)}
            v = v.rearrange("p (%s) -> p %s" % (names, names), **kw)
        if shape[0] < 128:
            v = v[0:shape[0]]
        return v

    def bank(self, i, dt=F32):
        t = self.psum[i]
        return t[:, :] if dt == F32 else t[:, :].bitcast(dt)

    def _rect(self, a):
        name = a.tensor.name
        es = _esize(a.dtype)
        dims = a.ap
        boff = a.offset * es
        if name in self.pstride:
            ps = self.pstride[name]
            p0, f0 = divmod(boff, ps)
            if dims[0][0] * es == ps:
                pn = dims[0][1]
                rest = dims[1:]
            elif dims[0][0] == 0:
                pn = 1
                rest = dims[1:]
            else:
                pn = 1
                rest = dims
            ext = sum(s * (c - 1) for s, c in rest) + 1
            return (name, p0, p0 + pn, f0, f0 + ext * es)
        ext = sum(abs(s) * (c - 1) for s, c in dims) + 1
        return ("dram:" + name, 0, 1, boff, boff + ext * es)

    @staticmethod
    def _ovl(a, b):
        return a[1] < b[2] and b[1] < a[2] and a[3] < b[4] and b[3] < a[4]

    @staticmethod
    def _cov(a, b):
        return a[1] <= b[1] and a[2] >= b[2] and a[3] <= b[3] and a[4] >= b[4]

    def _buckets(self, r):
        bs = BUCKET if not r[0].startswith("dram:") else (1 << 20)
        return range(r[3] // bs, (r[4] - 1) // bs + 1)

    def _access(self, r, is_write, me, engkey, deps):
        t = self.track.setdefault(r[0], {})
        for b in self._buckets(r):
            d = t.setdefault(b, {"w": [], "r": {}})
            for (op, rr) in d["w"]:
                if self._ovl(r, rr):
                    deps.add(op)
            if is_write:
                for (ek, rr), op in d["r"].items():
                    if self._ovl(r, rr):
                        deps.add(op)
                d["w"] = [(op, rr) for (op, rr) in d["w"] if not self._cov(r, rr)]
                d["r"] = {k: op for k, op in d["r"].items() if not self._cov(r, k[1])}
                d["w"].append((me, r))
            else:
                d["r"][(engkey, r)] = me

    def op(self, eng, fn, outs=(), ins=(), dma=False):
        i = len(self.ops)
        deps = set()
        engkey = ("dma", i) if dma else eng
        for a in ins:
            if a is None or isinstance(a, (int, float)):
                continue
            self._access(self._rect(a), False, i, engkey, deps)
        for a in outs:
            if a is None:
                continue
            self._access(self._rect(a), True, i, engkey, deps)
        deps.discard(i)
        for a in list(ins) + list(outs):
            if a is None or isinstance(a, (int, float)):
                continue
            r = self._rect(a)
            if not r[0].startswith("pb"):
                continue
            for q in range(r[1] // 32, (r[2] - 1) // 32 + 1):
                d = self.bank_last.setdefault((r[0], q), {})
                for e2, v in d.items():
                    if e2 != eng:
                        deps.add(v)
                d[eng] = i
        deps.discard(i)
        o = dict(eng=eng, fn=fn, deps=deps, dma=dma)
        if dma:
            n = self.dma_cnt[eng]
            self.dma_cnt[eng] += 1
            o["dsem"] = ("d" + eng, n % NDMA[eng])
            o["dval"] = 16 * (n // NDMA[eng] + 1)
        self.ops.append(o)
        return i

    def mm(self, out, lhsT, rhs, start=True, stop=True):
        return self.op("pe", lambda e: e.matmul(out, lhsT=lhsT, rhs=rhs, start=start, stop=stop), [out], [lhsT, rhs])

    def transpose(self, out, in_, ident):
        return self.op("pe", lambda e: e.transpose(out, in_, ident), [out], [in_, ident])

    def act(self, out, in_, func, bias=None, scale=None, accum_out=None):
        kw = {}
        if bias is not None:
            kw["bias"] = bias
        if scale is not None:
            kw["scale"] = scale
        if accum_out is not None:
            kw["accum_out"] = accum_out
        ins = [in_] + [x for x in (bias, scale) if x is not None and not isinstance(x, (int, float))]
        return self.op("act", lambda e: e.activation(out, in_, func, **kw), [out, accum_out], ins)

    def tt(self, eng, out, in0, in1, op):
        return self.op(eng, lambda e: e.tensor_tensor(out, in0, in1, op), [out], [in0, in1])

    def ts(self, eng, out, in0, s1, s2, op0, op1=None):
        ins = [in0] + [x for x in (s1, s2) if x is not None and not isinstance(x, (int, float))]
        if op1 is None:
            return self.op(eng, lambda e: e.tensor_scalar(out, in0, s1, None, op0), [out], ins)
        return self.op(eng, lambda e: e.tensor_scalar(out, in0, s1, s2, op0, op1), [out], ins)

    def stt(self, eng, out, in0, scalar, in1, op0, op1):
        ins = [in0, in1] + ([scalar] if not isinstance(scalar, (int, float)) else [])
        return self.op(eng, lambda e: e.scalar_tensor_tensor(out, in0, scalar, in1, op0, op1), [out], ins)

    def copy(self, eng, out, in_):
        if eng == "act":
            return self.op("act", lambda e: e.copy(out, in_), [out], [in_])
        return self.op(eng, lambda e: e.tensor_copy(out, in_), [out], [in_])

    def memset(self, eng, ap, val):
        return self.op(eng, lambda e: e.memset(ap, val), [ap], [])

    def recip(self, out, in_):
        return self.op("dve", lambda e: e.reciprocal(out, in_), [out], [in_])

    def scan(self, out, d0, d1, init, op0, op1):
        return self.op("dve", lambda e: e.tensor_tensor_scan(out, d0, d1, init, op0, op1), [out], [d0, d1])

    def dma(self, q, out, in_):
        return self.op(q, lambda e: e.dma_start(out=out, in_=in_), [out], [in_], dma=True)

    def _token(self, o):
        if o["dma"]:
            return o["dsem"], o["dval"]
        c = o["cseq"]
        return (o["eng"], c // EPOCH), c % EPOCH + 1

    def sem(self, key):
        if key not in self.sems:
            self.sems[key] = self.stack.enter_context(self.nc.semaphore("s_%s_%s" % key))
        return self.sems[key]

    def emit(self):
        nc = self.nc
        cnt = {e: 0 for e in ENGMAP}
        for o in self.ops:
            if not o["dma"]:
                o["cseq"] = cnt[o["eng"]]
                cnt[o["eng"]] += 1
        for o in self.ops:
            self.sem(self._token(o)[0])
        self.nwaits = 0
        with nc.Block() as block:
            for e, bname in ENGMAP.items():
                mine = [o for o in self.ops if o["eng"] == e]
                if not mine:
                    continue

                def body(eng, e=e, mine=mine):
                    known = {}
                    for o in mine:
                        need = {}
                        for di in o["deps"]:
                            d = self.ops[di]
                            if d["eng"] == e and e == "pe" and not d["dma"]:
                                continue
                            sk, v = self._token(d)
                            if v > need.get(sk, 0):
                                need[sk] = v
                        if o["dma"]:
                            sk, v = o["dsem"], o["dval"] - 16
                            if v > 0 and v > need.get(sk, 0):
                                need[sk] = v
                        for sk in sorted(need):
                            v = need[sk]
                            if known.get(sk, 0) >= v:
                                continue
                            if not sk[0].startswith("d") and any(
                                k2[0] == sk[0] and k2[1] > sk[1] for k2 in known
                            ):
                                continue
                            eng.wait_ge(self.sems[sk], v)
                            self.nwaits += 1
                            known[sk] = v
                        ins = o["fn"](eng)
                        sk, v = self._token(o)
                        ins.then_inc(self.sems[sk], 16 if o["dma"] else 1)
                    if e in NDMA:
                        last = {}
                        for o in mine:
                            if o["dma"]:
                                last[o["dsem"]] = o["dval"]
                        for sk in sorted(last):
                            if known.get(sk, 0) < last[sk]:
                                eng.wait_ge(self.sems[sk], last[sk])

                getattr(block, bname)(body)


D = 1024
KC = 8
NL = 2
IN_COLS = 7424
D_FF = 2816
NFF = D_FF // 128
HS = 1024
TB = 512
NTB = HS // TB
CH = 64
NCH = TB // CH
C0 = float(np.exp(-0.5))
RATIO = 1
S1_EVERY = 2
PF = 2

VT_LAYOUT = [("mu", 26), ("pool_b", 4), ("pool_scale", 4), ("w0", 8), ("a0", 8), ("k_k", 8), ("k_a", 8),
             ("r_k", 8), ("ln_w", 8), ("ln_b", 8), ("v0", 8), ("gate_b", 24), ("conv_w", 132), ("conv_b", 44)]
VT_OFF = {}
_o = 0
for _n, _c in VT_LAYOUT:
    VT_OFF[_n] = _o
    _o += _c
NVL = _o

CST_IDENT = 0
CST_MASKC = 128
CST_CMASK = CST_MASKC + 512
CST_INVCNT = CST_CMASK + 512
CST_BDONES = CST_INVCNT + 16
CST_BDNEG = CST_BDONES + 128
NCST = CST_BDNEG + 128


def make_consts():
    c = np.zeros((128, NCST), np.float32)
    c[:, CST_IDENT:CST_IDENT + 128] = np.eye(128, dtype=np.float32)
    p = np.arange(128)
    hp_, tp = p // 64, p % 64
    q = np.arange(128)
    hq, tq = q // 64, q % 64
    same = (hp_[:, None] == hq[None, :])
    lower = same & (tq[None, :] < tp[:, None])
    upper = same & (tq[None, :] > tp[:, None])
    c[:, CST_MASKC:CST_MASKC + 128] = lower
    c[:, CST_MASKC + 128:CST_MASKC + 256] = upper
    c[:, CST_MASKC + 256:CST_MASKC + 384] = upper
    t64 = np.arange(64)
    incl = (tp[:, None] <= t64[None, :])
    c[:, CST_MASKC + 384:CST_MASKC + 448] = incl
    c[:, CST_MASKC + 448:CST_MASKC + 512] = incl
    cm = np.ones(512, np.float32)
    cm[::64] = 0.0
    c[:, CST_CMASK:CST_CMASK + 512] = cm[None, :]
    c[:, CST_INVCNT:CST_INVCNT + 16] = (1.0 / (np.arange(16) + 1.0))[None, :]
    c[:, CST_BDONES:CST_BDONES + 128] = same
    c[:, CST_BDNEG:CST_BDNEG + 128] = -same.astype(np.float32)
    return c


class _Stop(Exception):
    pass


def build(S=2048, taps=(), stop=None):
    NHF = S // HS
    nc = bass.Bass("TRN2", target_bir_lowering=False)
    P = Prog(nc)

    def din(name, shape):
        return nc.dram_tensor(name, list(shape), F32, kind="ExternalInput").ap()

    x_d = din("x", [S, D])
    mem_d = din("mem", [256, D])
    cst_d = din("cst", [128, NCST])
    vt_d = din("vt", [128, NL * NVL])
    gains_d = din("gains", [9, D])
    w_in_d = din("w_in", [NL, D, IN_COLS])
    pool_w_d = din("pool_w", [NL, 4, 128, 128])
    w_pp_d = din("w_proj_pool", [NL, 512, D])
    w_kv_d = din("w_mem_kv", [NL, D, 1024])
    w_pm_d = din("w_proj_mem", [NL, 512, D])
    w_upd_d = din("w_up_decay", [NL, 64, D])
    w_upa_d = din("w_up_a", [NL, 64, D])
    w_upg_d = din("w_up_g", [NL, 128, D])
    w_dv_d = din("w_down_v", [1, D, 32])
    w_uv_d = din("w_up_v", [1, 32, D])
    w_pr_d = din("w_proj_rwkv", [NL, D, D])
    w_o_d = din("w_o", [NL, D, D])
    w_fu_d = din("w_ffn_up", [NL, D, 2 * D_FF])
    w_fd_d = din("w_ffn_down", [NL, D_FF, D])
    out_d = nc.dram_tensor("out", [S, D], F32, kind="ExternalOutput").ap()
    xm_d = nc.dram_tensor("xm_scr", [S, D], F32, kind="Internal").ap()
    xl_d = nc.dram_tensor("xl_scr", [S, D], F32, kind="Internal").ap()
    vf_d = nc.dram_tensor("vf_scr", [8, 128, S], F32, kind="Internal").ap()
    v1_d = nc.dram_tensor("v1_scr", [8, 128, S], F32, kind="Internal").ap()
    tap_d = {}
    for name, shape in taps:
        tap_d[name] = nc.dram_tensor("tap_" + name, list(shape), F32, kind="ExternalOutput").ap()

    cst = P.tile([128, NCST], F32)
    VT = P.tile([128, NL * NVL], F32)
    OMU = P.tile([128, NL * 26], F32)
    OMKA = P.tile([128, NL * 8], F32)
    EPS = P.tile([128, 4], F32)
    ident_b = P.tile([128, 128], BF16)
    bdones_b = P.tile([128, 128], BF16)
    ones_b = P.tile([128, 128], BF16)
    memT = P.tile([128, 8, 256], BF16)
    kT = [P.tile([128, 4, 256], BF16) for _ in range(NL)]
    Vt = [P.tile([128, 2, 512], BF16) for _ in range(NL)]
    Hst = [P.tile([128, 8, 128], F32) for _ in range(NL)]
    zc = [P.tile([128, 26], F32) for _ in range(NL)]
    pc = [P.tile([128, 4, 16], F32) for _ in range(NL)]
    cc = [P.tile([128, 2 * NFF, 2], F32) for _ in range(NL)]
    H = P.tile([128, 8, HS], BF16)
    WB = [P.tile([128, 8, 256], BF16) for _ in range(4)]
    wb_i = [0]
    big0 = P.mark()
    YR = P.tile([128, 8, HS], BF16)
    off_po = P.mark()
    PO = P.tile([128, 4, HS], BF16)
    AO = P.tile([128, 4, HS], BF16)
    M = P.tile([128, 8, HS], BF16)
    big1 = P.mark()
    P.release(big0)
    ACTB = P.tile([128, NFF, HS], BF16)
    assert P.mark() <= big1
    P.release(big1)
    off_h2 = P.mark()
    H2 = P.tile([128, 8, HS], BF16)
    base = P.mark()

    ident_f = cst[:, CST_IDENT:CST_IDENT + 128]
    maskc = cst[:, CST_MASKC:CST_MASKC + 512]
    cmask = cst[:, CST_CMASK:CST_CMASK + 512]
    invcnt = cst[:, CST_INVCNT:CST_INVCNT + 16]

    def vcol(l, name, j=0):
        o = l * NVL + VT_OFF[name] + j
        return VT[:, o:o + 1]

    P.dma("sp", cst, cst_d)
    P.dma("sp", VT, vt_d)
    P.copy("dve", ident_b, ident_f)
    P.copy("dve", bdones_b, cst[:, CST_BDONES:CST_BDONES + 128])
    P.memset("pool", ones_b, 1.0)
    P.memset("pool", EPS[:, 0:1], 1e-6)
    P.memset("pool", EPS[:, 1:2], 64e-5)
    P.memset("pool", EPS[:, 2:3], 1e-12)
    for l in range(NL):
        P.memset("pool", Hst[l], 0.0)
        P.memset("pool", zc[l], 0.0)
        P.memset("pool", pc[l], 0.0)
        P.memset("pool", cc[l], 0.0)
        o = l * NVL + VT_OFF["mu"]
        P.ts("dve", OMU[:, l * 26:(l + 1) * 26], VT[:, o:o + 26], -1.0, 1.0, ALU.mult, ALU.add)
        o = l * NVL + VT_OFF["k_a"]
        P.ts("dve", OMKA[:, l * 8:(l + 1) * 8], VT[:, o:o + 8], -1.0, 1.0, ALU.mult, ALU.add)

    def load_gain(idx):
        g = P.tile([128, D], F32)
        P.dma("sp", g, gains_d[idx:idx + 1, :].to_broadcast([128, D]))
        return g

    def tap(name, ap_sb):
        if name in tap_d:
            m_ = P.mark()
            tmp = P.tile([128, HS], F32)
            for kc in range(ap_sb.shape[1]):
                P.copy("dve", tmp, ap_sb[:, kc, :])
                P.dma("sp", tap_d[name][:, kc, :], tmp)
            P.release(m_)

    def tm_norm(src, gain, hn, junk, st):
        P.act(junk, src, AF.Square, accum_out=st[:, 0:1])
        P.act(st[:, 1:2], st[:, 0:1], AF.Sqrt, bias=EPS[:, 0:1], scale=1.0 / D)
        P.recip(st[:, 2:3], st[:, 1:2])
        P.stt("dve", hn, src, st[:, 2:3], gain, ALU.mult, ALU.mult)

    def to_fm(hn, dst, t0, bank=7):
        pt = P.bank(bank, BF16)
        for kc in range(8):
            P.transpose(pt[:, kc * 128:(kc + 1) * 128], hn[:, kc * 128:(kc + 1) * 128], ident_b)
        P.copy("act", dst[:, :, t0:t0 + 128], pt[:, 0:1024].rearrange("p (k t) -> p k t", k=8))

    m0 = P.mark()
    g_mem = load_gain(0)
    mt_x = P.tile([128, D], F32)
    mt_h = P.tile([128, D], BF16)
    mt_j = P.tile([128, D], BF16)
    mt_s = P.tile([128, 4], F32)
    for mt in range(2):
        P.dma("sp", mt_x, mem_d[mt * 128:(mt + 1) * 128, :])
        tm_norm(mt_x, g_mem, mt_h, mt_j, mt_s)
        to_fm(mt_h, memT, mt * 128)
    P.release(m0)

    proj_bank = [0]
    rwkv_top = [0]

    def load_w(dst, src):
        P.dma("pool", dst, src)

    def proj(w2d, chunks, rhs, consume, kchunks=8, pf=PF):
        groups = []
        i = 0
        while i < len(chunks):
            n = 2 if (i + 1 < len(chunks) and chunks[i + 1] == chunks[i] + 1) else 1
            groups.append((chunks[i], n))
            i += n
        loaded = {}

        def issue(gi):
            c0, n = groups[gi]
            wb = WB[wb_i[0] % 4]
            wb_i[0] += 1
            load_w(wb[:, 0:kchunks, 0:n * 128],
                   w2d[:, c0 * 128:(c0 + n) * 128].rearrange("(kc p) c -> p kc c", p=128))
            loaded[gi] = wb

        for gi in range(min(pf, len(groups))):
            issue(gi)
        ci = 0
        for gi, (c0, n) in enumerate(groups):
            if gi + pf < len(groups):
                issue(gi + pf)
            wb = loaded.pop(gi)
            for j in range(n):
                for tb in range(NTB):
                    ps = P.bank(proj_bank[0] % 3)
                    proj_bank[0] += 1
                    for kc in range(kchunks):
                        P.mm(ps, wb[:, kc, j * 128:(j + 1) * 128], rhs[:, kc, tb * TB:(tb + 1) * TB],
                             start=(kc == 0), stop=(kc == kchunks - 1))
                    consume(ci, c0 + j, tb, ps)
                ci += 1

    def residual_stage(t_half, src_x, dst_x, mm_fn, g_post, g_pre, Hdst):
        n = HS // 128
        xt = [P.tile([128, D], F32) for _ in range(3)]
        yt = [P.tile([128, D], F32) for _ in range(3)]
        hn = [P.tile([128, D], BF16) for _ in range(3)]
        jkA = P.tile([128, D], BF16)
        jkC = P.tile([128, D], BF16)
        st = [P.tile([128, 8], F32) for _ in range(3)]

        def stA(tt):
            b = tt % 3
            r0 = t_half + tt * 128
            P.dma("sp", xt[b], src_x[r0:r0 + 128, :])
            for hh in range(2):
                ps = P.bank((tt % 2) * 2 + hh)
                mm_fn(tt, hh, ps)
                P.copy("act", yt[b][:, hh * 512:(hh + 1) * 512], ps)

        def stB(tt):
            b = tt % 3
            r0 = t_half + tt * 128
            P.act(jkA, yt[b], AF.Square, accum_out=st[b][:, 0:1])
            P.act(st[b][:, 1:2], st[b][:, 0:1], AF.Sqrt, bias=EPS[:, 0:1], scale=1.0 / D)
            P.recip(st[b][:, 2:3], st[b][:, 1:2])
            P.stt("dve", yt[b], yt[b], st[b][:, 2:3], g_post, ALU.mult, ALU.mult)
            P.tt("dve", xt[b], xt[b], yt[b], ALU.add)
            P.dma("sp", dst_x[r0:r0 + 128, :], xt[b])

        def stC1(tt):
            b = tt % 3
            if g_pre is not None:
                tm_norm(xt[b], g_pre, hn[b], jkC, st[b][:, 4:8])

        def stC2(tt):
            b = tt % 3
            if g_pre is not None:
                to_fm(hn[b], Hdst, tt * 128)

        for step in range(n + 3):
            if 0 <= step - 3 < n:
                stC2(step - 3)
            if 0 <= step - 2 < n:
                stC1(step - 2)
            if 0 <= step - 1 < n:
                stB(step - 1)
            if step < n:
                stA(step)

    def phase(name):
        if stop == name:
            raise _Stop()

    try:
      phase('init')
      for hf in range(NHF):
          t_half = hf * HS
          for l in range(NL):
              w_in_l = w_in_d[l]
              if l == 0:
                  m0 = P.mark()
                  g_pre = load_gain(1)
                  xt = [P.tile([128, D], F32) for _ in range(2)]
                  hn = [P.tile([128, D], BF16) for _ in range(2)]
                  jk = P.tile([128, D], BF16)
                  st = [P.tile([128, 4], F32) for _ in range(2)]
                  for tt in range(HS // 128 + 1):
                      b = tt % 2
                      if tt < HS // 128:
                          P.dma("sp", xt[b], x_d[t_half + tt * 128:t_half + (tt + 1) * 128, :])
                          tm_norm(xt[b], g_pre, hn[b], jk, st[b])
                      if tt >= 1:
                          to_fm(hn[(tt - 1) % 2], H, (tt - 1) * 128)
                  P.release(m0)
              if (l, hf) == (0, 0):
                  tap("h0", H)
              phase("h%d%d" % (l, hf))
              x_res = x_d if l == 0 else xl_d
              x_lout = xl_d if l == 0 else out_d

              if hf == 0:
                  m0 = P.mark()
                  for g4 in range(4):
                      wb = WB[wb_i[0] % 4]
                      wb_i[0] += 1
                      load_w(wb, w_kv_d[l][:, g4 * 256:(g4 + 1) * 256].rearrange("(kc p) c -> p kc c", p=128))
                      if g4 < 2:
                          for j in range(2):
                              hh = g4 * 2 + j
                              ps = P.bank(3)
                              for kc in range(8):
                                  P.mm(ps[:, 0:256], wb[:, kc, j * 128:(j + 1) * 128], memT[:, kc, :],
                                       start=(kc == 0), stop=(kc == 7))
                              P.copy("act", kT[l][:, hh, :], ps[:, 0:256])
                      else:
                          for mt in range(2):
                              ps = P.bank(3)
                              for kc in range(8):
                                  P.mm(ps[:, 0:256], memT[:, kc, mt * 128:(mt + 1) * 128], wb[:, kc, :],
                                       start=(kc == 0), stop=(kc == 7))
                              P.copy("act", Vt[l][:, mt, (g4 - 2) * 256:(g4 - 1) * 256], ps[:, 0:256])
                  P.release(m0)

              phase('kv%d%d' % (l, hf))
              P.release(off_po)
              WDA = P.tile([128, D], BF16)
              WG = P.tile([128, D], BF16)
              LA = P.tile([128, HS], BF16)
              SDG = P.tile([128, HS], BF16)
              load_w(WDA[0:64, :], w_upd_d[l])
              load_w(WDA[64:128, :], w_upa_d[l])
              load_w(WG, w_upg_d[l])
              Zt = [P.tile([128, HS + 1], F32) for _ in range(3)]
              ZR = [z_[:, 0:HS] for z_ in Zt]

              def shift_consume(j, Z):
                  mu_c = VT[:, l * NVL + VT_OFF["mu"] + j:l * NVL + VT_OFF["mu"] + j + 1]
                  omu_c = OMU[:, l * 26 + j:l * 26 + j + 1]

                  def consume(ci, chunk, tb, ps):
                      if tb == 0:
                          P.copy("dve", Z[:, 0:1], zc[l][:, j:j + 1])
                      P.act(Z[:, 1 + tb * TB:1 + (tb + 1) * TB], ps, AF.Copy, scale=mu_c)
                      P.stt("dve", Z[:, tb * TB:(tb + 1) * TB], ps, omu_c, Z[:, tb * TB:(tb + 1) * TB],
                            ALU.mult, ALU.add)
                      if tb == NTB - 1:
                          P.copy("dve", zc[l][:, j:j + 1], Z[:, HS:HS + 1])
                  return consume

              _lc = {32: shift_consume(24, Zt[0]), 33: shift_consume(25, Zt[1])}
              proj(w_in_l, [32, 33], H, lambda ci, chunk, tb, ps: _lc[chunk](ci, chunk, tb, ps))
              P.act(LA[0:64, :], ZR[0][0:64, :], AF.Tanh)
              P.copy("pool", LA[64:128, :], ZR[0][64:128, :])
              P.act(SDG, ZR[1], AF.Sigmoid)
              phase('lora%d%d' % (l, hf))

              if l == 1:
                  WDV = P.tile([128, 8, 32], BF16)
                  WUV = P.tile([32, D], BF16)
                  VDB = P.tile([32, HS], BF16)
                  vb = P.tile([128, HS], BF16)
                  VFT = P.tile([128, HS], F32)
                  VG = P.tile([128, TB], F32)
                  load_w(WDV, w_dv_d[0].rearrange("(hp p) r -> p hp r", p=128))
                  load_w(WUV, w_uv_d[0])
                  for hp in range(8):
                      proj(w_in_l, [8 + 16 + hp], H, shift_consume(16 + hp, Zt[2]))
                      P.dma("sp", v1_d[hp][:, t_half:t_half + HS], ZR[2])
                      P.copy("pool", vb, ZR[2])
                      for tb in range(NTB):
                          P.mm(P.bank(3 + tb)[0:32, :], WDV[:, hp, :], vb[:, tb * TB:(tb + 1) * TB],
                               start=(hp == 0), stop=(hp == 7))
                  for tb in range(NTB):
                      P.copy("act", VDB[:, tb * TB:(tb + 1) * TB], P.bank(3 + tb)[0:32, :])

              SL = [P.tile([128, TB], F32) for _ in range(7)]
              SG, AA, LP, TMP, E2, E3, KK = SL
              KA, BT32, KT32 = [P.tile([128, TB], BF16) for _ in range(3)]
              MKB = P.tile([128, 2, 128], BF16)
              P.copy("dve", MKB[:, 0, :], cst[:, CST_BDONES:CST_BDONES + 128])
              P.copy("dve", MKB[:, 1, :], cst[:, CST_BDNEG:CST_BDNEG + 128])
              RS_a, TT_, RKa, KEFF = [P.tile([128, TB], F32) for _ in range(4)]
              KKR, BN = TMP, TMP
              SQ = P.tile([128, TB], BF16)
              RK = P.tile([128, TB], BF16)
              BBD, KBD, BHBD, KHBD, VBD = [P.tile([128, NCH, 128], BF16) for _ in range(5)]
              AMCp = P.tile([128, NCH, 256], BF16)
              PP = [P.tile([128, NCH, 2, 128], BF16) for _ in range(2)]
              TTs = P.tile([128, NCH, 128], BF16)
              Xb = P.tile([128, 128], BF16)
              Ub = P.tile([128, 128], BF16)
              Hb = P.tile([128, 128], BF16)
              YC = P.tile([128, TB], F32)
              RSq = P.tile([128, TB], F32)
              YB = P.tile([128, TB], BF16)
              SQ2 = P.tile([128, TB], BF16)
              QS = []
              for _ in range(2):
                  qs = dict(ABD=P.tile([128, NCH, 128], BF16), VTM=P.tile([128, NCH, 128], BF16),
                            BHTM=P.tile([128, NCH, 128], BF16), KHTM=P.tile([128, NCH, 128], BF16),
                            AMCq=P.tile([128, NCH, 256], BF16), TT=P.tile([128, NCH, 128], BF16))
                  QS.append(qs)
              S3 = [dict(RT=P.tile([128, TB], BF16), BON=P.tile([128, TB], BF16), Dq=P.tile([128, NCH, 1], F32))
                    for _ in range(3)]
              ZRr, ZRk, ZRv = ZR
              blocks = [(hp, tb) for hp in range(8) for tb in range(NTB)]
              blkmask = cst[:, CST_BDONES:CST_BDONES + 128]
              m4 = MKB[:, 0, :].rearrange("p (h t) -> p h t", h=2).unsqueeze(1).to_broadcast([128, NCH, 2, CH])
              m4f = blkmask.rearrange("p (h t) -> p h t", h=2).unsqueeze(1).to_broadcast([128, NCH, 2, CH])

              negmask = cst[:, CST_BDNEG:CST_BDNEG + 128]
              m4n = MKB[:, 1, :].rearrange("p (h t) -> p h t", h=2).unsqueeze(1).to_broadcast([128, NCH, 2, CH])

              def bdw(eng, dst, src, neg=False):
                  x4 = src.rearrange("p (c t) -> p c t", t=CH).unsqueeze(2).to_broadcast([128, NCH, 2, CH])
                  mk = m4n if neg else (m4 if src.dtype == BF16 else m4f)
                  P.tt(eng, dst.rearrange("p c (h t) -> p c h t", h=2), x4, mk, ALU.mult)

              def sbank():
                  ps = P.bank(proj_bank[0] % 3)
                  proj_bank[0] += 1
                  return ps

              def genS1(bi):
                  hp, tb = blocks[bi]
                  RT, BONq, Dq = (S3[bi % 3][k] for k in ("RT", "BON", "Dq"))
                  if tb == 0:
                      _rc = {8 + hp: shift_consume(hp, Zt[0]), 16 + hp: shift_consume(8 + hp, Zt[1]),
                             24 + hp: shift_consume(16 + hp, Zt[2])}
                      proj(w_in_l, [8 + hp, 16 + hp] + ([24 + hp] if l == 0 else []), H,
                           lambda ci, chunk, tb_, ps: _rc[chunk](ci, chunk, tb_, ps), pf=3)
                      yield
                      if l == 0:
                          P.dma("sp", vf_d[hp][:, t_half:t_half + HS], ZRv)
                      else:
                          P.dma("sp", ZRv, v1_d[hp][:, t_half:t_half + HS])
                          P.dma("sp", VFT, vf_d[hp][:, t_half:t_half + HS])
                          P.tt("pool", VFT, VFT, ZRv, ALU.subtract)
                          for tb2 in range(NTB):
                              sl2 = slice(tb2 * TB, (tb2 + 1) * TB)
                              ps = sbank()
                              P.mm(ps, WUV[:, hp * 128:(hp + 1) * 128], VDB[:, sl2])
                              P.act(VG, ps, AF.Sigmoid, bias=vcol(l, "v0", hp))
                              P.tt("dve", VFT[:, sl2], VFT[:, sl2], VG, ALU.mult)
                          P.tt("pool", ZRv, ZRv, VFT, ALU.add)
                      yield
                  hs_ = slice(hp * 128, (hp + 1) * 128)
                  sl = slice(tb * TB, (tb + 1) * TB)
                  zr, zk, zv = ZRr[:, sl], ZRk[:, sl], ZRv[:, sl]
                  KK3 = KK.rearrange("p (c t) -> p c t", t=CH)
                  E33 = E3.rearrange("p (c t) -> p c t", t=CH)
                  KA3 = KA.rearrange("p (c t) -> p c t", t=CH)
                  Dv = E33[:, :, CH - 1:CH]
                  ps_w = sbank()
                  P.mm(ps_w, WDA[0:64, hs_], LA[0:64, sl])
                  ps_a = sbank()
                  P.mm(ps_a, WDA[64:128, hs_], LA[64:128, sl])
                  P.act(KKR, zk, AF.Copy, scale=vcol(l, "k_k", hp))
                  P.act(RKa, zr, AF.Copy, scale=vcol(l, "r_k", hp))
                  yield
                  P.act(SG, ps_w, AF.Sigmoid, bias=vcol(l, "w0", hp))
                  P.act(AA, ps_a, AF.Sigmoid, bias=vcol(l, "a0", hp))
                  P.act(SQ, KKR, AF.Square)
                  yield
                  P.scan(LP, cmask, SG, 0.0, ALU.mult, ALU.add)
                  ps_s = sbank()
                  P.mm(ps_s, bdones_b, SQ)
                  P.act(TT_, AA, AF.Identity, bias=OMKA[:, l * 8 + hp:l * 8 + hp + 1], scale=vcol(l, "k_a", hp))
                  yield
                  P.act(RS_a, ps_s, AF.Ln, bias=EPS[:, 2:3])
                  P.act(E2, LP, AF.Exp, scale=C0)
                  P.act(E3, LP, AF.Exp, scale=-C0)
                  yield
                  P.act(RS_a, RS_a, AF.Exp, scale=-0.5)
                  P.tt("pool", KEFF, zk, TT_, ALU.mult)
                  yield
                  P.tt("pool", KK, KKR, RS_a, ALU.mult)
                  P.tt("pool", KT32, KEFF, E2, ALU.mult)
                  P.tt("pool", RK, RKa, KEFF, ALU.mult)
                  P.tt("pool", RT, zr, E3, ALU.mult)
                  P.copy("pool", Dq, Dv)
                  yield
                  P.tt("pool", BN, KK, AA, ALU.mult)
                  ps_b = sbank()
                  P.mm(ps_b, bdones_b, RK)
                  P.tt("pool", KA3[:, :, 1:CH], KK3[:, :, 1:CH], E33[:, :, 0:CH - 1], ALU.mult)
                  P.copy("pool", KA3[:, :, 0:1], KK3[:, :, 0:1])
                  yield
                  P.tt("pool", BT32, BN, E2, ALU.mult)
                  P.tt("dve", BONq, ps_b, zv, ALU.mult)
                  yield

              def stageP(bi):
                  hp, tb = blocks[bi]
                  qs = QS[bi % 2]
                  ABD, VTM, BHTM, KHTM, AMCq, TTq = (qs[k] for k in ("ABD", "VTM", "BHTM", "KHTM", "AMCq", "TT"))
                  RT = S3[bi % 3]["RT"]
                  sl = slice(tb * TB, (tb + 1) * TB)
                  zv = ZRv[:, sl]
                  Dv = E3.rearrange("p (c t) -> p c t", t=CH)[:, :, CH - 1:CH]
                  bdw("dve", ABD, KA, neg=True)
                  bdw("dve", BBD, BT32)
                  bdw("dve", KBD, KT32)
                  bdw("pool", VBD, zv)
                  yield
                  Db = Dv.to_broadcast([128, NCH, 128])
                  P.tt("dve", BHBD, BBD, Db, ALU.mult)
                  P.tt("pool", KHBD, KBD, Db, ALU.mult)
                  yield
                  for src, dst, e_ in ((BHBD, BHTM, "act"), (KHBD, KHTM, "dve"), (VBD, VTM, "act")):
                      pt = P.bank(3, BF16)
                      for c in range(NCH):
                          P.transpose(pt[:, c * 128:(c + 1) * 128], src[:, c, :], ident_b)
                      P.copy(e_, dst, pt[:, 0:1024].rearrange("p (c t) -> p c t", c=NCH))
                      yield
                  for c in range(NCH):
                      ps = P.bank(4 + c % 2)
                      cs = slice(c * CH, (c + 1) * CH)
                      P.mm(ps[:, 0:128], ABD[:, c, :], BBD[:, c, :])
                      P.mm(ps[:, 128:256], BBD[:, c, :], ABD[:, c, :])
                      P.mm(ps[:, 256:384], KBD[:, c, :], ABD[:, c, :])
                      P.mm(ps[:, 384:448], BBD[:, c, :], RT[:, cs])
                      P.mm(ps[:, 448:512], KBD[:, c, :], RT[:, cs])
                      P.tt("dve", AMCp[:, c, :], ps[:, 0:256], maskc[:, 0:256], ALU.mult)
                      P.tt("dve", AMCq[:, c, :], ps[:, 256:512], maskc[:, 256:512], ALU.mult)
                      if c % 2 == 1:
                          yield
                  for c in range(NCH):
                      P.tt("pool", TTs[:, c, :], AMCp[:, c, 128:256], ident_f, ALU.add)
                  TTb = [TTs, TTq]

                  def sq(lev):
                      for c2 in range(NCH // 2):
                          ps = P.bank(4 + c2 % 2)
                          for i in range(2):
                              c = c2 * 2 + i
                              if lev == 1:
                                  p_prev, pt_prev = AMCp[:, c, 0:128], AMCp[:, c, 128:256]
                              else:
                                  p_prev, pt_prev = PP[(lev - 1) % 2][:, c, 0, :], PP[(lev - 1) % 2][:, c, 1, :]
                              P.mm(ps[:, (i * 2) * 128:(i * 2 + 1) * 128], pt_prev, p_prev)
                              if lev < 5:
                                  P.mm(ps[:, (i * 2 + 1) * 128:(i * 2 + 2) * 128], p_prev, pt_prev)
                          e_ = "act" if c2 % 2 == 0 else "dve"
                          if lev < 5:
                              P.copy(e_, PP[lev % 2][:, c2 * 2:c2 * 2 + 2, :, :],
                                     ps.rearrange("p (c a t) -> p c a t", c=2, a=2))
                          else:
                              P.copy(e_, PP[lev % 2][:, c2 * 2:c2 * 2 + 2, 0, :],
                                     ps.rearrange("p (c a t) -> p c a t", c=2, a=2)[:, :, 0, :])
                          if c2 % 2 == 1:
                              yield

                  def ttapply(lev, cur):
                      for c4 in range(NCH // 4):
                          ps = P.bank(3)
                          for i in range(4):
                              c = c4 * 4 + i
                              P.mm(ps[:, i * 128:(i + 1) * 128], PP[lev % 2][:, c, 0, :], TTb[cur][:, c, :])
                          P.tt("dve", TTb[1 - cur][:, c4 * 4:c4 * 4 + 4, :],
                               ps.rearrange("p (c t) -> p c t", c=4), TTb[cur][:, c4 * 4:c4 * 4 + 4, :], ALU.add)
                          yield

                  def inverse():
                      cur = 0
                      yield from sq(1)
                      for lev in range(1, 6):
                          if lev < 5:
                              yield from sq(lev + 1)
                          yield from ttapply(lev, cur)
                          cur = 1 - cur
                      assert cur == 1

                  gi = inverse()
                  gs = genS1(bi + 1) if bi + 1 < len(blocks) else None
                  di, ds = False, gs is None
                  k_ = 0
                  while not (di and ds):
                      if not di:
                          try:
                              next(gi)
                          except StopIteration:
                              di = True
                      k_ += 1
                      if not ds and (di or k_ % S1_EVERY == 0):
                          try:
                              next(gs)
                          except StopIteration:
                              ds = True
                      yield

              def stageQ(bi):
                  hp, tb = blocks[bi]
                  qs = QS[bi % 2]
                  ABD, VTM, BHTM, KHTM, AMCq, TTq = (qs[k] for k in ("ABD", "VTM", "BHTM", "KHTM", "AMCq", "TT"))
                  RT, BONq, Dq = (S3[bi % 3][k] for k in ("RT", "BON", "Dq"))
                  hs_ = slice(hp * 128, (hp + 1) * 128)
                  sl = slice(tb * TB, (tb + 1) * TB)
                  Hs = Hst[l][:, hp, :]
                  P.copy("act", Hb, Hs)
                  psY = P.bank(7)
                  for c in range(NCH):
                      cs = slice(c * CH, (c + 1) * CH)
                      ps = P.bank(6)
                      P.mm(ps[:, 0:128], ABD[:, c, :], Hb, start=True, stop=False)
                      P.mm(ps[:, 0:128], AMCq[:, c, 0:128], VTM[:, c, :], start=False, stop=True)
                      P.copy("act", Xb, ps[:, 0:128])
                      yield
                      P.mm(ps[:, 128:256], TTq[:, c, :], Xb)
                      P.copy("act", Ub, ps[:, 128:256])
                      yield
                      P.mm(psY[:, cs], Hb, RT[:, cs], start=True, stop=False)
                      P.mm(psY[:, cs], Ub, AMCq[:, c, 128:192], start=False, stop=False)
                      P.mm(psY[:, cs], VTM[:, c, :], AMCq[:, c, 192:256], start=False, stop=True)
                      P.mm(ps[:, 256:384], BHTM[:, c, :], Ub, start=True, stop=False)
                      P.mm(ps[:, 256:384], KHTM[:, c, :], VTM[:, c, :], start=False, stop=True)
                      P.stt("dve", Hs, Hs, Dq[:, c, :], ps[:, 256:384], ALU.mult, ALU.add)
                      P.copy("act", Hb, Hs)
                      yield
                  P.copy("act", YC, psY)
                  P.copy("dve", YB, psY)
                  ps = P.bank(6)
                  P.mm(ps, bdones_b, YB)
                  P.stt("dve", YC, ps, -1.0 / 64, YC, ALU.mult, ALU.add)
                  P.act(SQ2, YC, AF.Square)
                  yield
                  ps = P.bank(7)
                  P.mm(ps, bdones_b, SQ2)
                  P.act(RSq, ps, AF.Ln, bias=EPS[:, 1:2], scale=1.0 / 64)
                  P.act(RSq, RSq, AF.Exp, scale=-0.5)
                  P.tt("dve", YC, YC, RSq, ALU.mult)
                  P.ts("dve", YC, YC, vcol(l, "ln_w", hp), vcol(l, "ln_b", hp), ALU.mult, ALU.add)
                  P.tt("pool", YC, YC, BONq, ALU.add)
                  yield
                  ps = P.bank(6)
                  P.mm(ps, WG[:, hs_], SDG[:, sl])
                  P.tt("dve", YR[:, hp, sl], YC, ps, ALU.mult)
                  yield

              def run_gens(gq, gp, ratio=RATIO):
                  dq = gq is None
                  dp = gp is None
                  while not (dq and dp):
                      if not dq:
                          try:
                              next(gq)
                          except StopIteration:
                              dq = True
                      for _ in range(ratio if not dq else 1000000):
                          if dp:
                              break
                          try:
                              next(gp)
                          except StopIteration:
                              dp = True

              rwkv_top[0] = P.mark()
              run_gens(None, genS1(0))
              run_gens(None, stageP(0))
              for bi in range(len(blocks)):
                  run_gens(stageQ(bi), stageP(bi + 1) if bi + 1 < len(blocks) else None)
              P.release(base)
              if (l, hf) == (0, 0):
                  tap("yr0", YR)
              phase("yr%d%d" % (l, hf))

              mP = P.mark()
              ZP = P.tile([128, 16 + HS], F32)
              SA = P.tile([128, 16 + HS], F32)
              SB_ = P.tile([128, 16 + HS], F32)
              PLD = P.tile([128, HS], BF16)
              PW = P.tile([128, 4, 128], BF16)
              load_w(PW, pool_w_d[l].rearrange("g c d -> c g d"))

              def pool_consume(ci, chunk, tb, ps):
                  g = chunk
                  if tb == 0:
                      P.copy("dve", ZP[:, 0:16], pc[l][:, g, :])
                  P.copy("act", ZP[:, 16 + tb * TB:16 + (tb + 1) * TB], ps)
                  if tb < NTB - 1:
                      return
                  P.copy("dve", pc[l][:, g, :], ZP[:, HS:HS + 16])
                  win = (2, 4, 8, 16)[g]
                  src = ZP
                  step = 1
                  bufs = [SA, SB_]
                  bi = 0
                  while step < win:
                      dst = bufs[bi]
                      bi = 1 - bi
                      lo = 2 * step - 1
                      P.tt("dve" if step in (1, 4) else "pool", dst[:, lo:16 + HS], src[:, lo:16 + HS],
                           src[:, lo - step:16 + HS - step], ALU.add)
                      src = dst
                      step *= 2
                  P.stt("dve", PLD, src[:, 16:16 + HS], 1.0 / win, ZP[:, 16:16 + HS], ALU.mult, ALU.subtract)
                  if hf == 0:
                      n = win - 1
                      P.tt("pool", SA[:, 0:n], src[:, 16:16 + n], invcnt[:, 0:n], ALU.mult)
                      P.tt("pool", PLD[:, 0:n], SA[:, 0:n], ZP[:, 16:16 + n], ALU.subtract)
                  for tb2 in range(NTB):
                      ps2 = P.bank(3 + tb2)
                      P.mm(ps2, PW[:, g, :], PLD[:, tb2 * TB:(tb2 + 1) * TB])
                      P.ts("dve", PO[:, g, tb2 * TB:(tb2 + 1) * TB], ps2, vcol(l, "pool_b", g), vcol(l, "pool_scale", g),
                           ALU.add, ALU.mult)

              proj(w_in_l, [0, 1, 2, 3], H, pool_consume)
              P.release(mP)
              if (l, hf) == (0, 0):
                  tap("po0", PO)
              phase("po%d%d" % (l, hf))

              mA = P.mark()
              QT = P.tile([128, HS], BF16)
              EX = P.tile([128, 2, TB], BF16)
              RD = P.tile([128, TB], F32)

              def attn_consume(ci, chunk, tb, ps):
                  hh = chunk - 4
                  P.copy("act", QT[:, tb * TB:(tb + 1) * TB], ps)
                  q = QT[:, tb * TB:(tb + 1) * TB]
                  for mt in range(2):
                      pss = P.bank(3 + mt)
                      P.mm(pss, kT[l][:, hh, mt * 128:(mt + 1) * 128], q)
                      P.act(EX[:, mt, :], pss, AF.Exp, scale=float(128 ** -0.5))
                  psd = P.bank(5)
                  pso = P.bank(6)
                  for mt in range(2):
                      P.mm(psd, ones_b, EX[:, mt, :], start=(mt == 0), stop=(mt == 1))
                  for mt in range(2):
                      P.mm(pso, Vt[l][:, mt, hh * 128:(hh + 1) * 128], EX[:, mt, :], start=(mt == 0), stop=(mt == 1))
                  P.act(RD, psd, AF.Ln)
                  P.act(RD, RD, AF.Exp, scale=-1.0)
                  P.tt("dve", AO[:, hh, tb * TB:(tb + 1) * TB], pso, RD, ALU.mult)

              proj(w_in_l, [4, 5, 6, 7], H, attn_consume)
              P.release(mA)
              if (l, hf) == (0, 0):
                  tap("ao0", AO)
              phase("ao%d%d" % (l, hf))

              mM = P.mark()
              WO = P.tile([128, 8, D], BF16)

              def load_wo(g4):
                  load_w(WO[:, :, g4 * 256:(g4 + 1) * 256],
                         w_o_d[l][:, g4 * 256:(g4 + 1) * 256].rearrange("(kc p) c -> p kc c", p=128))
              mM2 = P.mark()
              MS = [dict(GT=[P.tile([128, TB], F32) for _ in range(3)], ACCS=[P.tile([128, TB], F32) for _ in range(NTB)],
                         WPP=P.tile([128, 4, 128], BF16), WPM=P.tile([128, 4, 128], BF16), WPR=P.tile([128, 8, 128], BF16))
                    for _ in range(2)]
              def load_branch_w(j):
                  ms = MS[j % 2]
                  cs_ = slice(j * 128, (j + 1) * 128)
                  load_w(ms["WPP"], w_pp_d[l][:, cs_].rearrange("(kc p) c -> p kc c", p=128))
                  load_w(ms["WPM"], w_pm_d[l][:, cs_].rearrange("(kc p) c -> p kc c", p=128))
                  load_w(ms["WPR"], w_pr_d[l][:, cs_].rearrange("(kc p) c -> p kc c", p=128))

              def gate_consume(ci, chunk, tb, ps):
                  b = (chunk - 34) // 8
                  j = (chunk - 34) % 8
                  GT, ACCS, WPP, WPM, WPR = (MS[j % 2][k] for k in ("GT", "ACCS", "WPP", "WPM", "WPR"))
                  if b == 0 and tb == 0 and j + 1 < 8:
                      load_branch_w(j + 1)
                  if b == 1 and tb == 0 and 2 <= j < 6:
                      load_wo(j - 2)
                  P.act(GT[b], ps, AF.Sigmoid, bias=vcol(l, "gate_b", b * 8 + j))
                  src, w, nk = ((PO, WPP, 4), (YR, WPR, 8), (AO, WPM, 4))[b]
                  pv = P.bank(3 + b)
                  for kc in range(nk):
                      P.mm(pv, w[:, kc, :], src[:, kc, tb * TB:(tb + 1) * TB], start=(kc == 0), stop=(kc == nk - 1))
                  if b == 0:
                      P.tt("dve", ACCS[tb], pv, GT[b], ALU.mult)
                  else:
                      P.tt("dve", GT[b], pv, GT[b], ALU.mult)
                      if b == 1:
                          P.tt("dve", ACCS[tb], ACCS[tb], GT[b], ALU.add)
                      else:
                          P.tt("dve", M[:, j, tb * TB:(tb + 1) * TB], ACCS[tb], GT[b], ALU.add)

              load_branch_w(0)
              order = []
              for j in range(8):
                  order += [34 + j, 42 + j, 50 + j]
              proj(w_in_l, order, H, gate_consume)
              P.release(mM2)
              if (l, hf) == (0, 0):
                  tap("merged0", M)
              phase("merged%d%d" % (l, hf))

              mO = P.mark()
              g_post = load_gain(1 + l * 4 + 1)
              g_fpre = load_gain(1 + l * 4 + 2)
              def wo_mm(tt, hh, ps):
                  for kc in range(8):
                      P.mm(ps, M[:, kc, tt * 128:(tt + 1) * 128], WO[:, kc, hh * 512:(hh + 1) * 512],
                           start=(kc == 0), stop=(kc == 7))

              residual_stage(t_half, x_res, xm_d, wo_mm, g_post, g_fpre, H2)
              P.release(mM)
              if (l, hf) == (0, 0):
                  tap("h2_0", H2)
              phase("h2%d%d" % (l, hf))

              mF = P.mark()
              FS = [dict(UG=P.tile([128, 2 + HS], F32), UV=P.tile([128, 2 + HS], F32), CG=P.tile([128, HS], F32),
                         CV=P.tile([128, HS], F32)) for _ in range(2)]
              w_fu_l = w_fu_d[l]
              wd_off = P.mark()
              WD = P.tile([128, NFF, D], BF16)

              def load_wd(kc):
                  load_w(WD[:, kc:kc + 2, :],
                         w_fd_d[l][kc * 128:(kc + 2) * 128, :].rearrange("(kc p) c -> p kc c", p=128))

              def conv(U, c, out):
                  cw = [vcol(l, "conv_w", k * 2 * NFF + c) for k in range(3)]
                  P.act(out, U[:, 0:HS], AF.Identity, bias=vcol(l, "conv_b", c), scale=cw[0])
                  P.stt("dve", out, U[:, 1:HS + 1], cw[1], out, ALU.mult, ALU.add)
                  P.stt("dve", out, U[:, 2:HS + 2], cw[2], out, ALU.mult, ALU.add)

              def up_consume(ci, chunk, tb, ps):
                  c = chunk % NFF
                  UG, UV, CG, CV = (FS[c % 2][k] for k in ("UG", "UV", "CG", "CV"))
                  U = UG if chunk < NFF else UV
                  if tb == 0:
                      P.copy("dve", U[:, 0:2], cc[l][:, chunk, :])
                  P.copy("act", U[:, 2 + tb * TB:2 + (tb + 1) * TB], ps)
                  if tb == NTB - 1:
                      P.copy("dve", cc[l][:, chunk, :], U[:, HS:HS + 2])
                      if chunk >= NFF:
                          if c % 2 == 0:
                              load_wd(c)
                          conv(UG, c, CG)
                          conv(UV, NFF + c, CV)

                          def tail(c=c, CG=CG, CV=CV):
                              P.act(CG, CG, AF.Gelu_apprx_tanh)
                              P.tt("dve", ACTB[:, c, :], CG, CV, ALU.mult)
                          up_pending.append(tail)
                      elif up_pending:
                          up_pending.pop(0)()

              order = []
              for c in range(NFF):
                  order += [c, NFF + c]
              up_pending = []
              proj(w_fu_l, order, H2, up_consume)
              while up_pending:
                  up_pending.pop(0)()
              P.release(mF)
              if (l, hf) == (0, 0):
                  tap("act0", ACTB)
              phase("act%d%d" % (l, hf))

              P.release(off_h2)
              g_fpost = load_gain(1 + l * 4 + 3)
              g_next = load_gain(1 + (l + 1) * 4 + 0) if l + 1 < NL else None
              def fd_mm(tt, hh, ps):
                  for kc in range(NFF):
                      P.mm(ps, ACTB[:, kc, tt * 128:(tt + 1) * 128], WD[:, kc, hh * 512:(hh + 1) * 512],
                           start=(kc == 0), stop=(kc == NFF - 1))

              residual_stage(t_half, xm_d, x_lout, fd_mm, g_fpost, g_next, H)
              assert P.mark() <= wd_off, (P.mark(), wd_off)
              P.release(base)

    except _Stop:
        pass
    P.rwkv_top = rwkv_top[0]
    P.emit()
    return nc, P


_CACHE = {}


def host_tables(inp):
    cols = []
    for l in range(NL):
        rows = []
        rows.append(inp["mu_shift"][l].reshape(26, 128))
        rows.append(inp["pool_b"][l].reshape(4, 128))
        rows.append(inp["pool_scale"][l].reshape(4, 128))
        rows.append(inp["w0"][l].reshape(8, 128))
        rows.append(inp["a0"][l].reshape(8, 128))
        rows.append(inp["k_k"][l].reshape(8, 128))
        rows.append(inp["k_a"][l].reshape(8, 128))
        rows.append(inp["r_k"][l].reshape(8, 128))
        rows.append(inp["ln_x_w"][l].reshape(8, 128))
        rows.append(inp["ln_x_b"][l].reshape(8, 128))
        rows.append(inp["v0"][l - 1].reshape(8, 128) if l > 0 else np.zeros((8, 128), np.float32))
        rows.append(inp["gate_b"][l].reshape(24, 128))
        rows.append(inp["conv_w"][l].reshape(3 * 44, 128))
        rows.append(inp["conv_b"][l].reshape(44, 128))
        cols.append(np.concatenate(rows, 0))
    vt = np.ascontiguousarray(np.concatenate(cols, 0).T.astype(np.float32))
    gains = [inp["mem_norm"].reshape(1, D)]
    for l in range(NL):
        gains += [inp["norm_mix_pre"][l:l + 1], inp["norm_mix_post"][l:l + 1],
                  inp["norm_ffn_pre"][l:l + 1], inp["norm_ffn_post"][l:l + 1]]
    gains = np.ascontiguousarray(np.concatenate(gains, 0).astype(np.float32))
    return vt, gains


def make_in_maps(inp, n_cores, S):
    inp = {k: np.asarray(v) for k, v in inp.items()}
    vt, gains = host_tables(inp)
    cst = make_consts()
    shared = dict(cst=cst, vt=vt, gains=gains)
    for k in ("w_in", "pool_w", "w_proj_pool", "w_mem_kv", "w_proj_mem", "w_up_decay", "w_up_a", "w_up_g",
              "w_down_v", "w_up_v", "w_proj_rwkv", "w_o", "w_ffn_up", "w_ffn_down"):
        shared[k] = np.ascontiguousarray(inp[k], dtype=np.float32)
    maps = []
    for b in range(n_cores):
        m = dict(shared)
        m["x"] = np.ascontiguousarray(inp["x"][b, :S], dtype=np.float32)
        m["mem"] = np.ascontiguousarray(inp["mem"][b], dtype=np.float32)
        maps.append(m)
    return maps


def kernel(**inputs):
    S = 2048
    if "nc" not in _CACHE:
        _CACHE["nc"] = build(S)
    nc, P = _CACHE["nc"]
    maps = make_in_maps(inputs, 8, S)
    res = run_bass_kernel_spmd(nc, maps, core_ids=list(range(8)))
    out = np.stack([np.asarray(r["out"], dtype=np.float32) for r in res.results], 0)
    return out
```

```python
import contextlib
import numpy as np
import concourse.bass as bass
import concourse.mybir as mybir
from concourse.bass_utils import run_bass_kernel_spmd

F32 = mybir.dt.float32
BF16 = mybir.dt.bfloat16
U8 = mybir.dt.uint8
AF = mybir.ActivationFunctionType
ALU = mybir.AluOpType

EPOCH = 8192
NDMA = {"sp": 8, "act": 4, "pool": 8}
BUCKET = 1024
ENGMAP = {"pe": "tensor", "act": "scalar", "dve": "vector", "pool": "gpsimd", "sp": "sync"}


def _esize(dt):
    return mybir.dt.size(dt)


class Prog:
    def __init__(self, nc, sbuf_bytes=212736):
        self.nc = nc
        self.ops = []
        self.stack = contextlib.ExitStack()
        self.sems = {}
        self.dma_cnt = {q: 0 for q in NDMA}
        self.pstride = {}
        self.track = {}
        self.bank_last = {}
        self.arena = self.stack.enter_context(nc.sbuf_tensor("arena", [128, sbuf_bytes], U8))
        self.pstride["arena"] = sbuf_bytes
        self.sbuf_bytes = sbuf_bytes
        self.top = 0
        self.peak = 0
        self.psum = []
        for i in range(8):
            t = self.stack.enter_context(nc.psum_tensor("pb%d" % i, [128, 512], F32))
            self.pstride["pb%d" % i] = 2048
            self.psum.append(t)

    def mark(self):
        return self.top

    def release(self, m):
        self.top = m

    def tile(self, shape, dt):
        nb = int(np.prod(shape[1:])) * _esize(dt)
        off = self.top
        self.top += (nb + 63) // 64 * 64
        self.peak = max(self.peak, self.top)
        assert self.top <= self.sbuf_bytes, ("SBUF overflow", self.top)
        v = self.arena[:, off:off + nb].bitcast(dt)
        if len(shape) > 2:
            names = " ".join("d%d" % i for i in range(1, len(shape)))
            kw = {"d%d" % i: shape[i] for i in range(1, len(shape) - 1)}
            v = v.rearrange("p (%s) -> p %s" % (names, names), **kw)
        if shape[0] < 128:
            v = v[0:shape[0]]
        return v

    def bank(self, i, dt=F32):
        t = self.psum[i]
        return t[:, :] if dt == F32 else t[:, :].bitcast(dt)

    def _rect(self, a):
        name = a.tensor.name
        es = _esize(a.dtype)
        dims = a.ap
        boff = a.offset * es
        if name in self.pstride:
            ps = self.pstride[name]
            p0, f0 = divmod(boff, ps)
            if dims[0][0] * es == ps:
                pn = dims[0][1]
                rest = dims[1:]
            elif dims[0][0] == 0:
                pn = 1
                rest = dims[1:]
            else:
                pn = 1
                rest = dims
            ext = sum(s * (c - 1) for s, c in rest) + 1
            return (name, p0, p0 + pn, f0, f0 + ext * es)
        ext = sum(abs(s) * (c - 1) for s, c in dims) + 1
        return ("dram:" + name, 0, 1, boff, boff + ext * es)

    @staticmethod
    def _ovl(a, b):
        return a[1] < b[2] and b[1] < a[2] and a[3] < b[4] and b[3] < a[4]

    @staticmethod
    def _cov(a, b):
        return a[1] <= b[1] and a[2] >= b[2] and a[3] <= b[3] and a[4] >= b[4]

    def _buckets(self, r):
        bs = BUCKET if not r[0].startswith("dram:") else (1 << 20)
        return range(r[3] // bs, (r[4] - 1) // bs + 1)

    def _access(self, r, is_write, me, engkey, deps):
        t = self.track.setdefault(r[0], {})
        for b in self._buckets(r):
            d = t.setdefault(b, {"w": [], "r": {}})
            for (op, rr) in d["w"]:
                if self._ovl(r, rr):
                    deps.add(op)
            if is_write:
                for (ek, rr), op in d["r"].items():
                    if self._ovl(r, rr):
                        deps.add(op)
                d["w"] = [(op, rr) for (op, rr) in d["w"] if not self._cov(r, rr)]
                d["r"] = {k: op for k, op in d["r"].items() if not self._cov(r, k[1])}
                d["w"].append((me, r))
            else:
                d["r"][(engkey, r)] = me

    def op(self, eng, fn, outs=(), ins=(), dma=False):
        i = len(self.ops)
        deps = set()
        engkey = ("dma", i) if dma else eng
        for a in ins:
            if a is None or isinstance(a, (int, float)):
                continue
            self._access(self._rect(a), False, i, engkey, deps)
        for a in outs:
            if a is None:
                continue
            self._access(self._rect(a), True, i, engkey, deps)
        deps.discard(i)
        for a in list(ins) + list(outs):
            if a is None or isinstance(a, (int, float)):
                continue
            r = self._rect(a)
            if not r[0].startswith("pb"):
                continue
            for q in range(r[1] // 32, (r[2] - 1) // 32 + 1):
                d = self.bank_last.setdefault((r[0], q), {})
                for e2, v in d.items():
                    if e2 != eng:
                        deps.add(v)
                d[eng] = i
        deps.discard(i)
        o = dict(eng=eng, fn=fn, deps=deps, dma=dma)
        if dma:
            n = self.dma_cnt[eng]
            self.dma_cnt[eng] += 1
            o["dsem"] = ("d" + eng, n % NDMA[eng])
            o["dval"] = 16 * (n // NDMA[eng] + 1)
        self.ops.append(o)
        return i

    def mm(self, out, lhsT, rhs, start=True, stop=True):
        return self.op("pe", lambda e: e.matmul(out, lhsT=lhsT, rhs=rhs, start=start, stop=stop), [out], [lhsT, rhs])

    def transpose(self, out, in_, ident):
        return self.op("pe", lambda e: e.transpose(out, in_, ident), [out], [in_, ident])

    def act(self, out, in_, func, bias=None, scale=None, accum_out=None):
        kw = {}
        if bias is not None:
            kw["bias"] = bias
        if scale is not None:
            kw["scale"] = scale
        if accum_out is not None:
            kw["accum_out"] = accum_out
        ins = [in_] + [x for x in (bias, scale) if x is not None and not isinstance(x, (int, float))]
        return self.op("act", lambda e: e.activation(out, in_, func, **kw), [out, accum_out], ins)

    def tt(self, eng, out, in0, in1, op):
        return self.op(eng, lambda e: e.tensor_tensor(out, in0, in1, op), [out], [in0, in1])

    def ts(self, eng, out, in0, s1, s2, op0, op1=None):
        ins = [in0] + [x for x in (s1, s2) if x is not None and not isinstance(x, (int, float))]
        if op1 is None:
            return self.op(eng, lambda e: e.tensor_scalar(out, in0, s1, None, op0), [out], ins)
        return self.op(eng, lambda e: e.tensor_scalar(out, in0, s1, s2, op0, op1), [out], ins)

    def stt(self, eng, out, in0, scalar, in1, op0, op1):
        ins = [in0, in1] + ([scalar] if not isinstance(scalar, (int, float)) else [])
        return self.op(eng, lambda e: e.scalar_tensor_tensor(out, in0, scalar, in1, op0, op1), [out], ins)

    def copy(self, eng, out, in_):
        if eng == "act":
            return self.op("act", lambda e: e.copy(out, in_), [out], [in_])
        return self.op(eng, lambda e: e.tensor_copy(out, in_), [out], [in_])

    def memset(self, eng, ap, val):
        return self.op(eng, lambda e: e.memset(ap, val), [ap], [])

    def recip(self, out, in_):
        return self.op("dve", lambda e: e.reciprocal(out, in_), [out], [in_])

    def scan(self, out, d0, d1, init, op0, op1):
        return self.op("dve", lambda e: e.tensor_tensor_scan(out, d0, d1, init, op0, op1), [out], [d0, d1])

    def dma(self, q, out, in_):
        return self.op(q, lambda e: e.dma_start(out=out, in_=in_), [out], [in_], dma=True)

    def _token(self, o):
        if o["dma"]:
            return o["dsem"], o["dval"]
        c = o["cseq"]
        return (o["eng"], c // EPOCH), c % EPOCH + 1

    def sem(self, key):
        if key not in self.sems:
            self.sems[key] = self.stack.enter_context(self.nc.semaphore("s_%s_%s" % key))
        return self.sems[key]

    def emit(self):
        nc = self.nc
        cnt = {e: 0 for e in ENGMAP}
        for o in self.ops:
            if not o["dma"]:
                o["cseq"] = cnt[o["eng"]]
                cnt[o["eng"]] += 1
        for o in self.ops:
            self.sem(self._token(o)[0])
        self.nwaits = 0
        with nc.Block() as block:
            for e, bname in ENGMAP.items():
                mine = [o for o in self.ops if o["eng"] == e]
                if not mine:
                    continue

                def body(eng, e=e, mine=mine):
                    known = {}
                    for o in mine:
                        need = {}
                        for di in o["deps"]:
                            d = self.ops[di]
                            if d["eng"] == e and e == "pe" and not d["dma"]:
                                continue
                            sk, v = self._token(d)
                            if v > need.get(sk, 0):
                                need[sk] = v
                        if o["dma"]:
                            sk, v = o["dsem"], o["dval"] - 16
                            if v > 0 and v > need.get(sk, 0):
                                need[sk] = v
                        for sk in sorted(need):
                            v = need[sk]
                            if known.get(sk, 0) >= v:
                                continue
                            if not sk[0].startswith("d") and any(
                                k2[0] == sk[0] and k2[1] > sk[1] for k2 in known
                            ):
                                continue
                            eng.wait_ge(self.sems[sk], v)
                            self.nwaits += 1
                            known[sk] = v
                        ins = o["fn"](eng)
                        sk, v = self._token(o)
                        ins.then_inc(self.sems[sk], 16 if o["dma"] else 1)
                    if e in NDMA:
                        last = {}
                        for o in mine:
                            if o["dma"]:
                                last[o["dsem"]] = o["dval"]
                        for sk in sorted(last):
                            if known.get(sk, 0) < last[sk]:
                                eng.wait_ge(self.sems[sk], last[sk])

                getattr(block, bname)(body)


D = 1024
KC = 8
NL = 2
IN_COLS = 7424
D_FF = 2816
NFF = D_FF // 128
HS = 1024
TB = 512
NTB = HS // TB
CH = 64
NCH = TB // CH
C0 = float(np.exp(-0.5))
RATIO = 1
S1_EVERY = 2

VT_LAYOUT = [("mu", 26), ("pool_b", 4), ("pool_scale", 4), ("w0", 8), ("a0", 8), ("k_k", 8), ("k_a", 8),
             ("r_k", 8), ("ln_w", 8), ("ln_b", 8), ("v0", 8), ("gate_b", 24), ("conv_w", 132), ("conv_b", 44)]
VT_OFF = {}
_o = 0
for _n, _c in VT_LAYOUT:
    VT_OFF[_n] = _o
    _o += _c
NVL = _o

CST_IDENT = 0
CST_MASKC = 128
CST_CMASK = CST_MASKC + 512
CST_INVCNT = CST_CMASK + 512
CST_BDONES = CST_INVCNT + 16
CST_BDNEG = CST_BDONES + 128
NCST = CST_BDNEG + 128


def make_consts():
    c = np.zeros((128, NCST), np.float32)
    c[:, CST_IDENT:CST_IDENT + 128] = np.eye(128, dtype=np.float32)
    p = np.arange(128)
    hp_, tp = p // 64, p % 64
    q = np.arange(128)
    hq, tq = q // 64, q % 64
    same = (hp_[:, None] == hq[None, :])
    lower = same & (tq[None, :] < tp[:, None])
    upper = same & (tq[None, :] > tp[:, None])
    c[:, CST_MASKC:CST_MASKC + 128] = lower
    c[:, CST_MASKC + 128:CST_MASKC + 256] = upper
    c[:, CST_MASKC + 256:CST_MASKC + 384] = upper
    t64 = np.arange(64)
    incl = (tp[:, None] <= t64[None, :])
    c[:, CST_MASKC + 384:CST_MASKC + 448] = incl
    c[:, CST_MASKC + 448:CST_MASKC + 512] = incl
    cm = np.ones(512, np.float32)
    cm[::64] = 0.0
    c[:, CST_CMASK:CST_CMASK + 512] = cm[None, :]
    c[:, CST_INVCNT:CST_INVCNT + 16] = (1.0 / (np.arange(16) + 1.0))[None, :]
    c[:, CST_BDONES:CST_BDONES + 128] = same
    c[:, CST_BDNEG:CST_BDNEG + 128] = -same.astype(np.float32)
    return c


class _Stop(Exception):
    pass


def build(S=2048, taps=(), stop=None):
    NHF = S // HS
    nc = bass.Bass("TRN2", target_bir_lowering=False)
    P = Prog(nc)

    def din(name, shape):
        return nc.dram_tensor(name, list(shape), F32, kind="ExternalInput").ap()

    x_d = din("x", [S, D])
    mem_d = din("mem", [256, D])
    cst_d = din("cst", [128, NCST])
    vt_d = din("vt", [128, NL * NVL])
    gains_d = din("gains", [9, D])
    w_in_d = din("w_in", [NL, D, IN_COLS])
    pool_w_d = din("pool_w", [NL, 4, 128, 128])
    w_pp_d = din("w_proj_pool", [NL, 512, D])
    w_kv_d = din("w_mem_kv", [NL, D, 1024])
    w_pm_d = din("w_proj_mem", [NL, 512, D])
    w_upd_d = din("w_up_decay", [NL, 64, D])
    w_upa_d = din("w_up_a", [NL, 64, D])
    w_upg_d = din("w_up_g", [NL, 128, D])
    w_dv_d = din("w_down_v", [1, D, 32])
    w_uv_d = din("w_up_v", [1, 32, D])
    w_pr_d = din("w_proj_rwkv", [NL, D, D])
    w_o_d = din("w_o", [NL, D, D])
    w_fu_d = din("w_ffn_up", [NL, D, 2 * D_FF])
    w_fd_d = din("w_ffn_down", [NL, D_FF, D])
    out_d = nc.dram_tensor("out", [S, D], F32, kind="ExternalOutput").ap()
    xm_d = nc.dram_tensor("xm_scr", [S, D], F32, kind="Internal").ap()
    xl_d = nc.dram_tensor("xl_scr", [S, D], F32, kind="Internal").ap()
    vf_d = nc.dram_tensor("vf_scr", [8, 128, S], F32, kind="Internal").ap()
    v1_d = nc.dram_tensor("v1_scr", [8, 128, S], F32, kind="Internal").ap()
    tap_d = {}
    for name, shape in taps:
        tap_d[name] = nc.dram_tensor("tap_" + name, list(shape), F32, kind="ExternalOutput").ap()

    cst = P.tile([128, NCST], F32)
    VT = P.tile([128, NL * NVL], F32)
    OMU = P.tile([128, NL * 26], F32)
    OMKA = P.tile([128, NL * 8], F32)
    EPS = P.tile([128, 4], F32)
    ident_b = P.tile([128, 128], BF16)
    bdones_b = P.tile([128, 128], BF16)
    ones_b = P.tile([128, 128], BF16)
    memT = P.tile([128, 8, 256], BF16)
    kT = [P.tile([128, 4, 256], BF16) for _ in range(NL)]
    Vt = [P.tile([128, 2, 512], BF16) for _ in range(NL)]
    Hst = [P.tile([128, 8, 128], F32) for _ in range(NL)]
    zc = [P.tile([128, 26], F32) for _ in range(NL)]
    pc = [P.tile([128, 4, 16], F32) for _ in range(NL)]
    cc = [P.tile([128, 2 * NFF, 2], F32) for _ in range(NL)]
    H = P.tile([128, 8, HS], BF16)
    WB = [P.tile([128, 8, 256], BF16) for _ in range(4)]
    wb_i = [0]
    big0 = P.mark()
    YR = P.tile([128, 8, HS], BF16)
    off_po = P.mark()
    PO = P.tile([128, 4, HS], BF16)
    AO = P.tile([128, 4, HS], BF16)
    M = P.tile([128, 8, HS], BF16)
    big1 = P.mark()
    P.release(big0)
    ACTB = P.tile([128, NFF, HS], BF16)
    assert P.mark() <= big1
    P.release(big1)
    off_h2 = P.mark()
    H2 = P.tile([128, 8, HS], BF16)
    base = P.mark()

    ident_f = cst[:, CST_IDENT:CST_IDENT + 128]
    maskc = cst[:, CST_MASKC:CST_MASKC + 512]
    cmask = cst[:, CST_CMASK:CST_CMASK + 512]
    invcnt = cst[:, CST_INVCNT:CST_INVCNT + 16]

    def vcol(l, name, j=0):
        o = l * NVL + VT_OFF[name] + j
        return VT[:, o:o + 1]

    P.dma("sp", cst, cst_d)
    P.dma("sp", VT, vt_d)
    P.copy("dve", ident_b, ident_f)
    P.copy("dve", bdones_b, cst[:, CST_BDONES:CST_BDONES + 128])
    P.memset("pool", ones_b, 1.0)
    P.memset("pool", EPS[:, 0:1], 1e-6)
    P.memset("pool", EPS[:, 1:2], 64e-5)
    P.memset("pool", EPS[:, 2:3], 1e-12)
    for l in range(NL):
        P.memset("pool", Hst[l], 0.0)
        P.memset("pool", zc[l], 0.0)
        P.memset("pool", pc[l], 0.0)
        P.memset("pool", cc[l], 0.0)
        o = l * NVL + VT_OFF["mu"]
        P.ts("dve", OMU[:, l * 26:(l + 1) * 26], VT[:, o:o + 26], -1.0, 1.0, ALU.mult, ALU.add)
        o = l * NVL + VT_OFF["k_a"]
        P.ts("dve", OMKA[:, l * 8:(l + 1) * 8], VT[:, o:o + 8], -1.0, 1.0, ALU.mult, ALU.add)

    def load_gain(idx):
        g = P.tile([128, D], F32)
        P.dma("sp", g, gains_d[idx:idx + 1, :].to_broadcast([128, D]))
        return g

    def tap(name, ap_sb):
        if name in tap_d:
            m_ = P.mark()
            tmp = P.tile([128, HS], F32)
            for kc in range(ap_sb.shape[1]):
                P.copy("dve", tmp, ap_sb[:, kc, :])
                P.dma("sp", tap_d[name][:, kc, :], tmp)
            P.release(m_)

    def tm_norm(src, gain, hn, junk, st):
        P.act(junk, src, AF.Square, accum_out=st[:, 0:1])
        P.act(st[:, 1:2], st[:, 0:1], AF.Sqrt, bias=EPS[:, 0:1], scale=1.0 / D)
        P.recip(st[:, 2:3], st[:, 1:2])
        P.stt("dve", hn, src, st[:, 2:3], gain, ALU.mult, ALU.mult)

    def to_fm(hn, dst, t0, bank=7):
        pt = P.bank(bank, BF16)
        for kc in range(8):
            P.transpose(pt[:, kc * 128:(kc + 1) * 128], hn[:, kc * 128:(kc + 1) * 128], ident_b)
        P.copy("act", dst[:, :, t0:t0 + 128], pt[:, 0:1024].rearrange("p (k t) -> p k t", k=8))

    m0 = P.mark()
    g_mem = load_gain(0)
    mt_x = P.tile([128, D], F32)
    mt_h = P.tile([128, D], BF16)
    mt_j = P.tile([128, D], BF16)
    mt_s = P.tile([128, 4], F32)
    for mt in range(2):
        P.dma("sp", mt_x, mem_d[mt * 128:(mt + 1) * 128, :])
        tm_norm(mt_x, g_mem, mt_h, mt_j, mt_s)
        to_fm(mt_h, memT, mt * 128)
    P.release(m0)

    proj_bank = [0]
    rwkv_top = [0]

    def load_w(dst, src):
        P.dma("pool", dst, src)

    def proj(w2d, chunks, rhs, consume, kchunks=8, pf=2):
        groups = []
        i = 0
        while i < len(chunks):
            n = 2 if (i + 1 < len(chunks) and chunks[i + 1] == chunks[i] + 1) else 1
            groups.append((chunks[i], n))
            i += n
        loaded = {}

        def issue(gi):
            c0, n = groups[gi]
            wb = WB[wb_i[0] % 4]
            wb_i[0] += 1
            load_w(wb[:, 0:kchunks, 0:n * 128],
                   w2d[:, c0 * 128:(c0 + n) * 128].rearrange("(kc p) c -> p kc c", p=128))
            loaded[gi] = wb

        for gi in range(min(pf, len(groups))):
            issue(gi)
        ci = 0
        for gi, (c0, n) in enumerate(groups):
            if gi + pf < len(groups):
                issue(gi + pf)
            wb = loaded.pop(gi)
            for j in range(n):
                for tb in range(NTB):
                    ps = P.bank(proj_bank[0] % 3)
                    proj_bank[0] += 1
                    for kc in range(kchunks):
                        P.mm(ps, wb[:, kc, j * 128:(j + 1) * 128], rhs[:, kc, tb * TB:(tb + 1) * TB],
                             start=(kc == 0), stop=(kc == kchunks - 1))
                    consume(ci, c0 + j, tb, ps)
                ci += 1

    def residual_stage(t_half, src_x, dst_x, mm_fn, g_post, g_pre, Hdst):
        n = HS // 128
        xt = [P.tile([128, D], F32) for _ in range(3)]
        yt = [P.tile([128, D], F32) for _ in range(3)]
        hn = [P.tile([128, D], BF16) for _ in range(3)]
        jkA = P.tile([128, D], BF16)
        jkC = P.tile([128, D], BF16)
        st = [P.tile([128, 8], F32) for _ in range(3)]

        def stA(tt):
            b = tt % 3
            r0 = t_half + tt * 128
            P.dma("sp", xt[b], src_x[r0:r0 + 128, :])
            for hh in range(2):
                ps = P.bank((tt % 2) * 2 + hh)
                mm_fn(tt, hh, ps)
                P.copy("act", yt[b][:, hh * 512:(hh + 1) * 512], ps)

        def stB(tt):
            b = tt % 3
            r0 = t_half + tt * 128
            P.act(jkA, yt[b], AF.Square, accum_out=st[b][:, 0:1])
            P.act(st[b][:, 1:2], st[b][:, 0:1], AF.Sqrt, bias=EPS[:, 0:1], scale=1.0 / D)
            P.recip(st[b][:, 2:3], st[b][:, 1:2])
            P.stt("dve", yt[b], yt[b], st[b][:, 2:3], g_post, ALU.mult, ALU.mult)
            P.tt("dve", xt[b], xt[b], yt[b], ALU.add)
            P.dma("sp", dst_x[r0:r0 + 128, :], xt[b])

        def stC1(tt):
            b = tt % 3
            if g_pre is not None:
                tm_norm(xt[b], g_pre, hn[b], jkC, st[b][:, 4:8])

        def stC2(tt):
            b = tt % 3
            if g_pre is not None:
                to_fm(hn[b], Hdst, tt * 128)

        for step in range(n + 3):
            if 0 <= step - 3 < n:
                stC2(step - 3)
            if 0 <= step - 2 < n:
                stC1(step - 2)
            if 0 <= step - 1 < n:
                stB(step - 1)
            if step < n:
                stA(step)

    def phase(name):
        if stop == name:
            raise _Stop()

    try:
      phase('init')
      for hf in range(NHF):
          t_half = hf * HS
          for l in range(NL):
              w_in_l = w_in_d[l]
              if l == 0:
                  m0 = P.mark()
                  g_pre = load_gain(1)
                  xt = [P.tile([128, D], F32) for _ in range(2)]
                  hn = [P.tile([128, D], BF16) for _ in range(2)]
                  jk = P.tile([128, D], BF16)
                  st = [P.tile([128, 4], F32) for _ in range(2)]
                  for tt in range(HS // 128 + 1):
                      b = tt % 2
                      if tt < HS // 128:
                          P.dma("sp", xt[b], x_d[t_half + tt * 128:t_half + (tt + 1) * 128, :])
                          tm_norm(xt[b], g_pre, hn[b], jk, st[b])
                      if tt >= 1:
                          to_fm(hn[(tt - 1) % 2], H, (tt - 1) * 128)
                  P.release(m0)
              if (l, hf) == (0, 0):
                  tap("h0", H)
              phase("h%d%d" % (l, hf))
              x_res = x_d if l == 0 else xl_d
              x_lout = xl_d if l == 0 else out_d

              if hf == 0:
                  m0 = P.mark()
                  for g4 in range(4):
                      wb = WB[wb_i[0] % 4]
                      wb_i[0] += 1
                      load_w(wb, w_kv_d[l][:, g4 * 256:(g4 + 1) * 256].rearrange("(kc p) c -> p kc c", p=128))
                      if g4 < 2:
                          for j in range(2):
                              hh = g4 * 2 + j
                              ps = P.bank(3)
                              for kc in range(8):
                                  P.mm(ps[:, 0:256], wb[:, kc, j * 128:(j + 1) * 128], memT[:, kc, :],
                                       start=(kc == 0), stop=(kc == 7))
                              P.copy("act", kT[l][:, hh, :], ps[:, 0:256])
                      else:
                          for mt in range(2):
                              ps = P.bank(3)
                              for kc in range(8):
                                  P.mm(ps[:, 0:256], memT[:, kc, mt * 128:(mt + 1) * 128], wb[:, kc, :],
                                       start=(kc == 0), stop=(kc == 7))
                              P.copy("act", Vt[l][:, mt, (g4 - 2) * 256:(g4 - 1) * 256], ps[:, 0:256])
                  P.release(m0)

              phase('kv%d%d' % (l, hf))
              P.release(off_po)
              WDA = P.tile([128, D], BF16)
              WG = P.tile([128, D], BF16)
              LA = P.tile([128, HS], BF16)
              SDG = P.tile([128, HS], BF16)
              load_w(WDA[0:64, :], w_upd_d[l])
              load_w(WDA[64:128, :], w_upa_d[l])
              load_w(WG, w_upg_d[l])
              Zt = [P.tile([128, HS + 1], F32) for _ in range(3)]
              ZR = [z_[:, 0:HS] for z_ in Zt]

              def shift_consume(j, Z):
                  mu_c = VT[:, l * NVL + VT_OFF["mu"] + j:l * NVL + VT_OFF["mu"] + j + 1]
                  omu_c = OMU[:, l * 26 + j:l * 26 + j + 1]

                  def consume(ci, chunk, tb, ps):
                      if tb == 0:
                          P.copy("dve", Z[:, 0:1], zc[l][:, j:j + 1])
                      P.act(Z[:, 1 + tb * TB:1 + (tb + 1) * TB], ps, AF.Copy, scale=mu_c)
                      P.stt("dve", Z[:, tb * TB:(tb + 1) * TB], ps, omu_c, Z[:, tb * TB:(tb + 1) * TB],
                            ALU.mult, ALU.add)
                      if tb == NTB - 1:
                          P.copy("dve", zc[l][:, j:j + 1], Z[:, HS:HS + 1])
                  return consume

              _lc = {32: shift_consume(24, Zt[0]), 33: shift_consume(25, Zt[1])}
              proj(w_in_l, [32, 33], H, lambda ci, chunk, tb, ps: _lc[chunk](ci, chunk, tb, ps))
              P.act(LA[0:64, :], ZR[0][0:64, :], AF.Tanh)
              P.copy("pool", LA[64:128, :], ZR[0][64:128, :])
              P.act(SDG, ZR[1], AF.Sigmoid)
              phase('lora%d%d' % (l, hf))

              if l == 1:
                  WDV = P.tile([128, 8, 32], BF16)
                  WUV = P.tile([32, D], BF16)
                  VDB = P.tile([32, HS], BF16)
                  vb = P.tile([128, HS], BF16)
                  VFT = P.tile([128, HS], F32)
                  VG = P.tile([128, TB], F32)
                  load_w(WDV, w_dv_d[0].rearrange("(hp p) r -> p hp r", p=128))
                  load_w(WUV, w_uv_d[0])
                  vb2 = P.tile([128, HS], BF16)
                  vbs = [vb, vb2]
                  _vc = [shift_consume(16 + hp, Zt[hp % 3]) for hp in range(8)]

                  def vpass_consume(ci, chunk, tb, ps):
                      hp = chunk - 24
                      _vc[hp](ci, chunk, tb, ps)
                      if tb == NTB - 1:
                          zr_ = ZR[hp % 3]
                          vb_ = vbs[hp % 2]
                          P.dma("sp", v1_d[hp][:, t_half:t_half + HS], zr_)
                          P.copy("act", vb_, zr_)
                          for tb2 in range(NTB):
                              P.mm(P.bank(3 + tb2)[0:32, :], WDV[:, hp, :], vb_[:, tb2 * TB:(tb2 + 1) * TB],
                                   start=(hp == 0), stop=(hp == 7))

                  proj(w_in_l, [24 + hp for hp in range(8)], H, vpass_consume)
                  for tb in range(NTB):
                      P.copy("act", VDB[:, tb * TB:(tb + 1) * TB], P.bank(3 + tb)[0:32, :])

              SL = [P.tile([128, TB], F32) for _ in range(7)]
              SG, AA, LP, TMP, E2, E3, KK = SL
              KA, BT32, KT32 = [P.tile([128, TB], BF16) for _ in range(3)]
              MKB = P.tile([128, 2, 128], BF16)
              P.copy("dve", MKB[:, 0, :], cst[:, CST_BDONES:CST_BDONES + 128])
              P.copy("dve", MKB[:, 1, :], cst[:, CST_BDNEG:CST_BDNEG + 128])
              RS_a, TT_, RKa, KEFF = [P.tile([128, TB], F32) for _ in range(4)]
              KKR, BN = TMP, TMP
              SQ = P.tile([128, TB], BF16)
              RK = P.tile([128, TB], BF16)
              BBD, KBD, BHBD, KHBD, VBD = [P.tile([128, NCH, 128], BF16) for _ in range(5)]
              AMCp = P.tile([128, NCH, 256], BF16)
              PP = [P.tile([128, NCH, 2, 128], BF16) for _ in range(2)]
              TTs = P.tile([128, NCH, 128], BF16)
              Xb = P.tile([128, 128], BF16)
              Ub = P.tile([128, 128], BF16)
              Hb = P.tile([128, 128], BF16)
              YC = P.tile([128, TB], F32)
              RSq = P.tile([128, TB], F32)
              YB = P.tile([128, TB], BF16)
              SQ2 = P.tile([128, TB], BF16)
              QS = []
              for _ in range(2):
                  qs = dict(ABD=P.tile([128, NCH, 128], BF16), VTM=P.tile([128, NCH, 128], BF16),
                            BHTM=P.tile([128, NCH, 128], BF16), KHTM=P.tile([128, NCH, 128], BF16),
                            AMCq=P.tile([128, NCH, 256], BF16), TT=P.tile([128, NCH, 128], BF16))
                  QS.append(qs)
              S3 = [dict(RT=P.tile([128, TB], BF16), BON=P.tile([128, TB], BF16), Dq=P.tile([128, NCH, 1], F32))
                    for _ in range(3)]
              ZRr, ZRk, ZRv = ZR
              blocks = [(hp, tb) for hp in range(8) for tb in range(NTB)]
              blkmask = cst[:, CST_BDONES:CST_BDONES + 128]
              m4 = MKB[:, 0, :].rearrange("p (h t) -> p h t", h=2).unsqueeze(1).to_broadcast([128, NCH, 2, CH])
              m4f = blkmask.rearrange("p (h t) -> p h t", h=2).unsqueeze(1).to_broadcast([128, NCH, 2, CH])

              negmask = cst[:, CST_BDNEG:CST_BDNEG + 128]
              m4n = MKB[:, 1, :].rearrange("p (h t) -> p h t", h=2).unsqueeze(1).to_broadcast([128, NCH, 2, CH])

              def bdw(eng, dst, src, neg=False):
                  x4 = src.rearrange("p (c t) -> p c t", t=CH).unsqueeze(2).to_broadcast([128, NCH, 2, CH])
                  mk = m4n if neg else (m4 if src.dtype == BF16 else m4f)
                  P.tt(eng, dst.rearrange("p c (h t) -> p c h t", h=2), x4, mk, ALU.mult)

              def sbank():
                  ps = P.bank(proj_bank[0] % 3)
                  proj_bank[0] += 1
                  return ps

              def genS1(bi):
                  hp, tb = blocks[bi]
                  RT, BONq, Dq = (S3[bi % 3][k] for k in ("RT", "BON", "Dq"))
                  if tb == 0:
                      _rc = {8 + hp: shift_consume(hp, Zt[0]), 16 + hp: shift_consume(8 + hp, Zt[1]),
                             24 + hp: shift_consume(16 + hp, Zt[2])}
                      proj(w_in_l, [8 + hp, 16 + hp] + ([24 + hp] if l == 0 else []), H,
                           lambda ci, chunk, tb_, ps: _rc[chunk](ci, chunk, tb_, ps), pf=3)
                      yield
                      if l == 0:
                          P.dma("sp", vf_d[hp][:, t_half:t_half + HS], ZRv)
                      else:
                          P.dma("sp", ZRv, v1_d[hp][:, t_half:t_half + HS])
                          P.dma("sp", VFT, vf_d[hp][:, t_half:t_half + HS])
                          P.tt("pool", VFT, VFT, ZRv, ALU.subtract)
                          for tb2 in range(NTB):
                              sl2 = slice(tb2 * TB, (tb2 + 1) * TB)
                              ps = sbank()
                              P.mm(ps, WUV[:, hp * 128:(hp + 1) * 128], VDB[:, sl2])
                              P.act(VG, ps, AF.Sigmoid, bias=vcol(l, "v0", hp))
                              P.tt("dve", VFT[:, sl2], VFT[:, sl2], VG, ALU.mult)
                          P.tt("pool", ZRv, ZRv, VFT, ALU.add)
                      yield
                  hs_ = slice(hp * 128, (hp + 1) * 128)
                  sl = slice(tb * TB, (tb + 1) * TB)
                  zr, zk, zv = ZRr[:, sl], ZRk[:, sl], ZRv[:, sl]
                  KK3 = KK.rearrange("p (c t) -> p c t", t=CH)
                  E33 = E3.rearrange("p (c t) -> p c t", t=CH)
                  KA3 = KA.rearrange("p (c t) -> p c t", t=CH)
                  Dv = E33[:, :, CH - 1:CH]
                  ps_w = sbank()
                  P.mm(ps_w, WDA[0:64, hs_], LA[0:64, sl])
                  ps_a = sbank()
                  P.mm(ps_a, WDA[64:128, hs_], LA[64:128, sl])
                  P.act(KKR, zk, AF.Copy, scale=vcol(l, "k_k", hp))
                  P.act(RKa, zr, AF.Copy, scale=vcol(l, "r_k", hp))
                  yield
                  P.act(SG, ps_w, AF.Sigmoid, bias=vcol(l, "w0", hp))
                  P.act(AA, ps_a, AF.Sigmoid, bias=vcol(l, "a0", hp))
                  P.act(SQ, KKR, AF.Square)
                  yield
                  P.scan(LP, cmask, SG, 0.0, ALU.mult, ALU.add)
                  ps_s = sbank()
                  P.mm(ps_s, bdones_b, SQ)
                  P.act(TT_, AA, AF.Identity, bias=OMKA[:, l * 8 + hp:l * 8 + hp + 1], scale=vcol(l, "k_a", hp))
                  yield
                  P.act(RS_a, ps_s, AF.Ln, bias=EPS[:, 2:3])
                  P.act(E2, LP, AF.Exp, scale=C0)
                  P.act(E3, LP, AF.Exp, scale=-C0)
                  yield
                  P.act(RS_a, RS_a, AF.Exp, scale=-0.5)
                  P.tt("pool", KEFF, zk, TT_, ALU.mult)
                  yield
                  P.tt("pool", KK, KKR, RS_a, ALU.mult)
                  P.tt("pool", KT32, KEFF, E2, ALU.mult)
                  P.tt("pool", RK, RKa, KEFF, ALU.mult)
                  P.tt("pool", RT, zr, E3, ALU.mult)
                  P.copy("pool", Dq, Dv)
                  yield
                  P.tt("pool", BN, KK, AA, ALU.mult)
                  ps_b = sbank()
                  P.mm(ps_b, bdones_b, RK)
                  P.tt("pool", KA3[:, :, 1:CH], KK3[:, :, 1:CH], E33[:, :, 0:CH - 1], ALU.mult)
                  P.copy("pool", KA3[:, :, 0:1], KK3[:, :, 0:1])
                  yield
                  P.tt("pool", BT32, BN, E2, ALU.mult)
                  P.tt("dve", BONq, ps_b, zv, ALU.mult)
                  yield

              def stageP(bi):
                  hp, tb = blocks[bi]
                  qs = QS[bi % 2]
                  ABD, VTM, BHTM, KHTM, AMCq, TTq = (qs[k] for k in ("ABD", "VTM", "BHTM", "KHTM", "AMCq", "TT"))
                  RT = S3[bi % 3]["RT"]
                  sl = slice(tb * TB, (tb + 1) * TB)
                  zv = ZRv[:, sl]
                  Dv = E3.rearrange("p (c t) -> p c t", t=CH)[:, :, CH - 1:CH]
                  bdw("dve", ABD, KA, neg=True)
                  bdw("dve", BBD, BT32)
                  bdw("dve", KBD, KT32)
                  bdw("pool", VBD, zv)
                  yield
                  Db = Dv.to_broadcast([128, NCH, 128])
                  P.tt("dve", BHBD, BBD, Db, ALU.mult)
                  P.tt("pool", KHBD, KBD, Db, ALU.mult)
                  yield
                  for src, dst, e_ in ((BHBD, BHTM, "act"), (KHBD, KHTM, "dve"), (VBD, VTM, "act")):
                      pt = P.bank(3, BF16)
                      for c in range(NCH):
                          P.transpose(pt[:, c * 128:(c + 1) * 128], src[:, c, :], ident_b)
                      P.copy(e_, dst, pt[:, 0:1024].rearrange("p (c t) -> p c t", c=NCH))
                      yield
                  for c in range(NCH):
                      ps = P.bank(4 + c % 2)
                      cs = slice(c * CH, (c + 1) * CH)
                      P.mm(ps[:, 0:128], ABD[:, c, :], BBD[:, c, :])
                      P.mm(ps[:, 128:256], BBD[:, c, :], ABD[:, c, :])
                      P.mm(ps[:, 256:384], KBD[:, c, :], ABD[:, c, :])
                      P.mm(ps[:, 384:448], BBD[:, c, :], RT[:, cs])
                      P.mm(ps[:, 448:512], KBD[:, c, :], RT[:, cs])
                      P.tt("dve", AMCp[:, c, :], ps[:, 0:256], maskc[:, 0:256], ALU.mult)
                      P.tt("dve", AMCq[:, c, :], ps[:, 256:512], maskc[:, 256:512], ALU.mult)
                      if c % 2 == 1:
                          yield
                  for c in range(NCH):
                      P.tt("pool", TTs[:, c, :], AMCp[:, c, 128:256], ident_f, ALU.add)
                  TTb = [TTs, TTq]

                  def sq(lev):
                      for c2 in range(NCH // 2):
                          ps = P.bank(4 + c2 % 2)
                          for i in range(2):
                              c = c2 * 2 + i
                              if lev == 1:
                                  p_prev, pt_prev = AMCp[:, c, 0:128], AMCp[:, c, 128:256]
                              else:
                                  p_prev, pt_prev = PP[(lev - 1) % 2][:, c, 0, :], PP[(lev - 1) % 2][:, c, 1, :]
                              P.mm(ps[:, (i * 2) * 128:(i * 2 + 1) * 128], pt_prev, p_prev)
                              if lev < 5:
                                  P.mm(ps[:, (i * 2 + 1) * 128:(i * 2 + 2) * 128], p_prev, pt_prev)
                          e_ = "act" if c2 % 2 == 0 else "dve"
                          if lev < 5:
                              P.copy(e_, PP[lev % 2][:, c2 * 2:c2 * 2 + 2, :, :],
                                     ps.rearrange("p (c a t) -> p c a t", c=2, a=2))
                          else:
                              P.copy(e_, PP[lev % 2][:, c2 * 2:c2 * 2 + 2, 0, :],
                                     ps.rearrange("p (c a t) -> p c a t", c=2, a=2)[:, :, 0, :])
                          if c2 % 2 == 1:
                              yield

                  def ttapply(lev, cur):
                      for c4 in range(NCH // 4):
                          ps = P.bank(3)
                          for i in range(4):
                              c = c4 * 4 + i
                              P.mm(ps[:, i * 128:(i + 1) * 128], PP[lev % 2][:, c, 0, :], TTb[cur][:, c, :])
                          P.tt("dve", TTb[1 - cur][:, c4 * 4:c4 * 4 + 4, :],
                               ps.rearrange("p (c t) -> p c t", c=4), TTb[cur][:, c4 * 4:c4 * 4 + 4, :], ALU.add)
                          yield

                  def inverse():
                      cur = 0
                      yield from sq(1)
                      for lev in range(1, 6):
                          if lev < 5:
                              yield from sq(lev + 1)
                          yield from ttapply(lev, cur)
                          cur = 1 - cur
                      assert cur == 1

                  gi = inverse()
                  gs = genS1(bi + 1) if bi + 1 < len(blocks) else None
                  di, ds = False, gs is None
                  k_ = 0
                  while not (di and ds):
                      if not di:
                          try:
                              next(gi)
                          except StopIteration:
                              di = True
                      k_ += 1
                      if not ds and (di or k_ % S1_EVERY == 0):
                          try:
                              next(gs)
                          except StopIteration:
                              ds = True
                      yield

              def stageQ(bi):
                  hp, tb = blocks[bi]
                  qs = QS[bi % 2]
                  ABD, VTM, BHTM, KHTM, AMCq, TTq = (qs[k] for k in ("ABD", "VTM", "BHTM", "KHTM", "AMCq", "TT"))
                  RT, BONq, Dq = (S3[bi % 3][k] for k in ("RT", "BON", "Dq"))
                  hs_ = slice(hp * 128, (hp + 1) * 128)
                  sl = slice(tb * TB, (tb + 1) * TB)
                  Hs = Hst[l][:, hp, :]
                  P.copy("act", Hb, Hs)
                  psY = P.bank(7)
                  for c in range(NCH):
                      cs = slice(c * CH, (c + 1) * CH)
                      ps = P.bank(6)
                      P.mm(ps[:, 0:128], ABD[:, c, :], Hb, start=True, stop=False)
                      P.mm(ps[:, 0:128], AMCq[:, c, 0:128], VTM[:, c, :], start=False, stop=True)
                      P.copy("act", Xb, ps[:, 0:128])
                      yield
                      P.mm(ps[:, 128:256], TTq[:, c, :], Xb)
                      P.copy("act", Ub, ps[:, 128:256])
                      yield
                      P.mm(psY[:, cs], Hb, RT[:, cs], start=True, stop=False)
                      P.mm(psY[:, cs], Ub, AMCq[:, c, 128:192], start=False, stop=False)
                      P.mm(psY[:, cs], VTM[:, c, :], AMCq[:, c, 192:256], start=False, stop=True)
                      P.mm(ps[:, 256:384], BHTM[:, c, :], Ub, start=True, stop=False)
                      P.mm(ps[:, 256:384], KHTM[:, c, :], VTM[:, c, :], start=False, stop=True)
                      P.stt("dve", Hs, Hs, Dq[:, c, :], ps[:, 256:384], ALU.mult, ALU.add)
                      P.copy("act", Hb, Hs)
                      yield
                  P.copy("act", YC, psY)
                  P.copy("dve", YB, psY)
                  ps = P.bank(6)
                  P.mm(ps, bdones_b, YB)
                  P.stt("dve", YC, ps, -1.0 / 64, YC, ALU.mult, ALU.add)
                  P.act(SQ2, YC, AF.Square)
                  yield
                  ps = P.bank(7)
                  P.mm(ps, bdones_b, SQ2)
                  P.act(RSq, ps, AF.Ln, bias=EPS[:, 1:2], scale=1.0 / 64)
                  P.act(RSq, RSq, AF.Exp, scale=-0.5)
                  P.tt("dve", YC, YC, RSq, ALU.mult)
                  P.ts("dve", YC, YC, vcol(l, "ln_w", hp), vcol(l, "ln_b", hp), ALU.mult, ALU.add)
                  P.tt("pool", YC, YC, BONq, ALU.add)
                  yield
                  ps = P.bank(6)
                  P.mm(ps, WG[:, hs_], SDG[:, sl])
                  P.tt("dve", YR[:, hp, sl], YC, ps, ALU.mult)
                  yield

              def run_gens(gq, gp, ratio=RATIO):
                  dq = gq is None
                  dp = gp is None
                  while not (dq and dp):
                      if not dq:
                          try:
                              next(gq)
                          except StopIteration:
                              dq = True
                      for _ in range(ratio if not dq else 1000000):
                          if dp:
                              break
                          try:
                              next(gp)
                          except StopIteration:
                              dp = True

              rwkv_top[0] = P.mark()
              run_gens(None, genS1(0))
              run_gens(None, stageP(0))
              for bi in range(len(blocks)):
                  run_gens(stageQ(bi), stageP(bi + 1) if bi + 1 < len(blocks) else None)
              P.release(base)
              if (l, hf) == (0, 0):
                  tap("yr0", YR)
              phase("yr%d%d" % (l, hf))

              mP = P.mark()
              ZP = P.tile([128, 16 + HS], F32)
              SA = P.tile([128, 16 + HS], F32)
              SB_ = P.tile([128, 16 + HS], F32)
              PLD = P.tile([128, HS], BF16)
              PW = P.tile([128, 4, 128], BF16)
              load_w(PW, pool_w_d[l].rearrange("g c d -> c g d"))

              def pool_consume(ci, chunk, tb, ps):
                  g = chunk
                  if tb == 0:
                      P.copy("dve", ZP[:, 0:16], pc[l][:, g, :])
                  P.copy("act", ZP[:, 16 + tb * TB:16 + (tb + 1) * TB], ps)
                  if tb < NTB - 1:
                      return
                  P.copy("dve", pc[l][:, g, :], ZP[:, HS:HS + 16])
                  win = (2, 4, 8, 16)[g]
                  src = ZP
                  step = 1
                  bufs = [SA, SB_]
                  bi = 0
                  while step < win:
                      dst = bufs[bi]
                      bi = 1 - bi
                      lo = 2 * step - 1
                      P.tt("dve" if step in (1, 4) else "pool", dst[:, lo:16 + HS], src[:, lo:16 + HS],
                           src[:, lo - step:16 + HS - step], ALU.add)
                      src = dst
                      step *= 2
                  P.stt("dve", PLD, src[:, 16:16 + HS], 1.0 / win, ZP[:, 16:16 + HS], ALU.mult, ALU.subtract)
                  if hf == 0:
                      n = win - 1
                      P.tt("pool", SA[:, 0:n], src[:, 16:16 + n], invcnt[:, 0:n], ALU.mult)
                      P.tt("pool", PLD[:, 0:n], SA[:, 0:n], ZP[:, 16:16 + n], ALU.subtract)
                  for tb2 in range(NTB):
                      ps2 = P.bank(3 + tb2)
                      P.mm(ps2, PW[:, g, :], PLD[:, tb2 * TB:(tb2 + 1) * TB])
                      P.ts("dve", PO[:, g, tb2 * TB:(tb2 + 1) * TB], ps2, vcol(l, "pool_b", g), vcol(l, "pool_scale", g),
                           ALU.add, ALU.mult)

              proj(w_in_l, [0, 1, 2, 3], H, pool_consume)
              P.release(mP)
              if (l, hf) == (0, 0):
                  tap("po0", PO)
              phase("po%d%d" % (l, hf))

              mA = P.mark()
              QT = P.tile([128, HS], BF16)
              EX = P.tile([128, 2, TB], BF16)
              RD = P.tile([128, TB], F32)

              def attn_consume(ci, chunk, tb, ps):
                  hh = chunk - 4
                  P.copy("act", QT[:, tb * TB:(tb + 1) * TB], ps)
                  q = QT[:, tb * TB:(tb + 1) * TB]
                  for mt in range(2):
                      pss = P.bank(3 + mt)
                      P.mm(pss, kT[l][:, hh, mt * 128:(mt + 1) * 128], q)
                      P.act(EX[:, mt, :], pss, AF.Exp, scale=float(128 ** -0.5))
                  psd = P.bank(5)
                  pso = P.bank(6)
                  for mt in range(2):
                      P.mm(psd, ones_b, EX[:, mt, :], start=(mt == 0), stop=(mt == 1))
                  for mt in range(2):
                      P.mm(pso, Vt[l][:, mt, hh * 128:(hh + 1) * 128], EX[:, mt, :], start=(mt == 0), stop=(mt == 1))
                  P.recip(RD, psd)
                  P.tt("dve", AO[:, hh, tb * TB:(tb + 1) * TB], pso, RD, ALU.mult)

              proj(w_in_l, [4, 5, 6, 7], H, attn_consume)
              P.release(mA)
              if (l, hf) == (0, 0):
                  tap("ao0", AO)
              phase("ao%d%d" % (l, hf))

              mM = P.mark()
              WO = P.tile([128, 8, D], BF16)

              def load_wo(g4):
                  load_w(WO[:, :, g4 * 256:(g4 + 1) * 256],
                         w_o_d[l][:, g4 * 256:(g4 + 1) * 256].rearrange("(kc p) c -> p kc c", p=128))
              mM2 = P.mark()
              MS = [dict(GT=[P.tile([128, TB], F32) for _ in range(3)], ACCS=[P.tile([128, TB], F32) for _ in range(NTB)],
                         WPP=P.tile([128, 4, 128], BF16), WPM=P.tile([128, 4, 128], BF16), WPR=P.tile([128, 8, 128], BF16))
                    for _ in range(2)]
              def load_branch_w(j):
                  ms = MS[j % 2]
                  cs_ = slice(j * 128, (j + 1) * 128)
                  load_w(ms["WPP"], w_pp_d[l][:, cs_].rearrange("(kc p) c -> p kc c", p=128))
                  load_w(ms["WPM"], w_pm_d[l][:, cs_].rearrange("(kc p) c -> p kc c", p=128))
                  load_w(ms["WPR"], w_pr_d[l][:, cs_].rearrange("(kc p) c -> p kc c", p=128))

              def gate_consume(ci, chunk, tb, ps):
                  b = (chunk - 34) // 8
                  j = (chunk - 34) % 8
                  GT, ACCS, WPP, WPM, WPR = (MS[j % 2][k] for k in ("GT", "ACCS", "WPP", "WPM", "WPR"))
                  if b == 0 and tb == 0 and j + 1 < 8:
                      load_branch_w(j + 1)
                  if b == 1 and tb == 0 and 2 <= j < 6:
                      load_wo(j - 2)
                  P.act(GT[b], ps, AF.Sigmoid, bias=vcol(l, "gate_b", b * 8 + j))
                  src, w, nk = ((PO, WPP, 4), (YR, WPR, 8), (AO, WPM, 4))[b]
                  pv = P.bank(3 + b)
                  for kc in range(nk):
                      P.mm(pv, w[:, kc, :], src[:, kc, tb * TB:(tb + 1) * TB], start=(kc == 0), stop=(kc == nk - 1))
                  if b == 0:
                      P.tt("dve", ACCS[tb], pv, GT[b], ALU.mult)
                  else:
                      P.tt("dve", GT[b], pv, GT[b], ALU.mult)
                      if b == 1:
                          P.tt("dve", ACCS[tb], ACCS[tb], GT[b], ALU.add)
                      else:
                          P.tt("dve", M[:, j, tb * TB:(tb + 1) * TB], ACCS[tb], GT[b], ALU.add)

              load_branch_w(0)
              order = []
              for j in range(8):
                  order += [34 + j, 42 + j, 50 + j]
              proj(w_in_l, order, H, gate_consume)
              P.release(mM2)
              if (l, hf) == (0, 0):
                  tap("merged0", M)
              phase("merged%d%d" % (l, hf))

              mO = P.mark()
              g_post = load_gain(1 + l * 4 + 1)
              g_fpre = load_gain(1 + l * 4 + 2)
              def wo_mm(tt, hh, ps):
                  for kc in range(8):
                      P.mm(ps, M[:, kc, tt * 128:(tt + 1) * 128], WO[:, kc, hh * 512:(hh + 1) * 512],
                           start=(kc == 0), stop=(kc == 7))

              residual_stage(t_half, x_res, xm_d, wo_mm, g_post, g_fpre, H2)
              P.release(mM)
              if (l, hf) == (0, 0):
                  tap("h2_0", H2)
              phase("h2%d%d" % (l, hf))

              mF = P.mark()
              FS = [dict(UG=P.tile([128, 2 + HS], F32), UV=P.tile([128, 2 + HS], F32), CG=P.tile([128, HS], F32),
                         CV=P.tile([128, HS], F32)) for _ in range(2)]
              w_fu_l = w_fu_d[l]
              wd_off = P.mark()
              WD = P.tile([128, NFF, D], BF16)

              def load_wd(kc):
                  load_w(WD[:, kc:kc + 2, :],
                         w_fd_d[l][kc * 128:(kc + 2) * 128, :].rearrange("(kc p) c -> p kc c", p=128))

              def conv(U, c, out):
                  cw = [vcol(l, "conv_w", k * 2 * NFF + c) for k in range(3)]
                  P.act(out, U[:, 0:HS], AF.Identity, bias=vcol(l, "conv_b", c), scale=cw[0])
                  P.stt("dve", out, U[:, 1:HS + 1], cw[1], out, ALU.mult, ALU.add)
                  P.stt("dve", out, U[:, 2:HS + 2], cw[2], out, ALU.mult, ALU.add)

              def up_consume(ci, chunk, tb, ps):
                  c = chunk % NFF
                  UG, UV, CG, CV = (FS[c % 2][k] for k in ("UG", "UV", "CG", "CV"))
                  U = UG if chunk < NFF else UV
                  if tb == 0:
                      P.copy("dve", U[:, 0:2], cc[l][:, chunk, :])
                  P.copy("act", U[:, 2 + tb * TB:2 + (tb + 1) * TB], ps)
                  if tb == NTB - 1:
                      P.copy("dve", cc[l][:, chunk, :], U[:, HS:HS + 2])
                      if chunk >= NFF:
                          if c % 2 == 0:
                              load_wd(c)
                          conv(UG, c, CG)
                          conv(UV, NFF + c, CV)

                          def tail(c=c, CG=CG, CV=CV):
                              P.act(CG, CG, AF.Gelu_apprx_tanh)
                              P.tt("dve", ACTB[:, c, :], CG, CV, ALU.mult)
                          up_pending.append(tail)
                      elif up_pending:
                          up_pending.pop(0)()

              order = []
              for c in range(NFF):
                  order += [c, NFF + c]
              up_pending = []
              proj(w_fu_l, order, H2, up_consume)
              while up_pending:
                  up_pending.pop(0)()
              P.release(mF)
              if (l, hf) == (0, 0):
                  tap("act0", ACTB)
              phase("act%d%d" % (l, hf))

              P.release(off_h2)
              g_fpost = load_gain(1 + l * 4 + 3)
              g_next = load_gain(1 + (l + 1) * 4 + 0) if l + 1 < NL else None
              def fd_mm(tt, hh, ps):
                  for kc in range(NFF):
                      P.mm(ps, ACTB[:, kc, tt * 128:(tt + 1) * 128], WD[:, kc, hh * 512:(hh + 1) * 512],
                           start=(kc == 0), stop=(kc == NFF - 1))

              residual_stage(t_half, xm_d, x_lout, fd_mm, g_fpost, g_next, H)
              assert P.mark() <= wd_off, (P.mark(), wd_off)
              P.release(base)

    except _Stop:
        pass
    P.rwkv_top = rwkv_top[0]
    P.emit()
    return nc, P


_CACHE = {}


def host_tables(inp):
    cols = []
    for l in range(NL):
        rows = []
        rows.append(inp["mu_shift"][l].reshape(26, 128))
        rows.append(inp["pool_b"][l].reshape(4, 128))
        rows.append(inp["pool_scale"][l].reshape(4, 128))
        rows.append(inp["w0"][l].reshape(8, 128))
        rows.append(inp["a0"][l].reshape(8, 128))
        rows.append(inp["k_k"][l].reshape(8, 128))
        rows.append(inp["k_a"][l].reshape(8, 128))
        rows.append(inp["r_k"][l].reshape(8, 128))
        rows.append(inp["ln_x_w"][l].reshape(8, 128))
        rows.append(inp["ln_x_b"][l].reshape(8, 128))
        rows.append(inp["v0"][l - 1].reshape(8, 128) if l > 0 else np.zeros((8, 128), np.float32))
        rows.append(inp["gate_b"][l].reshape(24, 128))
        rows.append(inp["conv_w"][l].reshape(3 * 44, 128))
        rows.append(inp["conv_b"][l].reshape(44, 128))
        cols.append(np.concatenate(rows, 0))
    vt = np.ascontiguousarray(np.concatenate(cols, 0).T.astype(np.float32))
    gains = [inp["mem_norm"].reshape(1, D)]
    for l in range(NL):
        gains += [inp["norm_mix_pre"][l:l + 1], inp["norm_mix_post"][l:l + 1],
                  inp["norm_ffn_pre"][l:l + 1], inp["norm_ffn_post"][l:l + 1]]
    gains = np.ascontiguousarray(np.concatenate(gains, 0).astype(np.float32))
    return vt, gains


def make_in_maps(inp, n_cores, S):
    inp = {k: np.asarray(v) for k, v in inp.items()}
    vt, gains = host_tables(inp)
    cst = make_consts()
    shared = dict(cst=cst, vt=vt, gains=gains)
    for k in ("w_in", "pool_w", "w_proj_pool", "w_mem_kv", "w_proj_mem", "w_up_decay", "w_up_a", "w_up_g",
              "w_down_v", "w_up_v", "w_proj_rwkv", "w_o", "w_ffn_up", "w_ffn_down"):
        shared[k] = np.ascontiguousarray(inp[k], dtype=np.float32)
    maps = []
    for b in range(n_cores):
        m = dict(shared)
        m["x"] = np.ascontiguousarray(inp["x"][b, :S], dtype=np.float32)
        m["mem"] = np.ascontiguousarray(inp["mem"][b], dtype=np.float32)
        maps.append(m)
    return maps


def kernel(**inputs):
    S = 2048
    if "nc" not in _CACHE:
        _CACHE["nc"] = build(S)
    nc, P = _CACHE["nc"]
    maps = make_in_maps(inputs, 8, S)
    res = run_bass_kernel_spmd(nc, maps, core_ids=list(range(8)))
    out = np.stack([np.asarray(r["out"], dtype=np.float32) for r in res.results], 0)
    return out
```

```python
import contextlib
import numpy as np
import concourse.bass as bass
import concourse.mybir as mybir
from concourse.bass_utils import run_bass_kernel_spmd

F32 = mybir.dt.float32
BF16 = mybir.dt.bfloat16
U8 = mybir.dt.uint8
AF = mybir.ActivationFunctionType
ALU = mybir.AluOpType

EPOCH = 8192
NDMA = {"sp": 8, "act": 4, "pool": 8}
BUCKET = 1024
ENGMAP = {"pe": "tensor", "act": "scalar", "dve": "vector", "pool": "gpsimd", "sp": "sync"}


def _esize(dt):
    return mybir.dt.size(dt)


class Prog:
    def __init__(self, nc, sbuf_bytes=212736):
        self.nc = nc
        self.ops = []
        self.stack = contextlib.ExitStack()
        self.sems = {}
        self.dma_cnt = {q: 0 for q in NDMA}
        self.pstride = {}
        self.track = {}
        self.bank_last = {}
        self.arena = self.stack.enter_context(nc.sbuf_tensor("arena", [128, sbuf_bytes], U8))
        self.pstride["arena"] = sbuf_bytes
        self.sbuf_bytes = sbuf_bytes
        self.top = 0
        self.peak = 0
        self.psum = []
        for i in range(8):
            t = self.stack.enter_context(nc.psum_tensor("pb%d" % i, [128, 512], F32))
            self.pstride["pb%d" % i] = 2048
            self.psum.append(t)

    def mark(self):
        return self.top

    def release(self, m):
        self.top = m

    def tile(self, shape, dt):
        nb = int(np.prod(shape[1:])) * _esize(dt)
        off = self.top
        self.top += (nb + 63) // 64 * 64
        self.peak = max(self.peak, self.top)
        assert self.top <= self.sbuf_bytes, ("SBUF overflow", self.top)
        v = self.arena[:, off:off + nb].bitcast(dt)
        if len(shape) > 2:
            names = " ".join("d%d" % i for i in range(1, len(shape)))
            kw = {"d%d" % i: shape[i] for i in range(1, len(shape) - 1)}
            v = v.rearrange("p (%s) -> p %s" % (names, names), **kw)
        if shape[0] < 128:
            v = v[0:shape[0]]
        return v

    def bank(self, i, dt=F32):
        t = self.psum[i]
        return t[:, :] if dt == F32 else t[:, :].bitcast(dt)

    def _rect(self, a):
        name = a.tensor.name
        es = _esize(a.dtype)
        dims = a.ap
        boff = a.offset * es
        if name in self.pstride:
            ps = self.pstride[name]
            p0, f0 = divmod(boff, ps)
            if dims[0][0] * es == ps:
                pn = dims[0][1]
                rest = dims[1:]
            elif dims[0][0] == 0:
                pn = 1
                rest = dims[1:]
            else:
                pn = 1
                rest = dims
            ext = sum(s * (c - 1) for s, c in rest) + 1
            return (name, p0, p0 + pn, f0, f0 + ext * es)
        ext = sum(abs(s) * (c - 1) for s, c in dims) + 1
        return ("dram:" + name, 0, 1, boff, boff + ext * es)

    @staticmethod
    def _ovl(a, b):
        return a[1] < b[2] and b[1] < a[2] and a[3] < b[4] and b[3] < a[4]

    @staticmethod
    def _cov(a, b):
        return a[1] <= b[1] and a[2] >= b[2] and a[3] <= b[3] and a[4] >= b[4]

    def _buckets(self, r):
        bs = BUCKET if not r[0].startswith("dram:") else (1 << 20)
        return range(r[3] // bs, (r[4] - 1) // bs + 1)

    def _access(self, r, is_write, me, engkey, deps):
        t = self.track.setdefault(r[0], {})
        for b in self._buckets(r):
            d = t.setdefault(b, {"w": [], "r": {}})
            for (op, rr) in d["w"]:
                if self._ovl(r, rr):
                    deps.add(op)
            if is_write:
                for (ek, rr), op in d["r"].items():
                    if self._ovl(r, rr):
                        deps.add(op)
                d["w"] = [(op, rr) for (op, rr) in d["w"] if not self._cov(r, rr)]
                d["r"] = {k: op for k, op in d["r"].items() if not self._cov(r, k[1])}
                d["w"].append((me, r))
            else:
                d["r"][(engkey, r)] = me

    def op(self, eng, fn, outs=(), ins=(), dma=False):
        i = len(self.ops)
        deps = set()
        engkey = ("dma", i) if dma else eng
        for a in ins:
            if a is None or isinstance(a, (int, float)):
                continue
            self._access(self._rect(a), False, i, engkey, deps)
        for a in outs:
            if a is None:
                continue
            self._access(self._rect(a), True, i, engkey, deps)
        deps.discard(i)
        for a in list(ins) + list(outs):
            if a is None or isinstance(a, (int, float)):
                continue
            r = self._rect(a)
            if not r[0].startswith("pb"):
                continue
            for q in range(r[1] // 32, (r[2] - 1) // 32 + 1):
                d = self.bank_last.setdefault((r[0], q), {})
                for e2, v in d.items():
                    if e2 != eng:
                        deps.add(v)
                d[eng] = i
        deps.discard(i)
        o = dict(eng=eng, fn=fn, deps=deps, dma=dma)
        if dma:
            n = self.dma_cnt[eng]
            self.dma_cnt[eng] += 1
            o["dsem"] = ("d" + eng, n % NDMA[eng])
            o["dval"] = 16 * (n // NDMA[eng] + 1)
        self.ops.append(o)
        return i

    def mm(self, out, lhsT, rhs, start=True, stop=True):
        return self.op("pe", lambda e: e.matmul(out, lhsT=lhsT, rhs=rhs, start=start, stop=stop), [out], [lhsT, rhs])

    def transpose(self, out, in_, ident):
        return self.op("pe", lambda e: e.transpose(out, in_, ident), [out], [in_, ident])

    def act(self, out, in_, func, bias=None, scale=None, accum_out=None):
        kw = {}
        if bias is not None:
            kw["bias"] = bias
        if scale is not None:
            kw["scale"] = scale
        if accum_out is not None:
            kw["accum_out"] = accum_out
        ins = [in_] + [x for x in (bias, scale) if x is not None and not isinstance(x, (int, float))]
        return self.op("act", lambda e: e.activation(out, in_, func, **kw), [out, accum_out], ins)

    def tt(self, eng, out, in0, in1, op):
        return self.op(eng, lambda e: e.tensor_tensor(out, in0, in1, op), [out], [in0, in1])

    def ts(self, eng, out, in0, s1, s2, op0, op1=None):
        ins = [in0] + [x for x in (s1, s2) if x is not None and not isinstance(x, (int, float))]
        if op1 is None:
            return self.op(eng, lambda e: e.tensor_scalar(out, in0, s1, None, op0), [out], ins)
        return self.op(eng, lambda e: e.tensor_scalar(out, in0, s1, s2, op0, op1), [out], ins)

    def stt(self, eng, out, in0, scalar, in1, op0, op1):
        ins = [in0, in1] + ([scalar] if not isinstance(scalar, (int, float)) else [])
        return self.op(eng, lambda e: e.scalar_tensor_tensor(out, in0, scalar, in1, op0, op1), [out], ins)

    def copy(self, eng, out, in_):
        if eng == "act":
            return self.op("act", lambda e: e.copy(out, in_), [out], [in_])
        return self.op(eng, lambda e: e.tensor_copy(out, in_), [out], [in_])

    def memset(self, eng, ap, val):
        return self.op(eng, lambda e: e.memset(ap, val), [ap], [])

    def recip(self, out, in_):
        return self.op("dve", lambda e: e.reciprocal(out, in_), [out], [in_])

    def scan(self, out, d0, d1, init, op0, op1):
        return self.op("dve", lambda e: e.tensor_tensor_scan(out, d0, d1, init, op0, op1), [out], [d0, d1])

    def dma(self, q, out, in_):
        return self.op(q, lambda e: e.dma_start(out=out, in_=in_), [out], [in_], dma=True)

    def _token(self, o):
        if o["dma"]:
            return o["dsem"], o["dval"]
        c = o["cseq"]
        return (o["eng"], c // EPOCH), c % EPOCH + 1

    def sem(self, key):
        if key not in self.sems:
            self.sems[key] = self.stack.enter_context(self.nc.semaphore("s_%s_%s" % key))
        return self.sems[key]

    def emit(self):
        nc = self.nc
        cnt = {e: 0 for e in ENGMAP}
        for o in self.ops:
            if not o["dma"]:
                o["cseq"] = cnt[o["eng"]]
                cnt[o["eng"]] += 1
        for o in self.ops:
            self.sem(self._token(o)[0])
        self.nwaits = 0
        with nc.Block() as block:
            for e, bname in ENGMAP.items():
                mine = [o for o in self.ops if o["eng"] == e]
                if not mine:
                    continue

                def body(eng, e=e, mine=mine):
                    known = {}
                    for o in mine:
                        need = {}
                        for di in o["deps"]:
                            d = self.ops[di]
                            if d["eng"] == e and e == "pe" and not d["dma"]:
                                continue
                            sk, v = self._token(d)
                            if v > need.get(sk, 0):
                                need[sk] = v
                        if o["dma"]:
                            sk, v = o["dsem"], o["dval"] - 16
                            if v > 0 and v > need.get(sk, 0):
                                need[sk] = v
                        for sk in sorted(need):
                            v = need[sk]
                            if known.get(sk, 0) >= v:
                                continue
                            if not sk[0].startswith("d") and any(
                                k2[0] == sk[0] and k2[1] > sk[1] for k2 in known
                            ):
                                continue
                            eng.wait_ge(self.sems[sk], v)
                            self.nwaits += 1
                            known[sk] = v
                        ins = o["fn"](eng)
                        sk, v = self._token(o)
                        ins.then_inc(self.sems[sk], 16 if o["dma"] else 1)
                    if e in NDMA:
                        last = {}
                        for o in mine:
                            if o["dma"]:
                                last[o["dsem"]] = o["dval"]
                        for sk in sorted(last):
                            if known.get(sk, 0) < last[sk]:
                                eng.wait_ge(self.sems[sk], last[sk])

                getattr(block, bname)(body)


D = 1024
KC = 8
NL = 2
IN_COLS = 7424
D_FF = 2816
NFF = D_FF // 128
HS = 1024
TB = 512
NTB = HS // TB
CH = 64
NCH = TB // CH
C0 = float(np.exp(-0.5))
RATIO = 1
S1_EVERY = 2

VT_LAYOUT = [("mu", 26), ("pool_b", 4), ("pool_scale", 4), ("w0", 8), ("a0", 8), ("k_k", 8), ("k_a", 8),
             ("r_k", 8), ("ln_w", 8), ("ln_b", 8), ("v0", 8), ("gate_b", 24), ("conv_w", 132), ("conv_b", 44)]
VT_OFF = {}
_o = 0
for _n, _c in VT_LAYOUT:
    VT_OFF[_n] = _o
    _o += _c
NVL = _o

CST_IDENT = 0
CST_MASKC = 128
CST_CMASK = CST_MASKC + 512
CST_INVCNT = CST_CMASK + 512
CST_BDONES = CST_INVCNT + 16
CST_BDNEG = CST_BDONES + 128
NCST = CST_BDNEG + 128


def make_consts():
    c = np.zeros((128, NCST), np.float32)
    c[:, CST_IDENT:CST_IDENT + 128] = np.eye(128, dtype=np.float32)
    p = np.arange(128)
    hp_, tp = p // 64, p % 64
    q = np.arange(128)
    hq, tq = q // 64, q % 64
    same = (hp_[:, None] == hq[None, :])
    lower = same & (tq[None, :] < tp[:, None])
    upper = same & (tq[None, :] > tp[:, None])
    c[:, CST_MASKC:CST_MASKC + 128] = lower
    c[:, CST_MASKC + 128:CST_MASKC + 256] = upper
    c[:, CST_MASKC + 256:CST_MASKC + 384] = upper
    t64 = np.arange(64)
    incl = (tp[:, None] <= t64[None, :])
    c[:, CST_MASKC + 384:CST_MASKC + 448] = incl
    c[:, CST_MASKC + 448:CST_MASKC + 512] = incl
    cm = np.ones(512, np.float32)
    cm[::64] = 0.0
    c[:, CST_CMASK:CST_CMASK + 512] = cm[None, :]
    c[:, CST_INVCNT:CST_INVCNT + 16] = (1.0 / (np.arange(16) + 1.0))[None, :]
    c[:, CST_BDONES:CST_BDONES + 128] = same
    c[:, CST_BDNEG:CST_BDNEG + 128] = -same.astype(np.float32)
    return c


class _Stop(Exception):
    pass


def build(S=2048, taps=(), stop=None):
    NHF = S // HS
    nc = bass.Bass("TRN2", target_bir_lowering=False)
    P = Prog(nc)

    def din(name, shape):
        return nc.dram_tensor(name, list(shape), F32, kind="ExternalInput").ap()

    x_d = din("x", [S, D])
    mem_d = din("mem", [256, D])
    cst_d = din("cst", [128, NCST])
    vt_d = din("vt", [128, NL * NVL])
    gains_d = din("gains", [9, D])
    w_in_d = din("w_in", [NL, D, IN_COLS])
    pool_w_d = din("pool_w", [NL, 4, 128, 128])
    w_pp_d = din("w_proj_pool", [NL, 512, D])
    w_kv_d = din("w_mem_kv", [NL, D, 1024])
    w_pm_d = din("w_proj_mem", [NL, 512, D])
    w_upd_d = din("w_up_decay", [NL, 64, D])
    w_upa_d = din("w_up_a", [NL, 64, D])
    w_upg_d = din("w_up_g", [NL, 128, D])
    w_dv_d = din("w_down_v", [1, D, 32])
    w_uv_d = din("w_up_v", [1, 32, D])
    w_pr_d = din("w_proj_rwkv", [NL, D, D])
    w_o_d = din("w_o", [NL, D, D])
    w_fu_d = din("w_ffn_up", [NL, D, 2 * D_FF])
    w_fd_d = din("w_ffn_down", [NL, D_FF, D])
    out_d = nc.dram_tensor("out", [S, D], F32, kind="ExternalOutput").ap()
    xm_d = nc.dram_tensor("xm_scr", [S, D], F32, kind="Internal").ap()
    xl_d = nc.dram_tensor("xl_scr", [S, D], F32, kind="Internal").ap()
    vf_d = nc.dram_tensor("vf_scr", [8, 128, S], F32, kind="Internal").ap()
    v1_d = nc.dram_tensor("v1_scr", [8, 128, S], F32, kind="Internal").ap()
    tap_d = {}
    for name, shape in taps:
        tap_d[name] = nc.dram_tensor("tap_" + name, list(shape), F32, kind="ExternalOutput").ap()

    cst = P.tile([128, NCST], F32)
    VT = P.tile([128, NL * NVL], F32)
    OMU = P.tile([128, NL * 26], F32)
    OMKA = P.tile([128, NL * 8], F32)
    EPS = P.tile([128, 4], F32)
    ident_b = P.tile([128, 128], BF16)
    bdones_b = P.tile([128, 128], BF16)
    ones_b = P.tile([128, 128], BF16)
    memT = P.tile([128, 8, 256], BF16)
    kT = [P.tile([128, 4, 256], BF16) for _ in range(NL)]
    Vt = [P.tile([128, 2, 512], BF16) for _ in range(NL)]
    Hst = [P.tile([128, 8, 128], F32) for _ in range(NL)]
    zc = [P.tile([128, 26], F32) for _ in range(NL)]
    pc = [P.tile([128, 4, 16], F32) for _ in range(NL)]
    cc = [P.tile([128, 2 * NFF, 2], F32) for _ in range(NL)]
    H = P.tile([128, 8, HS], BF16)
    WB = [P.tile([128, 8, 256], BF16) for _ in range(4)]
    wb_i = [0]
    big0 = P.mark()
    YR = P.tile([128, 8, HS], BF16)
    off_po = P.mark()
    PO = P.tile([128, 4, HS], BF16)
    AO = P.tile([128, 4, HS], BF16)
    M = P.tile([128, 8, HS], BF16)
    big1 = P.mark()
    P.release(big0)
    ACTB = P.tile([128, NFF, HS], BF16)
    assert P.mark() <= big1
    P.release(big1)
    off_h2 = P.mark()
    H2 = P.tile([128, 8, HS], BF16)
    base = P.mark()

    ident_f = cst[:, CST_IDENT:CST_IDENT + 128]
    maskc = cst[:, CST_MASKC:CST_MASKC + 512]
    cmask = cst[:, CST_CMASK:CST_CMASK + 512]
    invcnt = cst[:, CST_INVCNT:CST_INVCNT + 16]

    def vcol(l, name, j=0):
        o = l * NVL + VT_OFF[name] + j
        return VT[:, o:o + 1]

    P.dma("sp", cst, cst_d)
    P.dma("sp", VT, vt_d)
    P.copy("dve", ident_b, ident_f)
    P.copy("dve", bdones_b, cst[:, CST_BDONES:CST_BDONES + 128])
    P.memset("pool", ones_b, 1.0)
    P.memset("pool", EPS[:, 0:1], 1e-6)
    P.memset("pool", EPS[:, 1:2], 64e-5)
    P.memset("pool", EPS[:, 2:3], 1e-12)
    for l in range(NL):
        P.memset("pool", Hst[l], 0.0)
        P.memset("pool", zc[l], 0.0)
        P.memset("pool", pc[l], 0.0)
        P.memset("pool", cc[l], 0.0)
        o = l * NVL + VT_OFF["mu"]
        P.ts("dve", OMU[:, l * 26:(l + 1) * 26], VT[:, o:o + 26], -1.0, 1.0, ALU.mult, ALU.add)
        o = l * NVL + VT_OFF["k_a"]
        P.ts("dve", OMKA[:, l * 8:(l + 1) * 8], VT[:, o:o + 8], -1.0, 1.0, ALU.mult, ALU.add)

    def load_gain(idx):
        g = P.tile([128, D], F32)
        P.dma("sp", g, gains_d[idx:idx + 1, :].to_broadcast([128, D]))
        return g

    def tap(name, ap_sb):
        if name in tap_d:
            m_ = P.mark()
            tmp = P.tile([128, HS], F32)
            for kc in range(ap_sb.shape[1]):
                P.copy("dve", tmp, ap_sb[:, kc, :])
                P.dma("sp", tap_d[name][:, kc, :], tmp)
            P.release(m_)

    def tm_norm(src, gain, hn, junk, st):
        P.act(junk, src, AF.Square, accum_out=st[:, 0:1])
        P.act(st[:, 1:2], st[:, 0:1], AF.Sqrt, bias=EPS[:, 0:1], scale=1.0 / D)
        P.recip(st[:, 2:3], st[:, 1:2])
        P.stt("dve", hn, src, st[:, 2:3], gain, ALU.mult, ALU.mult)

    def to_fm(hn, dst, t0, bank=7):
        pt = P.bank(bank, BF16)
        for kc in range(8):
            P.transpose(pt[:, kc * 128:(kc + 1) * 128], hn[:, kc * 128:(kc + 1) * 128], ident_b)
        P.copy("act", dst[:, :, t0:t0 + 128], pt[:, 0:1024].rearrange("p (k t) -> p k t", k=8))

    m0 = P.mark()
    g_mem = load_gain(0)
    mt_x = P.tile([128, D], F32)
    mt_h = P.tile([128, D], BF16)
    mt_j = P.tile([128, D], BF16)
    mt_s = P.tile([128, 4], F32)
    for mt in range(2):
        P.dma("sp", mt_x, mem_d[mt * 128:(mt + 1) * 128, :])
        tm_norm(mt_x, g_mem, mt_h, mt_j, mt_s)
        to_fm(mt_h, memT, mt * 128)
    P.release(m0)

    proj_bank = [0]
    rwkv_top = [0]

    def load_w(dst, src):
        P.dma("pool", dst, src)

    def proj(w2d, chunks, rhs, consume, kchunks=8, pf=2):
        groups = []
        i = 0
        while i < len(chunks):
            n = 2 if (i + 1 < len(chunks) and chunks[i + 1] == chunks[i] + 1) else 1
            groups.append((chunks[i], n))
            i += n
        loaded = {}

        def issue(gi):
            c0, n = groups[gi]
            wb = WB[wb_i[0] % 4]
            wb_i[0] += 1
            load_w(wb[:, 0:kchunks, 0:n * 128],
                   w2d[:, c0 * 128:(c0 + n) * 128].rearrange("(kc p) c -> p kc c", p=128))
            loaded[gi] = wb

        for gi in range(min(pf, len(groups))):
            issue(gi)
        ci = 0
        for gi, (c0, n) in enumerate(groups):
            if gi + pf < len(groups):
                issue(gi + pf)
            wb = loaded.pop(gi)
            for j in range(n):
                for tb in range(NTB):
                    ps = P.bank(proj_bank[0] % 3)
                    proj_bank[0] += 1
                    for kc in range(kchunks):
                        P.mm(ps, wb[:, kc, j * 128:(j + 1) * 128], rhs[:, kc, tb * TB:(tb + 1) * TB],
                             start=(kc == 0), stop=(kc == kchunks - 1))
                    consume(ci, c0 + j, tb, ps)
                ci += 1

    def residual_stage(t_half, src_x, dst_x, mm_fn, g_post, g_pre, Hdst):
        n = HS // 128
        xt = [P.tile([128, D], F32) for _ in range(3)]
        yt = [P.tile([128, D], F32) for _ in range(3)]
        hn = [P.tile([128, D], BF16) for _ in range(3)]
        jkA = P.tile([128, D], BF16)
        jkC = P.tile([128, D], BF16)
        st = [P.tile([128, 8], F32) for _ in range(3)]

        def stA(tt):
            b = tt % 3
            r0 = t_half + tt * 128
            P.dma("sp", xt[b], src_x[r0:r0 + 128, :])
            for hh in range(2):
                ps = P.bank((tt % 2) * 2 + hh)
                mm_fn(tt, hh, ps)
                P.copy("act", yt[b][:, hh * 512:(hh + 1) * 512], ps)

        def stB(tt):
            b = tt % 3
            r0 = t_half + tt * 128
            P.act(jkA, yt[b], AF.Square, accum_out=st[b][:, 0:1])
            P.act(st[b][:, 1:2], st[b][:, 0:1], AF.Sqrt, bias=EPS[:, 0:1], scale=1.0 / D)
            P.recip(st[b][:, 2:3], st[b][:, 1:2])
            P.stt("dve", yt[b], yt[b], st[b][:, 2:3], g_post, ALU.mult, ALU.mult)
            P.tt("dve", xt[b], xt[b], yt[b], ALU.add)
            P.dma("sp", dst_x[r0:r0 + 128, :], xt[b])

        def stC1(tt):
            b = tt % 3
            if g_pre is not None:
                tm_norm(xt[b], g_pre, hn[b], jkC, st[b][:, 4:8])

        def stC2(tt):
            b = tt % 3
            if g_pre is not None:
                to_fm(hn[b], Hdst, tt * 128)

        for step in range(n + 3):
            if 0 <= step - 3 < n:
                stC2(step - 3)
            if 0 <= step - 2 < n:
                stC1(step - 2)
            if 0 <= step - 1 < n:
                stB(step - 1)
            if step < n:
                stA(step)

    def phase(name):
        if stop == name:
            raise _Stop()

    try:
      phase('init')
      for hf in range(NHF):
          t_half = hf * HS
          for l in range(NL):
              w_in_l = w_in_d[l]
              if l == 0:
                  m0 = P.mark()
                  g_pre = load_gain(1)
                  xt = [P.tile([128, D], F32) for _ in range(2)]
                  hn = [P.tile([128, D], BF16) for _ in range(2)]
                  jk = P.tile([128, D], BF16)
                  st = [P.tile([128, 4], F32) for _ in range(2)]
                  for tt in range(HS // 128 + 1):
                      b = tt % 2
                      if tt < HS // 128:
                          P.dma("sp", xt[b], x_d[t_half + tt * 128:t_half + (tt + 1) * 128, :])
                          tm_norm(xt[b], g_pre, hn[b], jk, st[b])
                      if tt >= 1:
                          to_fm(hn[(tt - 1) % 2], H, (tt - 1) * 128)
                  P.release(m0)
              if (l, hf) == (0, 0):
                  tap("h0", H)
              phase("h%d%d" % (l, hf))
              x_res = x_d if l == 0 else xl_d
              x_lout = xl_d if l == 0 else out_d

              if hf == 0:
                  m0 = P.mark()
                  for g4 in range(4):
                      wb = WB[wb_i[0] % 4]
                      wb_i[0] += 1
                      load_w(wb, w_kv_d[l][:, g4 * 256:(g4 + 1) * 256].rearrange("(kc p) c -> p kc c", p=128))
                      if g4 < 2:
                          for j in range(2):
                              hh = g4 * 2 + j
                              ps = P.bank(3)
                              for kc in range(8):
                                  P.mm(ps[:, 0:256], wb[:, kc, j * 128:(j + 1) * 128], memT[:, kc, :],
                                       start=(kc == 0), stop=(kc == 7))
                              P.copy("act", kT[l][:, hh, :], ps[:, 0:256])
                      else:
                          for mt in range(2):
                              ps = P.bank(3)
                              for kc in range(8):
                                  P.mm(ps[:, 0:256], memT[:, kc, mt * 128:(mt + 1) * 128], wb[:, kc, :],
                                       start=(kc == 0), stop=(kc == 7))
                              P.copy("act", Vt[l][:, mt, (g4 - 2) * 256:(g4 - 1) * 256], ps[:, 0:256])
                  P.release(m0)

              phase('kv%d%d' % (l, hf))
              P.release(off_po)
              WDA = P.tile([128, D], BF16)
              WG = P.tile([128, D], BF16)
              LA = P.tile([128, HS], BF16)
              SDG = P.tile([128, HS], BF16)
              load_w(WDA[0:64, :], w_upd_d[l])
              load_w(WDA[64:128, :], w_upa_d[l])
              load_w(WG, w_upg_d[l])
              Zt = [P.tile([128, HS + 1], F32) for _ in range(3)]
              ZR = [z_[:, 0:HS] for z_ in Zt]

              def shift_consume(j, Z):
                  mu_c = VT[:, l * NVL + VT_OFF["mu"] + j:l * NVL + VT_OFF["mu"] + j + 1]
                  omu_c = OMU[:, l * 26 + j:l * 26 + j + 1]

                  def consume(ci, chunk, tb, ps):
                      if tb == 0:
                          P.copy("dve", Z[:, 0:1], zc[l][:, j:j + 1])
                      P.act(Z[:, 1 + tb * TB:1 + (tb + 1) * TB], ps, AF.Copy, scale=mu_c)
                      P.stt("dve", Z[:, tb * TB:(tb + 1) * TB], ps, omu_c, Z[:, tb * TB:(tb + 1) * TB],
                            ALU.mult, ALU.add)
                      if tb == NTB - 1:
                          P.copy("dve", zc[l][:, j:j + 1], Z[:, HS:HS + 1])
                  return consume

              _lc = {32: shift_consume(24, Zt[0]), 33: shift_consume(25, Zt[1])}
              proj(w_in_l, [32, 33], H, lambda ci, chunk, tb, ps: _lc[chunk](ci, chunk, tb, ps))
              P.act(LA[0:64, :], ZR[0][0:64, :], AF.Tanh)
              P.copy("pool", LA[64:128, :], ZR[0][64:128, :])
              P.act(SDG, ZR[1], AF.Sigmoid)
              phase('lora%d%d' % (l, hf))

              if l == 1:
                  WDV = P.tile([128, 8, 32], BF16)
                  WUV = P.tile([32, D], BF16)
                  VDB = P.tile([32, HS], BF16)
                  vb = P.tile([128, HS], BF16)
                  VFT = P.tile([128, HS], F32)
                  VG = P.tile([128, TB], F32)
                  load_w(WDV, w_dv_d[0].rearrange("(hp p) r -> p hp r", p=128))
                  load_w(WUV, w_uv_d[0])
                  vb2 = P.tile([128, HS], BF16)
                  vbs = [vb, vb2]
                  _vc = [shift_consume(16 + hp, Zt[hp % 3]) for hp in range(8)]

                  def vpass_consume(ci, chunk, tb, ps):
                      hp = chunk - 24
                      _vc[hp](ci, chunk, tb, ps)
                      if tb == NTB - 1:
                          zr_ = ZR[hp % 3]
                          vb_ = vbs[hp % 2]
                          P.dma("sp", v1_d[hp][:, t_half:t_half + HS], zr_)
                          P.copy("act", vb_, zr_)
                          for tb2 in range(NTB):
                              P.mm(P.bank(3 + tb2)[0:32, :], WDV[:, hp, :], vb_[:, tb2 * TB:(tb2 + 1) * TB],
                                   start=(hp == 0), stop=(hp == 7))

                  proj(w_in_l, [24 + hp for hp in range(8)], H, vpass_consume)
                  for tb in range(NTB):
                      P.copy("act", VDB[:, tb * TB:(tb + 1) * TB], P.bank(3 + tb)[0:32, :])

              SL = [P.tile([128, TB], F32) for _ in range(7)]
              SG, AA, LP, TMP, E2, E3, KK = SL
              KA, BT32, KT32 = [P.tile([128, TB], BF16) for _ in range(3)]
              MKB = P.tile([128, 2, 128], BF16)
              P.copy("dve", MKB[:, 0, :], cst[:, CST_BDONES:CST_BDONES + 128])
              P.copy("dve", MKB[:, 1, :], cst[:, CST_BDNEG:CST_BDNEG + 128])
              RS_a, TT_, RKa, KEFF = [P.tile([128, TB], F32) for _ in range(4)]
              KKR, BN = TMP, TMP
              SQ = P.tile([128, TB], BF16)
              RK = P.tile([128, TB], BF16)
              BBD, KBD, BHBD, KHBD, VBD = [P.tile([128, NCH, 128], BF16) for _ in range(5)]
              AMCp = P.tile([128, NCH, 256], BF16)
              PP = [P.tile([128, NCH, 2, 128], BF16) for _ in range(2)]
              TTs = P.tile([128, NCH, 128], BF16)
              Xb = P.tile([128, 128], BF16)
              Ub = P.tile([128, 128], BF16)
              Hb = P.tile([128, 128], BF16)
              YC = P.tile([128, TB], F32)
              RSq = P.tile([128, TB], F32)
              YB = P.tile([128, TB], BF16)
              SQ2 = P.tile([128, TB], BF16)
              QS = []
              for _ in range(2):
                  qs = dict(ABD=P.tile([128, NCH, 128], BF16), VTM=P.tile([128, NCH, 128], BF16),
                            BHTM=P.tile([128, NCH, 128], BF16), KHTM=P.tile([128, NCH, 128], BF16),
                            AMCq=P.tile([128, NCH, 256], BF16), TT=P.tile([128, NCH, 128], BF16))
                  QS.append(qs)
              S3 = [dict(RT=P.tile([128, TB], BF16), BON=P.tile([128, TB], BF16), Dq=P.tile([128, NCH, 1], F32))
                    for _ in range(3)]
              ZRr, ZRk, ZRv = ZR
              blocks = [(hp, tb) for hp in range(8) for tb in range(NTB)]
              blkmask = cst[:, CST_BDONES:CST_BDONES + 128]
              m4 = MKB[:, 0, :].rearrange("p (h t) -> p h t", h=2).unsqueeze(1).to_broadcast([128, NCH, 2, CH])
              m4f = blkmask.rearrange("p (h t) -> p h t", h=2).unsqueeze(1).to_broadcast([128, NCH, 2, CH])

              negmask = cst[:, CST_BDNEG:CST_BDNEG + 128]
              m4n = MKB[:, 1, :].rearrange("p (h t) -> p h t", h=2).unsqueeze(1).to_broadcast([128, NCH, 2, CH])

              def bdw(eng, dst, src, neg=False):
                  x4 = src.rearrange("p (c t) -> p c t", t=CH).unsqueeze(2).to_broadcast([128, NCH, 2, CH])
                  mk = m4n if neg else (m4 if src.dtype == BF16 else m4f)
                  P.tt(eng, dst.rearrange("p c (h t) -> p c h t", h=2), x4, mk, ALU.mult)

              def sbank():
                  ps = P.bank(proj_bank[0] % 3)
                  proj_bank[0] += 1
                  return ps

              def genS1(bi):
                  hp, tb = blocks[bi]
                  RT, BONq, Dq = (S3[bi % 3][k] for k in ("RT", "BON", "Dq"))
                  if tb == 0:
                      _rc = {8 + hp: shift_consume(hp, Zt[0]), 16 + hp: shift_consume(8 + hp, Zt[1]),
                             24 + hp: shift_consume(16 + hp, Zt[2])}
                      proj(w_in_l, [8 + hp, 16 + hp] + ([24 + hp] if l == 0 else []), H,
                           lambda ci, chunk, tb_, ps: _rc[chunk](ci, chunk, tb_, ps), pf=3)
                      yield
                      if l == 0:
                          P.dma("sp", vf_d[hp][:, t_half:t_half + HS], ZRv)
                      else:
                          P.dma("sp", ZRv, v1_d[hp][:, t_half:t_half + HS])
                          P.dma("sp", VFT, vf_d[hp][:, t_half:t_half + HS])
                          P.tt("pool", VFT, VFT, ZRv, ALU.subtract)
                          for tb2 in range(NTB):
                              sl2 = slice(tb2 * TB, (tb2 + 1) * TB)
                              ps = sbank()
                              P.mm(ps, WUV[:, hp * 128:(hp + 1) * 128], VDB[:, sl2])
                              P.act(VG, ps, AF.Sigmoid, bias=vcol(l, "v0", hp))
                              P.tt("dve", VFT[:, sl2], VFT[:, sl2], VG, ALU.mult)
                          P.tt("pool", ZRv, ZRv, VFT, ALU.add)
                      yield
                  hs_ = slice(hp * 128, (hp + 1) * 128)
                  sl = slice(tb * TB, (tb + 1) * TB)
                  zr, zk, zv = ZRr[:, sl], ZRk[:, sl], ZRv[:, sl]
                  KK3 = KK.rearrange("p (c t) -> p c t", t=CH)
                  E33 = E3.rearrange("p (c t) -> p c t", t=CH)
                  KA3 = KA.rearrange("p (c t) -> p c t", t=CH)
                  Dv = E33[:, :, CH - 1:CH]
                  ps_w = sbank()
                  P.mm(ps_w, WDA[0:64, hs_], LA[0:64, sl])
                  ps_a = sbank()
                  P.mm(ps_a, WDA[64:128, hs_], LA[64:128, sl])
                  P.act(KKR, zk, AF.Copy, scale=vcol(l, "k_k", hp))
                  P.act(RKa, zr, AF.Copy, scale=vcol(l, "r_k", hp))
                  yield
                  P.act(SG, ps_w, AF.Sigmoid, bias=vcol(l, "w0", hp))
                  P.act(AA, ps_a, AF.Sigmoid, bias=vcol(l, "a0", hp))
                  P.act(SQ, KKR, AF.Square)
                  yield
                  P.scan(LP, cmask, SG, 0.0, ALU.mult, ALU.add)
                  ps_s = sbank()
                  P.mm(ps_s, bdones_b, SQ)
                  P.act(TT_, AA, AF.Identity, bias=OMKA[:, l * 8 + hp:l * 8 + hp + 1], scale=vcol(l, "k_a", hp))
                  yield
                  P.act(RS_a, ps_s, AF.Ln, bias=EPS[:, 2:3])
                  P.act(E2, LP, AF.Exp, scale=C0)
                  P.act(E3, LP, AF.Exp, scale=-C0)
                  yield
                  P.act(RS_a, RS_a, AF.Exp, scale=-0.5)
                  P.tt("pool", KEFF, zk, TT_, ALU.mult)
                  yield
                  P.tt("pool", KK, KKR, RS_a, ALU.mult)
                  P.tt("pool", KT32, KEFF, E2, ALU.mult)
                  P.tt("pool", RK, RKa, KEFF, ALU.mult)
                  P.tt("pool", RT, zr, E3, ALU.mult)
                  P.copy("pool", Dq, Dv)
                  yield
                  P.tt("pool", BN, KK, AA, ALU.mult)
                  ps_b = sbank()
                  P.mm(ps_b, bdones_b, RK)
                  P.tt("pool", KA3[:, :, 1:CH], KK3[:, :, 1:CH], E33[:, :, 0:CH - 1], ALU.mult)
                  P.copy("pool", KA3[:, :, 0:1], KK3[:, :, 0:1])
                  yield
                  P.tt("pool", BT32, BN, E2, ALU.mult)
                  P.tt("dve", BONq, ps_b, zv, ALU.mult)
                  yield

              def stageP(bi):
                  hp, tb = blocks[bi]
                  qs = QS[bi % 2]
                  ABD, VTM, BHTM, KHTM, AMCq, TTq = (qs[k] for k in ("ABD", "VTM", "BHTM", "KHTM", "AMCq", "TT"))
                  RT = S3[bi % 3]["RT"]
                  sl = slice(tb * TB, (tb + 1) * TB)
                  zv = ZRv[:, sl]
                  Dv = E3.rearrange("p (c t) -> p c t", t=CH)[:, :, CH - 1:CH]
                  bdw("dve", ABD, KA, neg=True)
                  bdw("dve", BBD, BT32)
                  bdw("dve", KBD, KT32)
                  bdw("pool", VBD, zv)
                  yield
                  Db = Dv.to_broadcast([128, NCH, 128])
                  P.tt("dve", BHBD, BBD, Db, ALU.mult)
                  P.tt("pool", KHBD, KBD, Db, ALU.mult)
                  yield
                  for src, dst, e_ in ((BHBD, BHTM, "act"), (KHBD, KHTM, "dve"), (VBD, VTM, "act")):
                      pt = P.bank(3, BF16)
                      for c in range(NCH):
                          P.transpose(pt[:, c * 128:(c + 1) * 128], src[:, c, :], ident_b)
                      P.copy(e_, dst, pt[:, 0:1024].rearrange("p (c t) -> p c t", c=NCH))
                      yield
                  for c in range(NCH):
                      ps = P.bank(4 + c % 2)
                      cs = slice(c * CH, (c + 1) * CH)
                      P.mm(ps[:, 0:128], ABD[:, c, :], BBD[:, c, :])
                      P.mm(ps[:, 128:256], BBD[:, c, :], ABD[:, c, :])
                      P.mm(ps[:, 256:384], KBD[:, c, :], ABD[:, c, :])
                      P.mm(ps[:, 384:448], BBD[:, c, :], RT[:, cs])
                      P.mm(ps[:, 448:512], KBD[:, c, :], RT[:, cs])
                      P.tt("dve", AMCp[:, c, :], ps[:, 0:256], maskc[:, 0:256], ALU.mult)
                      P.tt("dve", AMCq[:, c, :], ps[:, 256:512], maskc[:, 256:512], ALU.mult)
                      if c % 2 == 1:
                          yield
                  for c in range(NCH):
                      P.tt("pool", TTs[:, c, :], AMCp[:, c, 128:256], ident_f, ALU.add)
                  TTb = [TTs, TTq]

                  def sq(lev):
                      for c2 in range(NCH // 2):
                          ps = P.bank(4 + c2 % 2)
                          for i in range(2):
                              c = c2 * 2 + i
                              if lev == 1:
                                  p_prev, pt_prev = AMCp[:, c, 0:128], AMCp[:, c, 128:256]
                              else:
                                  p_prev, pt_prev = PP[(lev - 1) % 2][:, c, 0, :], PP[(lev - 1) % 2][:, c, 1, :]
                              P.mm(ps[:, (i * 2) * 128:(i * 2 + 1) * 128], pt_prev, p_prev)
                              if lev < 5:
                                  P.mm(ps[:, (i * 2 + 1) * 128:(i * 2 + 2) * 128], p_prev, pt_prev)
                          e_ = "act" if c2 % 2 == 0 else "dve"
                          if lev < 5:
                              P.copy(e_, PP[lev % 2][:, c2 * 2:c2 * 2 + 2, :, :],
                                     ps.rearrange("p (c a t) -> p c a t", c=2, a=2))
                          else:
                              P.copy(e_, PP[lev % 2][:, c2 * 2:c2 * 2 + 2, 0, :],
                                     ps.rearrange("p (c a t) -> p c a t", c=2, a=2)[:, :, 0, :])
                          if c2 % 2 == 1:
                              yield

                  def ttapply(lev, cur):
                      for c4 in range(NCH // 4):
                          ps = P.bank(3)
                          for i in range(4):
                              c = c4 * 4 + i
                              P.mm(ps[:, i * 128:(i + 1) * 128], PP[lev % 2][:, c, 0, :], TTb[cur][:, c, :])
                          P.tt("dve", TTb[1 - cur][:, c4 * 4:c4 * 4 + 4, :],
                               ps.rearrange("p (c t) -> p c t", c=4), TTb[cur][:, c4 * 4:c4 * 4 + 4, :], ALU.add)
                          yield

                  def inverse():
                      cur = 0
                      yield from sq(1)
                      for lev in range(1, 6):
                          if lev < 5:
                              yield from sq(lev + 1)
                          yield from ttapply(lev, cur)
                          cur = 1 - cur
                      assert cur == 1

                  gi = inverse()
                  gs = genS1(bi + 1) if bi + 1 < len(blocks) else None
                  di, ds = False, gs is None
                  k_ = 0
                  while not (di and ds):
                      if not di:
                          try:
                              next(gi)
                          except StopIteration:
                              di = True
                      k_ += 1
                      if not ds and (di or k_ % S1_EVERY == 0):
                          try:
                              next(gs)
                          except StopIteration:
                              ds = True
                      yield

              def stageQ(bi):
                  hp, tb = blocks[bi]
                  qs = QS[bi % 2]
                  ABD, VTM, BHTM, KHTM, AMCq, TTq = (qs[k] for k in ("ABD", "VTM", "BHTM", "KHTM", "AMCq", "TT"))
                  RT, BONq, Dq = (S3[bi % 3][k] for k in ("RT", "BON", "Dq"))
                  hs_ = slice(hp * 128, (hp + 1) * 128)
                  sl = slice(tb * TB, (tb + 1) * TB)
                  Hs = Hst[l][:, hp, :]
                  P.copy("act", Hb, Hs)
                  psY = P.bank(7)
                  for c in range(NCH):
                      cs = slice(c * CH, (c + 1) * CH)
                      ps = P.bank(6)
                      P.mm(ps[:, 0:128], ABD[:, c, :], Hb, start=True, stop=False)
                      P.mm(ps[:, 0:128], AMCq[:, c, 0:128], VTM[:, c, :], start=False, stop=True)
                      P.copy("act", Xb, ps[:, 0:128])
                      yield
                      P.mm(ps[:, 128:256], TTq[:, c, :], Xb)
                      P.copy("act", Ub, ps[:, 128:256])
                      yield
                      P.mm(psY[:, cs], Hb, RT[:, cs], start=True, stop=False)
                      P.mm(psY[:, cs], Ub, AMCq[:, c, 128:192], start=False, stop=False)
                      P.mm(psY[:, cs], VTM[:, c, :], AMCq[:, c, 192:256], start=False, stop=True)
                      P.mm(ps[:, 256:384], BHTM[:, c, :], Ub, start=True, stop=False)
                      P.mm(ps[:, 256:384], KHTM[:, c, :], VTM[:, c, :], start=False, stop=True)
                      P.stt("dve", Hs, Hs, Dq[:, c, :], ps[:, 256:384], ALU.mult, ALU.add)
                      P.copy("act", Hb, Hs)
                      yield
                  P.copy("act", YC, psY)
                  P.copy("dve", YB, psY)
                  ps = P.bank(6)
                  P.mm(ps, bdones_b, YB)
                  P.stt("dve", YC, ps, -1.0 / 64, YC, ALU.mult, ALU.add)
                  P.act(SQ2, YC, AF.Square)
                  yield
                  ps = P.bank(7)
                  P.mm(ps, bdones_b, SQ2)
                  P.act(RSq, ps, AF.Ln, bias=EPS[:, 1:2], scale=1.0 / 64)
                  P.act(RSq, RSq, AF.Exp, scale=-0.5)
                  P.tt("dve", YC, YC, RSq, ALU.mult)
                  P.ts("dve", YC, YC, vcol(l, "ln_w", hp), vcol(l, "ln_b", hp), ALU.mult, ALU.add)
                  P.tt("pool", YC, YC, BONq, ALU.add)
                  yield
                  ps = P.bank(6)
                  P.mm(ps, WG[:, hs_], SDG[:, sl])
                  P.tt("dve", YR[:, hp, sl], YC, ps, ALU.mult)
                  yield

              def run_gens(gq, gp, ratio=RATIO):
                  dq = gq is None
                  dp = gp is None
                  while not (dq and dp):
                      if not dq:
                          try:
                              next(gq)
                          except StopIteration:
                              dq = True
                      for _ in range(ratio if not dq else 1000000):
                          if dp:
                              break
                          try:
                              next(gp)
                          except StopIteration:
                              dp = True

              rwkv_top[0] = P.mark()
              run_gens(None, genS1(0))
              run_gens(None, stageP(0))
              for bi in range(len(blocks)):
                  run_gens(stageQ(bi), stageP(bi + 1) if bi + 1 < len(blocks) else None)
              P.release(base)
              if (l, hf) == (0, 0):
                  tap("yr0", YR)
              phase("yr%d%d" % (l, hf))

              mP = P.mark()
              PSETS = [dict(ZP=P.tile([128, 16 + HS], F32), SA=P.tile([128, 16 + HS], F32),
                            SB=P.tile([128, 16 + HS], F32), PLD=P.tile([128, HS], BF16)) for _ in range(2)]
              PW = P.tile([128, 4, 128], BF16)
              load_w(PW, pool_w_d[l].rearrange("g c d -> c g d"))
              QT = P.tile([128, HS], BF16)
              EXs = [P.tile([128, 2, TB], BF16) for _ in range(2)]
              RDs = [P.tile([128, TB], F32) for _ in range(2)]
              pend = []
              sctr = [0]

              def defer(delay, fn):
                  pend.append([sctr[0] + delay, fn])

              def run_due(flush=False):
                  rest = []
                  for due, fn in pend:
                      if flush or due <= sctr[0]:
                          fn()
                      else:
                          rest.append([due, fn])
                  pend[:] = rest

              def pool_tail(g):
                  ZP, SA, SB_, PLD = (PSETS[g % 2][k] for k in ("ZP", "SA", "SB", "PLD"))
                  win = (2, 4, 8, 16)[g]
                  src = ZP
                  step = 1
                  bufs = [SA, SB_]
                  bi = 0
                  while step < win:
                      dst = bufs[bi]
                      bi = 1 - bi
                      lo = 2 * step - 1
                      P.tt("dve" if step in (1, 4) else "pool", dst[:, lo:16 + HS], src[:, lo:16 + HS],
                           src[:, lo - step:16 + HS - step], ALU.add)
                      src = dst
                      step *= 2
                  P.stt("dve", PLD, src[:, 16:16 + HS], 1.0 / win, ZP[:, 16:16 + HS], ALU.mult, ALU.subtract)
                  if hf == 0:
                      n = win - 1
                      P.tt("pool", SA[:, 0:n], src[:, 16:16 + n], invcnt[:, 0:n], ALU.mult)
                      P.tt("pool", PLD[:, 0:n], SA[:, 0:n], ZP[:, 16:16 + n], ALU.subtract)

                  def mix():
                      for tb2 in range(NTB):
                          ps2 = P.bank(3 + tb2)
                          P.mm(ps2, PW[:, g, :], PLD[:, tb2 * TB:(tb2 + 1) * TB])
                          P.ts("dve", PO[:, g, tb2 * TB:(tb2 + 1) * TB], ps2, vcol(l, "pool_b", g),
                               vcol(l, "pool_scale", g), ALU.add, ALU.mult)
                  defer(3, mix)

              def pa_consume(ci, chunk, tb, ps):
                  sctr[0] += 1
                  if chunk < 4:
                      g = chunk
                      ZP = PSETS[g % 2]["ZP"]
                      if tb == 0:
                          P.copy("dve", ZP[:, 0:16], pc[l][:, g, :])
                      P.copy("act", ZP[:, 16 + tb * TB:16 + (tb + 1) * TB], ps)
                      run_due()
                      if tb == NTB - 1:
                          P.copy("dve", pc[l][:, g, :], ZP[:, HS:HS + 16])
                          pool_tail(g)
                      return
                  hh = chunk - 4
                  k = hh * NTB + tb
                  q = QT[:, tb * TB:(tb + 1) * TB]
                  EX, RD = EXs[k % 2], RDs[k % 2]
                  P.copy("act", q, ps)
                  run_due()
                  psd = P.bank(5)
                  pso = P.bank(6 + k % 2)

                  def stB():
                      for mt in range(2):
                          pss = P.bank(3 + mt)
                          P.mm(pss, kT[l][:, hh, mt * 128:(mt + 1) * 128], q)
                          P.act(EX[:, mt, :], pss, AF.Exp, scale=float(128 ** -0.5))

                  def stC():
                      for mt in range(2):
                          P.mm(psd, ones_b, EX[:, mt, :], start=(mt == 0), stop=(mt == 1))
                      for mt in range(2):
                          P.mm(pso, Vt[l][:, mt, hh * 128:(hh + 1) * 128], EX[:, mt, :],
                               start=(mt == 0), stop=(mt == 1))

                  def stD():
                      P.act(RD, psd, AF.Ln)
                      P.act(RD, RD, AF.Exp, scale=-1.0)
                      P.tt("dve", AO[:, hh, tb * TB:(tb + 1) * TB], pso, RD, ALU.mult)
                  defer(1, stB)
                  defer(2, stC)
                  defer(3, stD)

              proj(w_in_l, [0, 1, 2, 3, 4, 5, 6, 7], H, pa_consume)
              for _ in range(4):
                  sctr[0] += 1
                  run_due()
              run_due(flush=True)
              P.release(mP)
              if (l, hf) == (0, 0):
                  tap("po0", PO)
                  tap("ao0", AO)
              phase("po%d%d" % (l, hf))
              phase("ao%d%d" % (l, hf))

              mM = P.mark()
              WO = P.tile([128, 8, D], BF16)

              def load_wo(g4):
                  load_w(WO[:, :, g4 * 256:(g4 + 1) * 256],
                         w_o_d[l][:, g4 * 256:(g4 + 1) * 256].rearrange("(kc p) c -> p kc c", p=128))
              mM2 = P.mark()
              MS = [dict(GT=[P.tile([128, TB], F32) for _ in range(3)], ACCS=[P.tile([128, TB], F32) for _ in range(NTB)],
                         WPP=P.tile([128, 4, 128], BF16), WPM=P.tile([128, 4, 128], BF16), WPR=P.tile([128, 8, 128], BF16))
                    for _ in range(2)]
              def load_branch_w(j):
                  ms = MS[j % 2]
                  cs_ = slice(j * 128, (j + 1) * 128)
                  load_w(ms["WPP"], w_pp_d[l][:, cs_].rearrange("(kc p) c -> p kc c", p=128))
                  load_w(ms["WPM"], w_pm_d[l][:, cs_].rearrange("(kc p) c -> p kc c", p=128))
                  load_w(ms["WPR"], w_pr_d[l][:, cs_].rearrange("(kc p) c -> p kc c", p=128))

              def gate_consume(ci, chunk, tb, ps):
                  b = (chunk - 34) // 8
                  j = (chunk - 34) % 8
                  GT, ACCS, WPP, WPM, WPR = (MS[j % 2][k] for k in ("GT", "ACCS", "WPP", "WPM", "WPR"))
                  if b == 0 and tb == 0 and j + 1 < 8:
                      load_branch_w(j + 1)
                  if b == 1 and tb == 0 and 2 <= j < 6:
                      load_wo(j - 2)
                  P.act(GT[b], ps, AF.Sigmoid, bias=vcol(l, "gate_b", b * 8 + j))
                  src, w, nk = ((PO, WPP, 4), (YR, WPR, 8), (AO, WPM, 4))[b]
                  pv = P.bank(3 + b)
                  for kc in range(nk):
                      P.mm(pv, w[:, kc, :], src[:, kc, tb * TB:(tb + 1) * TB], start=(kc == 0), stop=(kc == nk - 1))
                  if b == 0:
                      P.tt("dve", ACCS[tb], pv, GT[b], ALU.mult)
                  else:
                      P.tt("dve", GT[b], pv, GT[b], ALU.mult)
                      if b == 1:
                          P.tt("dve", ACCS[tb], ACCS[tb], GT[b], ALU.add)
                      else:
                          P.tt("dve", M[:, j, tb * TB:(tb + 1) * TB], ACCS[tb], GT[b], ALU.add)

              load_branch_w(0)
              order = []
              for j in range(8):
                  order += [34 + j, 42 + j, 50 + j]
              proj(w_in_l, order, H, gate_consume)
              P.release(mM2)
              if (l, hf) == (0, 0):
                  tap("merged0", M)
              phase("merged%d%d" % (l, hf))

              mO = P.mark()
              g_post = load_gain(1 + l * 4 + 1)
              g_fpre = load_gain(1 + l * 4 + 2)
              def wo_mm(tt, hh, ps):
                  for kc in range(8):
                      P.mm(ps, M[:, kc, tt * 128:(tt + 1) * 128], WO[:, kc, hh * 512:(hh + 1) * 512],
                           start=(kc == 0), stop=(kc == 7))

              residual_stage(t_half, x_res, xm_d, wo_mm, g_post, g_fpre, H2)
              P.release(mM)
              if (l, hf) == (0, 0):
                  tap("h2_0", H2)
              phase("h2%d%d" % (l, hf))

              mF = P.mark()
              FS = [dict(UG=P.tile([128, 2 + HS], F32), UV=P.tile([128, 2 + HS], F32), CG=P.tile([128, HS], F32),
                         CV=P.tile([128, HS], F32)) for _ in range(2)]
              w_fu_l = w_fu_d[l]
              wd_off = P.mark()
              WD = P.tile([128, NFF, D], BF16)

              def load_wd(kc):
                  load_w(WD[:, kc:kc + 2, :],
                         w_fd_d[l][kc * 128:(kc + 2) * 128, :].rearrange("(kc p) c -> p kc c", p=128))

              def conv(U, c, out):
                  cw = [vcol(l, "conv_w", k * 2 * NFF + c) for k in range(3)]
                  P.act(out, U[:, 0:HS], AF.Identity, bias=vcol(l, "conv_b", c), scale=cw[0])
                  P.stt("dve", out, U[:, 1:HS + 1], cw[1], out, ALU.mult, ALU.add)
                  P.stt("dve", out, U[:, 2:HS + 2], cw[2], out, ALU.mult, ALU.add)

              def up_consume(ci, chunk, tb, ps):
                  c = chunk % NFF
                  UG, UV, CG, CV = (FS[c % 2][k] for k in ("UG", "UV", "CG", "CV"))
                  U = UG if chunk < NFF else UV
                  if tb == 0:
                      P.copy("dve", U[:, 0:2], cc[l][:, chunk, :])
                  P.copy("act", U[:, 2 + tb * TB:2 + (tb + 1) * TB], ps)
                  if tb == NTB - 1:
                      P.copy("dve", cc[l][:, chunk, :], U[:, HS:HS + 2])
                      if chunk >= NFF:
                          if c % 2 == 0:
                              load_wd(c)
                          conv(UG, c, CG)
                          conv(UV, NFF + c, CV)

                          def tail(c=c, CG=CG, CV=CV):
                              P.act(CG, CG, AF.Gelu_apprx_tanh)
                              P.tt("dve", ACTB[:, c, :], CG, CV, ALU.mult)
                          up_pending.append(tail)
                      elif up_pending:
                          up_pending.pop(0)()

              order = []
              for c in range(NFF):
                  order += [c, NFF + c]
              up_pending = []
              proj(w_fu_l, order, H2, up_consume)
              while up_pending:
                  up_pending.pop(0)()
              P.release(mF)
              if (l, hf) == (0, 0):
                  tap("act0", ACTB)
              phase("act%d%d" % (l, hf))

              P.release(off_h2)
              g_fpost = load_gain(1 + l * 4 + 3)
              g_next = load_gain(1 + (l + 1) * 4 + 0) if l + 1 < NL else None
              def fd_mm(tt, hh, ps):
                  for kc in range(NFF):
                      P.mm(ps, ACTB[:, kc, tt * 128:(tt + 1) * 128], WD[:, kc, hh * 512:(hh + 1) * 512],
                           start=(kc == 0), stop=(kc == NFF - 1))

              residual_stage(t_half, xm_d, x_lout, fd_mm, g_fpost, g_next, H)
              assert P.mark() <= wd_off, (P.mark(), wd_off)
              P.release(base)

    except _Stop:
        pass
    P.rwkv_top = rwkv_top[0]
    P.emit()
    return nc, P


_CACHE = {}


def host_tables(inp):
    cols = []
    for l in range(NL):
        rows = []
        rows.append(inp["mu_shift"][l].reshape(26, 128))
        rows.append(inp["pool_b"][l].reshape(4, 128))
        rows.append(inp["pool_scale"][l].reshape(4, 128))
        rows.append(inp["w0"][l].reshape(8, 128))
        rows.append(inp["a0"][l].reshape(8, 128))
        rows.append(inp["k_k"][l].reshape(8, 128))
        rows.append(inp["k_a"][l].reshape(8, 128))
        rows.append(inp["r_k"][l].reshape(8, 128))
        rows.append(inp["ln_x_w"][l].reshape(8, 128))
        rows.append(inp["ln_x_b"][l].reshape(8, 128))
        rows.append(inp["v0"][l - 1].reshape(8, 128) if l > 0 else np.zeros((8, 128), np.float32))
        rows.append(inp["gate_b"][l].reshape(24, 128))
        rows.append(inp["conv_w"][l].reshape(3 * 44, 128))
        rows.append(inp["conv_b"][l].reshape(44, 128))
        cols.append(np.concatenate(rows, 0))
    vt = np.ascontiguousarray(np.concatenate(cols, 0).T.astype(np.float32))
    gains = [inp["mem_norm"].reshape(1, D)]
    for l in range(NL):
        gains += [inp["norm_mix_pre"][l:l + 1], inp["norm_mix_post"][l:l + 1],
                  inp["norm_ffn_pre"][l:l + 1], inp["norm_ffn_post"][l:l + 1]]
    gains = np.ascontiguousarray(np.concatenate(gains, 0).astype(np.float32))
    return vt, gains


def make_in_maps(inp, n_cores, S):
    inp = {k: np.asarray(v) for k, v in inp.items()}
    vt, gains = host_tables(inp)
    cst = make_consts()
    shared = dict(cst=cst, vt=vt, gains=gains)
    for k in ("w_in", "pool_w", "w_proj_pool", "w_mem_kv", "w_proj_mem", "w_up_decay", "w_up_a", "w_up_g",
              "w_down_v", "w_up_v", "w_proj_rwkv", "w_o", "w_ffn_up", "w_ffn_down"):
        shared[k] = np.ascontiguousarray(inp[k], dtype=np.float32)
    maps = []
    for b in range(n_cores):
        m = dict(shared)
        m["x"] = np.ascontiguousarray(inp["x"][b, :S], dtype=np.float32)
        m["mem"] = np.ascontiguousarray(inp["mem"][b], dtype=np.float32)
        maps.append(m)
    return maps


def kernel(**inputs):
    S = 2048
    if "nc" not in _CACHE:
        _CACHE["nc"] = build(S)
    nc, P = _CACHE["nc"]
    maps = make_in_maps(inputs, 8, S)
    res = run_bass_kernel_spmd(nc, maps, core_ids=list(range(8)))
    out = np.stack([np.asarray(r["out"], dtype=np.float32) for r in res.results], 0)
    return out
```

```python
import contextlib
import numpy as np
import concourse.bass as bass
import concourse.mybir as mybir
from concourse.bass_utils import run_bass_kernel_spmd

F32 = mybir.dt.float32
BF16 = mybir.dt.bfloat16
U8 = mybir.dt.uint8
AF = mybir.ActivationFunctionType
ALU = mybir.AluOpType

EPOCH = 8192
NDMA = {"sp": 8, "act": 4, "pool": 8}
BUCKET = 1024
ENGMAP = {"pe": "tensor", "act": "scalar", "dve": "vector", "pool": "gpsimd", "sp": "sync"}


def _esize(dt):
    return mybir.dt.size(dt)


class Prog:
    def __init__(self, nc, sbuf_bytes=212736):
        self.nc = nc
        self.ops = []
        self.stack = contextlib.ExitStack()
        self.sems = {}
        self.dma_cnt = {q: 0 for q in NDMA}
        self.pstride = {}
        self.track = {}
        self.bank_last = {}
        self.arena = self.stack.enter_context(nc.sbuf_tensor("arena", [128, sbuf_bytes], U8))
        self.pstride["arena"] = sbuf_bytes
        self.sbuf_bytes = sbuf_bytes
        self.top = 0
        self.peak = 0
        self.psum = []
        for i in range(8):
            t = self.stack.enter_context(nc.psum_tensor("pb%d" % i, [128, 512], F32))
            self.pstride["pb%d" % i] = 2048
            self.psum.append(t)

    def mark(self):
        return self.top

    def release(self, m):
        self.top = m

    def tile(self, shape, dt):
        nb = int(np.prod(shape[1:])) * _esize(dt)
        off = self.top
        self.top += (nb + 63) // 64 * 64
        self.peak = max(self.peak, self.top)
        assert self.top <= self.sbuf_bytes, ("SBUF overflow", self.top)
        v = self.arena[:, off:off + nb].bitcast(dt)
        if len(shape) > 2:
            names = " ".join("d%d" % i for i in range(1, len(shape)))
            kw = {"d%d" % i: shape[i] for i in range(1, len(shape) - 1)}
            v = v.rearrange("p (%s) -> p %s" % (names, names), **kw)
        if shape[0] < 128:
            v = v[0:shape[0]]
        return v

    def bank(self, i, dt=F32):
        t = self.psum[i]
        return t[:, :] if dt == F32 else t[:, :].bitcast(dt)

    def _rect(self, a):
        name = a.tensor.name
        es = _esize(a.dtype)
        dims = a.ap
        boff = a.offset * es
        if name in self.pstride:
            ps = self.pstride[name]
            p0, f0 = divmod(boff, ps)
            if dims[0][0] * es == ps:
                pn = dims[0][1]
                rest = dims[1:]
            elif dims[0][0] == 0:
                pn = 1
                rest = dims[1:]
            else:
                pn = 1
                rest = dims
            ext = sum(s * (c - 1) for s, c in rest) + 1
            return (name, p0, p0 + pn, f0, f0 + ext * es)
        ext = sum(abs(s) * (c - 1) for s, c in dims) + 1
        return ("dram:" + name, 0, 1, boff, boff + ext * es)

    @staticmethod
    def _ovl(a, b):
        return a[1] < b[2] and b[1] < a[2] and a[3] < b[4] and b[3] < a[4]

    @staticmethod
    def _cov(a, b):
        return a[1] <= b[1] and a[2] >= b[2] and a[3] <= b[3] and a[4] >= b[4]

    def _buckets(self, r):
        bs = BUCKET if not r[0].startswith("dram:") else (1 << 20)
        return range(r[3] // bs, (r[4] - 1) // bs + 1)

    def _access(self, r, is_write, me, engkey, deps):
        t = self.track.setdefault(r[0], {})
        for b in self._buckets(r):
            d = t.setdefault(b, {"w": [], "r": {}})
            for (op, rr) in d["w"]:
                if self._ovl(r, rr):
                    deps.add(op)
            if is_write:
                for (ek, rr), op in d["r"].items():
                    if self._ovl(r, rr):
                        deps.add(op)
                d["w"] = [(op, rr) for (op, rr) in d["w"] if not self._cov(r, rr)]
                d["r"] = {k: op for k, op in d["r"].items() if not self._cov(r, k[1])}
                d["w"].append((me, r))
            else:
                d["r"][(engkey, r)] = me

    def op(self, eng, fn, outs=(), ins=(), dma=False):
        i = len(self.ops)
        deps = set()
        engkey = ("dma", i) if dma else eng
        for a in ins:
            if a is None or isinstance(a, (int, float)):
                continue
            self._access(self._rect(a), False, i, engkey, deps)
        for a in outs:
            if a is None:
                continue
            self._access(self._rect(a), True, i, engkey, deps)
        deps.discard(i)
        for a in list(ins) + list(outs):
            if a is None or isinstance(a, (int, float)):
                continue
            r = self._rect(a)
            if not r[0].startswith("pb"):
                continue
            for q in range(r[1] // 32, (r[2] - 1) // 32 + 1):
                d = self.bank_last.setdefault((r[0], q), {})
                for e2, v in d.items():
                    if e2 != eng:
                        deps.add(v)
                d[eng] = i
        deps.discard(i)
        o = dict(eng=eng, fn=fn, deps=deps, dma=dma)
        if dma:
            n = self.dma_cnt[eng]
            self.dma_cnt[eng] += 1
            o["dsem"] = ("d" + eng, n % NDMA[eng])
            o["dval"] = 16 * (n // NDMA[eng] + 1)
        self.ops.append(o)
        return i

    def mm(self, out, lhsT, rhs, start=True, stop=True):
        return self.op("pe", lambda e: e.matmul(out, lhsT=lhsT, rhs=rhs, start=start, stop=stop), [out], [lhsT, rhs])

    def transpose(self, out, in_, ident):
        return self.op("pe", lambda e: e.transpose(out, in_, ident), [out], [in_, ident])

    def act(self, out, in_, func, bias=None, scale=None, accum_out=None):
        kw = {}
        if bias is not None:
            kw["bias"] = bias
        if scale is not None:
            kw["scale"] = scale
        if accum_out is not None:
            kw["accum_out"] = accum_out
        ins = [in_] + [x for x in (bias, scale) if x is not None and not isinstance(x, (int, float))]
        return self.op("act", lambda e: e.activation(out, in_, func, **kw), [out, accum_out], ins)

    def tt(self, eng, out, in0, in1, op):
        return self.op(eng, lambda e: e.tensor_tensor(out, in0, in1, op), [out], [in0, in1])

    def ts(self, eng, out, in0, s1, s2, op0, op1=None):
        ins = [in0] + [x for x in (s1, s2) if x is not None and not isinstance(x, (int, float))]
        if op1 is None:
            return self.op(eng, lambda e: e.tensor_scalar(out, in0, s1, None, op0), [out], ins)
        return self.op(eng, lambda e: e.tensor_scalar(out, in0, s1, s2, op0, op1), [out], ins)

    def stt(self, eng, out, in0, scalar, in1, op0, op1):
        ins = [in0, in1] + ([scalar] if not isinstance(scalar, (int, float)) else [])
        return self.op(eng, lambda e: e.scalar_tensor_tensor(out, in0, scalar, in1, op0, op1), [out], ins)

    def copy(self, eng, out, in_):
        if eng == "act":
            return self.op("act", lambda e: e.copy(out, in_), [out], [in_])
        return self.op(eng, lambda e: e.tensor_copy(out, in_), [out], [in_])

    def memset(self, eng, ap, val):
        return self.op(eng, lambda e: e.memset(ap, val), [ap], [])

    def recip(self, out, in_):
        return self.op("dve", lambda e: e.reciprocal(out, in_), [out], [in_])

    def scan(self, out, d0, d1, init, op0, op1):
        return self.op("dve", lambda e: e.tensor_tensor_scan(out, d0, d1, init, op0, op1), [out], [d0, d1])

    def dma(self, q, out, in_):
        return self.op(q, lambda e: e.dma_start(out=out, in_=in_), [out], [in_], dma=True)

    def _token(self, o):
        if o["dma"]:
            return o["dsem"], o["dval"]
        c = o["cseq"]
        return (o["eng"], c // EPOCH), c % EPOCH + 1

    def sem(self, key):
        if key not in self.sems:
            self.sems[key] = self.stack.enter_context(self.nc.semaphore("s_%s_%s" % key))
        return self.sems[key]

    def emit(self):
        nc = self.nc
        cnt = {e: 0 for e in ENGMAP}
        for o in self.ops:
            if not o["dma"]:
                o["cseq"] = cnt[o["eng"]]
                cnt[o["eng"]] += 1
        for o in self.ops:
            self.sem(self._token(o)[0])
        self.nwaits = 0
        with nc.Block() as block:
            for e, bname in ENGMAP.items():
                mine = [o for o in self.ops if o["eng"] == e]
                if not mine:
                    continue

                def body(eng, e=e, mine=mine):
                    known = {}
                    for o in mine:
                        need = {}
                        for di in o["deps"]:
                            d = self.ops[di]
                            if d["eng"] == e and e == "pe" and not d["dma"]:
                                continue
                            sk, v = self._token(d)
                            if v > need.get(sk, 0):
                                need[sk] = v
                        if o["dma"]:
                            sk, v = o["dsem"], o["dval"] - 16
                            if v > 0 and v > need.get(sk, 0):
                                need[sk] = v
                        for sk in sorted(need):
                            v = need[sk]
                            if known.get(sk, 0) >= v:
                                continue
                            if not sk[0].startswith("d") and any(
                                k2[0] == sk[0] and k2[1] > sk[1] for k2 in known
                            ):
                                continue
                            eng.wait_ge(self.sems[sk], v)
                            self.nwaits += 1
                            known[sk] = v
                        ins = o["fn"](eng)
                        sk, v = self._token(o)
                        ins.then_inc(self.sems[sk], 16 if o["dma"] else 1)
                    if e in NDMA:
                        last = {}
                        for o in mine:
                            if o["dma"]:
                                last[o["dsem"]] = o["dval"]
                        for sk in sorted(last):
                            if known.get(sk, 0) < last[sk]:
                                eng.wait_ge(self.sems[sk], last[sk])

                getattr(block, bname)(body)


D = 1024
KC = 8
NL = 2
IN_COLS = 7424
D_FF = 2816
NFF = D_FF // 128
HS = 1024
TB = 512
NTB = HS // TB
CH = 64
NCH = TB // CH
C0 = float(np.exp(-0.5))
RATIO = 1
S1_EVERY = 2

VT_LAYOUT = [("mu", 26), ("pool_b", 4), ("pool_scale", 4), ("w0", 8), ("a0", 8), ("k_k", 8), ("k_a", 8),
             ("r_k", 8), ("ln_w", 8), ("ln_b", 8), ("v0", 8), ("gate_b", 24), ("conv_w", 132), ("conv_b", 44)]
VT_OFF = {}
_o = 0
for _n, _c in VT_LAYOUT:
    VT_OFF[_n] = _o
    _o += _c
NVL = _o

CST_IDENT = 0
CST_MASKC = 128
CST_CMASK = CST_MASKC + 512
CST_INVCNT = CST_CMASK + 512
CST_BDONES = CST_INVCNT + 16
CST_BDNEG = CST_BDONES + 128
NCST = CST_BDNEG + 128


def make_consts():
    c = np.zeros((128, NCST), np.float32)
    c[:, CST_IDENT:CST_IDENT + 128] = np.eye(128, dtype=np.float32)
    p = np.arange(128)
    hp_, tp = p // 64, p % 64
    q = np.arange(128)
    hq, tq = q // 64, q % 64
    same = (hp_[:, None] == hq[None, :])
    lower = same & (tq[None, :] < tp[:, None])
    upper = same & (tq[None, :] > tp[:, None])
    c[:, CST_MASKC:CST_MASKC + 128] = lower
    c[:, CST_MASKC + 128:CST_MASKC + 256] = upper
    c[:, CST_MASKC + 256:CST_MASKC + 384] = upper
    t64 = np.arange(64)
    incl = (tp[:, None] <= t64[None, :])
    c[:, CST_MASKC + 384:CST_MASKC + 448] = incl
    c[:, CST_MASKC + 448:CST_MASKC + 512] = incl
    cm = np.ones(512, np.float32)
    cm[::64] = 0.0
    c[:, CST_CMASK:CST_CMASK + 512] = cm[None, :]
    c[:, CST_INVCNT:CST_INVCNT + 16] = (1.0 / (np.arange(16) + 1.0))[None, :]
    c[:, CST_BDONES:CST_BDONES + 128] = same
    c[:, CST_BDNEG:CST_BDNEG + 128] = -same.astype(np.float32)
    return c


class _Stop(Exception):
    pass


def build(S=2048, taps=(), stop=None):
    NHF = S // HS
    nc = bass.Bass("TRN2", target_bir_lowering=False)
    P = Prog(nc)

    def din(name, shape):
        return nc.dram_tensor(name, list(shape), F32, kind="ExternalInput").ap()

    x_d = din("x", [S, D])
    mem_d = din("mem", [256, D])
    cst_d = din("cst", [128, NCST])
    vt_d = din("vt", [128, NL * NVL])
    gains_d = din("gains", [9, D])
    w_in_d = din("w_in", [NL, D, IN_COLS])
    pool_w_d = din("pool_w", [NL, 4, 128, 128])
    w_pp_d = din("w_proj_pool", [NL, 512, D])
    w_kv_d = din("w_mem_kv", [NL, D, 1024])
    w_pm_d = din("w_proj_mem", [NL, 512, D])
    w_upd_d = din("w_up_decay", [NL, 64, D])
    w_upa_d = din("w_up_a", [NL, 64, D])
    w_upg_d = din("w_up_g", [NL, 128, D])
    w_dv_d = din("w_down_v", [1, D, 32])
    w_uv_d = din("w_up_v", [1, 32, D])
    w_pr_d = din("w_proj_rwkv", [NL, D, D])
    w_o_d = din("w_o", [NL, D, D])
    w_fu_d = din("w_ffn_up", [NL, D, 2 * D_FF])
    w_fd_d = din("w_ffn_down", [NL, D_FF, D])
    out_d = nc.dram_tensor("out", [S, D], F32, kind="ExternalOutput").ap()
    xm_d = nc.dram_tensor("xm_scr", [S, D], F32, kind="Internal").ap()
    xl_d = nc.dram_tensor("xl_scr", [S, D], F32, kind="Internal").ap()
    vf_d = nc.dram_tensor("vf_scr", [8, 128, S], F32, kind="Internal").ap()
    v1_d = nc.dram_tensor("v1_scr", [8, 128, S], F32, kind="Internal").ap()
    tap_d = {}
    for name, shape in taps:
        tap_d[name] = nc.dram_tensor("tap_" + name, list(shape), F32, kind="ExternalOutput").ap()

    cst = P.tile([128, NCST], F32)
    VT = P.tile([128, NL * NVL], F32)
    OMU = P.tile([128, NL * 26], F32)
    OMKA = P.tile([128, NL * 8], F32)
    EPS = P.tile([128, 4], F32)
    ident_b = P.tile([128, 128], BF16)
    bdones_b = P.tile([128, 128], BF16)
    ones_b = P.tile([128, 128], BF16)
    memT = P.tile([128, 8, 256], BF16)
    kT = [P.tile([128, 4, 256], BF16) for _ in range(NL)]
    Vt = [P.tile([128, 2, 512], BF16) for _ in range(NL)]
    Hst = [P.tile([128, 8, 128], F32) for _ in range(NL)]
    zc = [P.tile([128, 26], F32) for _ in range(NL)]
    pc = [P.tile([128, 4, 16], F32) for _ in range(NL)]
    cc = [P.tile([128, 2 * NFF, 2], F32) for _ in range(NL)]
    H = P.tile([128, 8, HS], BF16)
    WB = [P.tile([128, 8, 256], BF16) for _ in range(4)]
    wb_i = [0]
    big0 = P.mark()
    YR = P.tile([128, 8, HS], BF16)
    off_po = P.mark()
    PO = P.tile([128, 4, HS], BF16)
    AO = P.tile([128, 4, HS], BF16)
    M = P.tile([128, 8, HS], BF16)
    big1 = P.mark()
    P.release(big0)
    ACTB = P.tile([128, NFF, HS], BF16)
    assert P.mark() <= big1
    P.release(big1)
    off_h2 = P.mark()
    H2 = P.tile([128, 8, HS], BF16)
    base = P.mark()

    ident_f = cst[:, CST_IDENT:CST_IDENT + 128]
    maskc = cst[:, CST_MASKC:CST_MASKC + 512]
    cmask = cst[:, CST_CMASK:CST_CMASK + 512]
    invcnt = cst[:, CST_INVCNT:CST_INVCNT + 16]

    def vcol(l, name, j=0):
        o = l * NVL + VT_OFF[name] + j
        return VT[:, o:o + 1]

    P.dma("sp", cst, cst_d)
    P.dma("sp", VT, vt_d)
    P.copy("dve", ident_b, ident_f)
    P.copy("dve", bdones_b, cst[:, CST_BDONES:CST_BDONES + 128])
    P.memset("pool", ones_b, 1.0)
    P.memset("pool", EPS[:, 0:1], 1e-6)
    P.memset("pool", EPS[:, 1:2], 64e-5)
    P.memset("pool", EPS[:, 2:3], 1e-12)
    for l in range(NL):
        P.memset("pool", Hst[l], 0.0)
        P.memset("pool", zc[l], 0.0)
        P.memset("pool", pc[l], 0.0)
        P.memset("pool", cc[l], 0.0)
        o = l * NVL + VT_OFF["mu"]
        P.ts("dve", OMU[:, l * 26:(l + 1) * 26], VT[:, o:o + 26], -1.0, 1.0, ALU.mult, ALU.add)
        o = l * NVL + VT_OFF["k_a"]
        P.ts("dve", OMKA[:, l * 8:(l + 1) * 8], VT[:, o:o + 8], -1.0, 1.0, ALU.mult, ALU.add)

    def load_gain(idx):
        g = P.tile([128, D], F32)
        P.dma("sp", g, gains_d[idx:idx + 1, :].to_broadcast([128, D]))
        return g

    def tap(name, ap_sb):
        if name in tap_d:
            m_ = P.mark()
            tmp = P.tile([128, HS], F32)
            for kc in range(ap_sb.shape[1]):
                P.copy("dve", tmp, ap_sb[:, kc, :])
                P.dma("sp", tap_d[name][:, kc, :], tmp)
            P.release(m_)

    def tm_norm(src, gain, hn, junk, st):
        P.act(junk, src, AF.Square, accum_out=st[:, 0:1])
        P.act(st[:, 1:2], st[:, 0:1], AF.Sqrt, bias=EPS[:, 0:1], scale=1.0 / D)
        P.recip(st[:, 2:3], st[:, 1:2])
        P.stt("dve", hn, src, st[:, 2:3], gain, ALU.mult, ALU.mult)

    def to_fm(hn, dst, t0, bank=7):
        pt = P.bank(bank, BF16)
        for kc in range(8):
            P.transpose(pt[:, kc * 128:(kc + 1) * 128], hn[:, kc * 128:(kc + 1) * 128], ident_b)
        P.copy("act", dst[:, :, t0:t0 + 128], pt[:, 0:1024].rearrange("p (k t) -> p k t", k=8))

    m0 = P.mark()
    g_mem = load_gain(0)
    mt_x = P.tile([128, D], F32)
    mt_h = P.tile([128, D], BF16)
    mt_j = P.tile([128, D], BF16)
    mt_s = P.tile([128, 4], F32)
    for mt in range(2):
        P.dma("sp", mt_x, mem_d[mt * 128:(mt + 1) * 128, :])
        tm_norm(mt_x, g_mem, mt_h, mt_j, mt_s)
        to_fm(mt_h, memT, mt * 128)
    P.release(m0)

    proj_bank = [0]
    rwkv_top = [0]

    def load_w(dst, src):
        P.dma("pool", dst, src)

    def proj(w2d, chunks, rhs, consume, kchunks=8, pf=2):
        groups = []
        i = 0
        while i < len(chunks):
            n = 2 if (i + 1 < len(chunks) and chunks[i + 1] == chunks[i] + 1) else 1
            groups.append((chunks[i], n))
            i += n
        loaded = {}

        def issue(gi):
            c0, n = groups[gi]
            wb = WB[wb_i[0] % 4]
            wb_i[0] += 1
            load_w(wb[:, 0:kchunks, 0:n * 128],
                   w2d[:, c0 * 128:(c0 + n) * 128].rearrange("(kc p) c -> p kc c", p=128))
            loaded[gi] = wb

        for gi in range(min(pf, len(groups))):
            issue(gi)
        ci = 0
        for gi, (c0, n) in enumerate(groups):
            if gi + pf < len(groups):
                issue(gi + pf)
            wb = loaded.pop(gi)
            for j in range(n):
                for tb in range(NTB):
                    ps = P.bank(proj_bank[0] % 3)
                    proj_bank[0] += 1
                    for kc in range(kchunks):
                        P.mm(ps, wb[:, kc, j * 128:(j + 1) * 128], rhs[:, kc, tb * TB:(tb + 1) * TB],
                             start=(kc == 0), stop=(kc == kchunks - 1))
                    consume(ci, c0 + j, tb, ps)
                ci += 1

    def residual_stage(t_half, src_x, dst_x, mm_fn, g_post, g_pre, Hdst):
        n = HS // 128
        xt = [P.tile([128, D], F32) for _ in range(3)]
        yt = [P.tile([128, D], F32) for _ in range(3)]
        hn = [P.tile([128, D], BF16) for _ in range(3)]
        jkA = P.tile([128, D], BF16)
        jkC = P.tile([128, D], BF16)
        st = [P.tile([128, 8], F32) for _ in range(3)]

        def stA(tt):
            b = tt % 3
            r0 = t_half + tt * 128
            P.dma("sp", xt[b], src_x[r0:r0 + 128, :])
            for hh in range(2):
                ps = P.bank((tt % 2) * 2 + hh)
                mm_fn(tt, hh, ps)
                P.copy("act", yt[b][:, hh * 512:(hh + 1) * 512], ps)

        def stB(tt):
            b = tt % 3
            r0 = t_half + tt * 128
            P.act(jkA, yt[b], AF.Square, accum_out=st[b][:, 0:1])
            P.act(st[b][:, 1:2], st[b][:, 0:1], AF.Sqrt, bias=EPS[:, 0:1], scale=1.0 / D)
            P.recip(st[b][:, 2:3], st[b][:, 1:2])
            P.stt("dve", yt[b], yt[b], st[b][:, 2:3], g_post, ALU.mult, ALU.mult)
            P.tt("dve", xt[b], xt[b], yt[b], ALU.add)
            P.dma("sp", dst_x[r0:r0 + 128, :], xt[b])

        def stC1(tt):
            b = tt % 3
            if g_pre is not None:
                tm_norm(xt[b], g_pre, hn[b], jkC, st[b][:, 4:8])

        def stC2(tt):
            b = tt % 3
            if g_pre is not None:
                to_fm(hn[b], Hdst, tt * 128)

        for step in range(n + 3):
            if 0 <= step - 3 < n:
                stC2(step - 3)
            if 0 <= step - 2 < n:
                stC1(step - 2)
            if 0 <= step - 1 < n:
                stB(step - 1)
            if step < n:
                stA(step)

    def phase(name):
        if stop == name:
            raise _Stop()

    try:
      phase('init')
      for hf in range(NHF):
          t_half = hf * HS
          for l in range(NL):
              w_in_l = w_in_d[l]
              if l == 0:
                  m0 = P.mark()
                  g_pre = load_gain(1)
                  xt = [P.tile([128, D], F32) for _ in range(2)]
                  hn = [P.tile([128, D], BF16) for _ in range(2)]
                  jk = P.tile([128, D], BF16)
                  st = [P.tile([128, 4], F32) for _ in range(2)]
                  for tt in range(HS // 128 + 1):
                      b = tt % 2
                      if tt < HS // 128:
                          P.dma("sp", xt[b], x_d[t_half + tt * 128:t_half + (tt + 1) * 128, :])
                          tm_norm(xt[b], g_pre, hn[b], jk, st[b])
                      if tt >= 1:
                          to_fm(hn[(tt - 1) % 2], H, (tt - 1) * 128)
                  P.release(m0)
              if (l, hf) == (0, 0):
                  tap("h0", H)
              phase("h%d%d" % (l, hf))
              x_res = x_d if l == 0 else xl_d
              x_lout = xl_d if l == 0 else out_d

              if hf == 0:
                  m0 = P.mark()
                  for g4 in range(4):
                      wb = WB[wb_i[0] % 4]
                      wb_i[0] += 1
                      load_w(wb, w_kv_d[l][:, g4 * 256:(g4 + 1) * 256].rearrange("(kc p) c -> p kc c", p=128))
                      if g4 < 2:
                          for j in range(2):
                              hh = g4 * 2 + j
                              ps = P.bank(3)
                              for kc in range(8):
                                  P.mm(ps[:, 0:256], wb[:, kc, j * 128:(j + 1) * 128], memT[:, kc, :],
                                       start=(kc == 0), stop=(kc == 7))
                              P.copy("act", kT[l][:, hh, :], ps[:, 0:256])
                      else:
                          for mt in range(2):
                              ps = P.bank(3)
                              for kc in range(8):
                                  P.mm(ps[:, 0:256], memT[:, kc, mt * 128:(mt + 1) * 128], wb[:, kc, :],
                                       start=(kc == 0), stop=(kc == 7))
                              P.copy("act", Vt[l][:, mt, (g4 - 2) * 256:(g4 - 1) * 256], ps[:, 0:256])
                  P.release(m0)

              phase('kv%d%d' % (l, hf))
              P.release(off_po)
              WDA = P.tile([128, D], BF16)
              WG = P.tile([128, D], BF16)
              LA = P.tile([128, HS], BF16)
              SDG = P.tile([128, HS], BF16)
              load_w(WDA[0:64, :], w_upd_d[l])
              load_w(WDA[64:128, :], w_upa_d[l])
              load_w(WG, w_upg_d[l])
              Zt = [P.tile([128, HS + 1], F32) for _ in range(3)]
              ZR = [z_[:, 0:HS] for z_ in Zt]

              def shift_consume(j, Z):
                  mu_c = VT[:, l * NVL + VT_OFF["mu"] + j:l * NVL + VT_OFF["mu"] + j + 1]
                  omu_c = OMU[:, l * 26 + j:l * 26 + j + 1]

                  def consume(ci, chunk, tb, ps):
                      if tb == 0:
                          P.copy("dve", Z[:, 0:1], zc[l][:, j:j + 1])
                      P.act(Z[:, 1 + tb * TB:1 + (tb + 1) * TB], ps, AF.Copy, scale=mu_c)
                      P.stt("dve", Z[:, tb * TB:(tb + 1) * TB], ps, omu_c, Z[:, tb * TB:(tb + 1) * TB],
                            ALU.mult, ALU.add)
                      if tb == NTB - 1:
                          P.copy("dve", zc[l][:, j:j + 1], Z[:, HS:HS + 1])
                  return consume

              _lc = {32: shift_consume(24, Zt[0]), 33: shift_consume(25, Zt[1])}
              proj(w_in_l, [32, 33], H, lambda ci, chunk, tb, ps: _lc[chunk](ci, chunk, tb, ps))
              P.act(LA[0:64, :], ZR[0][0:64, :], AF.Tanh)
              P.copy("pool", LA[64:128, :], ZR[0][64:128, :])
              P.act(SDG, ZR[1], AF.Sigmoid)
              phase('lora%d%d' % (l, hf))

              if l == 1:
                  WDV = P.tile([128, 8, 32], BF16)
                  WUV = P.tile([32, D], BF16)
                  VDB = P.tile([32, HS], BF16)
                  vb = P.tile([128, HS], BF16)
                  VFT = P.tile([128, HS], F32)
                  VG = P.tile([128, TB], F32)
                  load_w(WDV, w_dv_d[0].rearrange("(hp p) r -> p hp r", p=128))
                  load_w(WUV, w_uv_d[0])
                  vb2 = P.tile([128, HS], BF16)
                  vbs = [vb, vb2]
                  _vc = [shift_consume(16 + hp, Zt[hp % 3]) for hp in range(8)]

                  def vpass_consume(ci, chunk, tb, ps):
                      hp = chunk - 24
                      _vc[hp](ci, chunk, tb, ps)
                      if tb == NTB - 1:
                          zr_ = ZR[hp % 3]
                          vb_ = vbs[hp % 2]
                          P.dma("sp", v1_d[hp][:, t_half:t_half + HS], zr_)
                          P.copy("act", vb_, zr_)
                          for tb2 in range(NTB):
                              P.mm(P.bank(3 + tb2)[0:32, :], WDV[:, hp, :], vb_[:, tb2 * TB:(tb2 + 1) * TB],
                                   start=(hp == 0), stop=(hp == 7))

                  proj(w_in_l, [24 + hp for hp in range(8)], H, vpass_consume)
                  for tb in range(NTB):
                      P.copy("act", VDB[:, tb * TB:(tb + 1) * TB], P.bank(3 + tb)[0:32, :])

              SL = [P.tile([128, TB], F32) for _ in range(7)]
              SG, AA, LP, TMP, E2, E3, KK = SL
              KA, BT32, KT32 = [P.tile([128, TB], BF16) for _ in range(3)]
              MKB = P.tile([128, 2, 128], BF16)
              P.copy("dve", MKB[:, 0, :], cst[:, CST_BDONES:CST_BDONES + 128])
              P.copy("dve", MKB[:, 1, :], cst[:, CST_BDNEG:CST_BDNEG + 128])
              RS_a, TT_, RKa, KEFF = [P.tile([128, TB], F32) for _ in range(4)]
              KKR, BN = TMP, TMP
              SQ = P.tile([128, TB], BF16)
              RK = P.tile([128, TB], BF16)
              BBD, KBD, BHBD, KHBD, VBD = [P.tile([128, NCH, 128], BF16) for _ in range(5)]
              AMCp = P.tile([128, NCH, 256], BF16)
              PP = [P.tile([128, NCH, 2, 128], BF16) for _ in range(2)]
              TTs = P.tile([128, NCH, 128], BF16)
              Xb = P.tile([128, 128], BF16)
              Ub = P.tile([128, 128], BF16)
              Hb = P.tile([128, 128], BF16)
              YC = P.tile([128, TB], F32)
              RSq = P.tile([128, TB], F32)
              YB = P.tile([128, TB], BF16)
              SQ2 = P.tile([128, TB], BF16)
              QS = []
              for _ in range(2):
                  qs = dict(ABD=P.tile([128, NCH, 128], BF16), VTM=P.tile([128, NCH, 128], BF16),
                            BHTM=P.tile([128, NCH, 128], BF16), KHTM=P.tile([128, NCH, 128], BF16),
                            AMCq=P.tile([128, NCH, 256], BF16), TT=P.tile([128, NCH, 128], BF16))
                  QS.append(qs)
              S3 = [dict(RT=P.tile([128, TB], BF16), BON=P.tile([128, TB], BF16), Dq=P.tile([128, NCH, 1], F32))
                    for _ in range(3)]
              ZRr, ZRk, ZRv = ZR
              blocks = [(hp, tb) for hp in range(8) for tb in range(NTB)]
              blkmask = cst[:, CST_BDONES:CST_BDONES + 128]
              m4 = MKB[:, 0, :].rearrange("p (h t) -> p h t", h=2).unsqueeze(1).to_broadcast([128, NCH, 2, CH])
              m4f = blkmask.rearrange("p (h t) -> p h t", h=2).unsqueeze(1).to_broadcast([128, NCH, 2, CH])

              negmask = cst[:, CST_BDNEG:CST_BDNEG + 128]
              m4n = MKB[:, 1, :].rearrange("p (h t) -> p h t", h=2).unsqueeze(1).to_broadcast([128, NCH, 2, CH])

              def bdw(eng, dst, src, neg=False):
                  x4 = src.rearrange("p (c t) -> p c t", t=CH).unsqueeze(2).to_broadcast([128, NCH, 2, CH])
                  mk = m4n if neg else (m4 if src.dtype == BF16 else m4f)
                  P.tt(eng, dst.rearrange("p c (h t) -> p c h t", h=2), x4, mk, ALU.mult)

              def sbank():
                  ps = P.bank(proj_bank[0] % 3)
                  proj_bank[0] += 1
                  return ps

              def genS1(bi):
                  hp, tb = blocks[bi]
                  RT, BONq, Dq = (S3[bi % 3][k] for k in ("RT", "BON", "Dq"))
                  if tb == 0:
                      _rc = {8 + hp: shift_consume(hp, Zt[0]), 16 + hp: shift_consume(8 + hp, Zt[1]),
                             24 + hp: shift_consume(16 + hp, Zt[2])}
                      proj(w_in_l, [8 + hp, 16 + hp] + ([24 + hp] if l == 0 else []), H,
                           lambda ci, chunk, tb_, ps: _rc[chunk](ci, chunk, tb_, ps), pf=3)
                      yield
                      if l == 0:
                          P.dma("sp", vf_d[hp][:, t_half:t_half + HS], ZRv)
                      else:
                          P.dma("sp", ZRv, v1_d[hp][:, t_half:t_half + HS])
                          P.dma("sp", VFT, vf_d[hp][:, t_half:t_half + HS])
                          P.tt("pool", VFT, VFT, ZRv, ALU.subtract)
                          for tb2 in range(NTB):
                              sl2 = slice(tb2 * TB, (tb2 + 1) * TB)
                              ps = sbank()
                              P.mm(ps, WUV[:, hp * 128:(hp + 1) * 128], VDB[:, sl2])
                              P.act(VG, ps, AF.Sigmoid, bias=vcol(l, "v0", hp))
                              P.tt("dve", VFT[:, sl2], VFT[:, sl2], VG, ALU.mult)
                          P.tt("pool", ZRv, ZRv, VFT, ALU.add)
                      yield
                  hs_ = slice(hp * 128, (hp + 1) * 128)
                  sl = slice(tb * TB, (tb + 1) * TB)
                  zr, zk, zv = ZRr[:, sl], ZRk[:, sl], ZRv[:, sl]
                  KK3 = KK.rearrange("p (c t) -> p c t", t=CH)
                  E33 = E3.rearrange("p (c t) -> p c t", t=CH)
                  KA3 = KA.rearrange("p (c t) -> p c t", t=CH)
                  Dv = E33[:, :, CH - 1:CH]
                  ps_w = sbank()
                  P.mm(ps_w, WDA[0:64, hs_], LA[0:64, sl])
                  ps_a = sbank()
                  P.mm(ps_a, WDA[64:128, hs_], LA[64:128, sl])
                  P.act(KKR, zk, AF.Copy, scale=vcol(l, "k_k", hp))
                  P.act(RKa, zr, AF.Copy, scale=vcol(l, "r_k", hp))
                  yield
                  P.act(SG, ps_w, AF.Sigmoid, bias=vcol(l, "w0", hp))
                  P.act(AA, ps_a, AF.Sigmoid, bias=vcol(l, "a0", hp))
                  P.act(SQ, KKR, AF.Square)
                  yield
                  P.scan(LP, cmask, SG, 0.0, ALU.mult, ALU.add)
                  ps_s = sbank()
                  P.mm(ps_s, bdones_b, SQ)
                  P.act(TT_, AA, AF.Identity, bias=OMKA[:, l * 8 + hp:l * 8 + hp + 1], scale=vcol(l, "k_a", hp))
                  yield
                  P.act(RS_a, ps_s, AF.Ln, bias=EPS[:, 2:3])
                  P.act(E2, LP, AF.Exp, scale=C0)
                  P.act(E3, LP, AF.Exp, scale=-C0)
                  yield
                  P.act(RS_a, RS_a, AF.Exp, scale=-0.5)
                  P.tt("pool", KEFF, zk, TT_, ALU.mult)
                  yield
                  P.tt("pool", KK, KKR, RS_a, ALU.mult)
                  P.tt("pool", KT32, KEFF, E2, ALU.mult)
                  P.tt("pool", RK, RKa, KEFF, ALU.mult)
                  P.tt("pool", RT, zr, E3, ALU.mult)
                  P.copy("pool", Dq, Dv)
                  yield
                  P.tt("pool", BN, KK, AA, ALU.mult)
                  ps_b = sbank()
                  P.mm(ps_b, bdones_b, RK)
                  P.tt("pool", KA3[:, :, 1:CH], KK3[:, :, 1:CH], E33[:, :, 0:CH - 1], ALU.mult)
                  P.copy("pool", KA3[:, :, 0:1], KK3[:, :, 0:1])
                  yield
                  P.tt("pool", BT32, BN, E2, ALU.mult)
                  P.tt("dve", BONq, ps_b, zv, ALU.mult)
                  yield

              def stageP(bi):
                  hp, tb = blocks[bi]
                  qs = QS[bi % 2]
                  ABD, VTM, BHTM, KHTM, AMCq, TTq = (qs[k] for k in ("ABD", "VTM", "BHTM", "KHTM", "AMCq", "TT"))
                  RT = S3[bi % 3]["RT"]
                  sl = slice(tb * TB, (tb + 1) * TB)
                  zv = ZRv[:, sl]
                  Dv = E3.rearrange("p (c t) -> p c t", t=CH)[:, :, CH - 1:CH]
                  bdw("dve", ABD, KA, neg=True)
                  bdw("dve", BBD, BT32)
                  bdw("dve", KBD, KT32)
                  bdw("pool", VBD, zv)
                  yield
                  Db = Dv.to_broadcast([128, NCH, 128])
                  P.tt("dve", BHBD, BBD, Db, ALU.mult)
                  P.tt("pool", KHBD, KBD, Db, ALU.mult)
                  yield
                  for src, dst, e_ in ((BHBD, BHTM, "act"), (KHBD, KHTM, "dve"), (VBD, VTM, "act")):
                      pt = P.bank(3, BF16)
                      for c in range(NCH):
                          P.transpose(pt[:, c * 128:(c + 1) * 128], src[:, c, :], ident_b)
                      P.copy(e_, dst, pt[:, 0:1024].rearrange("p (c t) -> p c t", c=NCH))
                      yield
                  for c in range(NCH):
                      ps = P.bank(4 + c % 2)
                      cs = slice(c * CH, (c + 1) * CH)
                      P.mm(ps[:, 0:128], ABD[:, c, :], BBD[:, c, :])
                      P.mm(ps[:, 128:256], BBD[:, c, :], ABD[:, c, :])
                      P.mm(ps[:, 256:384], KBD[:, c, :], ABD[:, c, :])
                      P.mm(ps[:, 384:448], BBD[:, c, :], RT[:, cs])
                      P.mm(ps[:, 448:512], KBD[:, c, :], RT[:, cs])
                      P.tt("dve", AMCp[:, c, :], ps[:, 0:256], maskc[:, 0:256], ALU.mult)
                      P.tt("dve", AMCq[:, c, :], ps[:, 256:512], maskc[:, 256:512], ALU.mult)
                      if c % 2 == 1:
                          yield
                  for c in range(NCH):
                      P.tt("pool", TTs[:, c, :], AMCp[:, c, 128:256], ident_f, ALU.add)
                  TTb = [TTs, TTq]

                  def sq(lev):
                      for c2 in range(NCH // 2):
                          ps = P.bank(4 + c2 % 2)
                          for i in range(2):
                              c = c2 * 2 + i
                              if lev == 1:
                                  p_prev, pt_prev = AMCp[:, c, 0:128], AMCp[:, c, 128:256]
                              else:
                                  p_prev, pt_prev = PP[(lev - 1) % 2][:, c, 0, :], PP[(lev - 1) % 2][:, c, 1, :]
                              P.mm(ps[:, (i * 2) * 128:(i * 2 + 1) * 128], pt_prev, p_prev)
                              if lev < 5:
                                  P.mm(ps[:, (i * 2 + 1) * 128:(i * 2 + 2) * 128], p_prev, pt_prev)
                          e_ = "act" if c2 % 2 == 0 else "dve"
                          if lev < 5:
                              P.copy(e_, PP[lev % 2][:, c2 * 2:c2 * 2 + 2, :, :],
                                     ps.rearrange("p (c a t) -> p c a t", c=2, a=2))
                          else:
                              P.copy(e_, PP[lev % 2][:, c2 * 2:c2 * 2 + 2, 0, :],
                                     ps.rearrange("p (c a t) -> p c a t", c=2, a=2)[:, :, 0, :])
                          if c2 % 2 == 1:
                              yield

                  def ttapply(lev, cur):
                      for c4 in range(NCH // 4):
                          ps = P.bank(3)
                          for i in range(4):
                              c = c4 * 4 + i
                              P.mm(ps[:, i * 128:(i + 1) * 128], PP[lev % 2][:, c, 0, :], TTb[cur][:, c, :])
                          P.tt("dve", TTb[1 - cur][:, c4 * 4:c4 * 4 + 4, :],
                               ps.rearrange("p (c t) -> p c t", c=4), TTb[cur][:, c4 * 4:c4 * 4 + 4, :], ALU.add)
                          yield

                  def inverse():
                      cur = 0
                      yield from sq(1)
                      for lev in range(1, 6):
                          if lev < 5:
                              yield from sq(lev + 1)
                          yield from ttapply(lev, cur)
                          cur = 1 - cur
                      assert cur == 1

                  gi = inverse()
                  gs = genS1(bi + 1) if bi + 1 < len(blocks) else None
                  di, ds = False, gs is None
                  k_ = 0
                  while not (di and ds):
                      if not di:
                          try:
                              next(gi)
                          except StopIteration:
                              di = True
                      k_ += 1
                      if not ds and (di or k_ % S1_EVERY == 0):
                          try:
                              next(gs)
                          except StopIteration:
                              ds = True
                      yield

              def stageQ(bi):
                  hp, tb = blocks[bi]
                  qs = QS[bi % 2]
                  ABD, VTM, BHTM, KHTM, AMCq, TTq = (qs[k] for k in ("ABD", "VTM", "BHTM", "KHTM", "AMCq", "TT"))
                  RT, BONq, Dq = (S3[bi % 3][k] for k in ("RT", "BON", "Dq"))
                  hs_ = slice(hp * 128, (hp + 1) * 128)
                  sl = slice(tb * TB, (tb + 1) * TB)
                  Hs = Hst[l][:, hp, :]
                  P.copy("act", Hb, Hs)
                  psY = P.bank(7)
                  for c in range(NCH):
                      cs = slice(c * CH, (c + 1) * CH)
                      ps = P.bank(6)
                      P.mm(ps[:, 0:128], ABD[:, c, :], Hb, start=True, stop=False)
                      P.mm(ps[:, 0:128], AMCq[:, c, 0:128], VTM[:, c, :], start=False, stop=True)
                      P.copy("act", Xb, ps[:, 0:128])
                      yield
                      P.mm(ps[:, 128:256], TTq[:, c, :], Xb)
                      P.copy("act", Ub, ps[:, 128:256])
                      yield
                      P.mm(psY[:, cs], Hb, RT[:, cs], start=True, stop=False)
                      P.mm(psY[:, cs], Ub, AMCq[:, c, 128:192], start=False, stop=False)
                      P.mm(psY[:, cs], VTM[:, c, :], AMCq[:, c, 192:256], start=False, stop=True)
                      P.mm(ps[:, 256:384], BHTM[:, c, :], Ub, start=True, stop=False)
                      P.mm(ps[:, 256:384], KHTM[:, c, :], VTM[:, c, :], start=False, stop=True)
                      P.stt("dve", Hs, Hs, Dq[:, c, :], ps[:, 256:384], ALU.mult, ALU.add)
                      P.copy("act", Hb, Hs)
                      yield
                  P.copy("act", YC, psY)
                  P.copy("dve", YB, psY)
                  ps = P.bank(6)
                  P.mm(ps, bdones_b, YB)
                  P.stt("dve", YC, ps, -1.0 / 64, YC, ALU.mult, ALU.add)
                  P.act(SQ2, YC, AF.Square)
                  yield
                  ps = P.bank(7)
                  P.mm(ps, bdones_b, SQ2)
                  P.act(RSq, ps, AF.Ln, bias=EPS[:, 1:2], scale=1.0 / 64)
                  P.act(RSq, RSq, AF.Exp, scale=-0.5)
                  P.tt("dve", YC, YC, RSq, ALU.mult)
                  P.ts("dve", YC, YC, vcol(l, "ln_w", hp), vcol(l, "ln_b", hp), ALU.mult, ALU.add)
                  P.tt("pool", YC, YC, BONq, ALU.add)
                  yield
                  ps = P.bank(6)
                  P.mm(ps, WG[:, hs_], SDG[:, sl])
                  P.tt("dve", YR[:, hp, sl], YC, ps, ALU.mult)
                  yield

              def run_gens(gq, gp, ratio=RATIO):
                  dq = gq is None
                  dp = gp is None
                  while not (dq and dp):
                      if not dq:
                          try:
                              next(gq)
                          except StopIteration:
                              dq = True
                      for _ in range(ratio if not dq else 1000000):
                          if dp:
                              break
                          try:
                              next(gp)
                          except StopIteration:
                              dp = True

              rwkv_top[0] = P.mark()
              run_gens(None, genS1(0))
              run_gens(None, stageP(0))
              for bi in range(len(blocks)):
                  run_gens(stageQ(bi), stageP(bi + 1) if bi + 1 < len(blocks) else None)
              P.release(base)
              if (l, hf) == (0, 0):
                  tap("yr0", YR)
              phase("yr%d%d" % (l, hf))

              mP = P.mark()
              BW0 = dict(WPP=P.tile([128, 4, 128], BF16), WPM=P.tile([128, 4, 128], BF16),
                         WPR=P.tile([128, 8, 128], BF16))
              mP1 = P.mark()
              PSETS = [dict(ZP=P.tile([128, 16 + HS], F32), SA=P.tile([128, 16 + HS], F32),
                            SB=P.tile([128, 16 + HS], F32), PLD=P.tile([128, HS], BF16)) for _ in range(2)]
              PW = P.tile([128, 4, 128], BF16)
              load_w(PW, pool_w_d[l].rearrange("g c d -> c g d"))
              QT = P.tile([128, HS], BF16)
              EXs = [P.tile([128, 2, TB], BF16) for _ in range(2)]
              RDs = [P.tile([128, TB], F32) for _ in range(2)]
              pend = []
              sctr = [0]

              def defer(delay, fn):
                  pend.append([sctr[0] + delay, fn])

              def run_due(flush=False):
                  rest = []
                  for due, fn in pend:
                      if flush or due <= sctr[0]:
                          fn()
                      else:
                          rest.append([due, fn])
                  pend[:] = rest

              def pool_tail(g):
                  ZP, SA, SB_, PLD = (PSETS[g % 2][k] for k in ("ZP", "SA", "SB", "PLD"))
                  win = (2, 4, 8, 16)[g]
                  src = ZP
                  step = 1
                  bufs = [SA, SB_]
                  bi = 0
                  while step < win:
                      dst = bufs[bi]
                      bi = 1 - bi
                      lo = 2 * step - 1
                      P.tt("dve" if step in (1, 4) else "pool", dst[:, lo:16 + HS], src[:, lo:16 + HS],
                           src[:, lo - step:16 + HS - step], ALU.add)
                      src = dst
                      step *= 2
                  P.stt("dve", PLD, src[:, 16:16 + HS], 1.0 / win, ZP[:, 16:16 + HS], ALU.mult, ALU.subtract)
                  if hf == 0:
                      n = win - 1
                      P.tt("pool", SA[:, 0:n], src[:, 16:16 + n], invcnt[:, 0:n], ALU.mult)
                      P.tt("pool", PLD[:, 0:n], SA[:, 0:n], ZP[:, 16:16 + n], ALU.subtract)

                  def mix():
                      for tb2 in range(NTB):
                          ps2 = P.bank(3 + tb2)
                          P.mm(ps2, PW[:, g, :], PLD[:, tb2 * TB:(tb2 + 1) * TB])
                          P.ts("dve", PO[:, g, tb2 * TB:(tb2 + 1) * TB], ps2, vcol(l, "pool_b", g),
                               vcol(l, "pool_scale", g), ALU.add, ALU.mult)
                  defer(3, mix)

              def pa_consume(ci, chunk, tb, ps):
                  sctr[0] += 1
                  if chunk < 4:
                      g = chunk
                      ZP = PSETS[g % 2]["ZP"]
                      if tb == 0:
                          P.copy("dve", ZP[:, 0:16], pc[l][:, g, :])
                      P.copy("act", ZP[:, 16 + tb * TB:16 + (tb + 1) * TB], ps)
                      run_due()
                      if tb == NTB - 1:
                          P.copy("dve", pc[l][:, g, :], ZP[:, HS:HS + 16])
                          pool_tail(g)
                      return
                  hh = chunk - 4
                  k = hh * NTB + tb
                  if k == 0:
                      load_w(BW0["WPP"], w_pp_d[l][:, 0:128].rearrange("(kc p) c -> p kc c", p=128))
                      load_w(BW0["WPM"], w_pm_d[l][:, 0:128].rearrange("(kc p) c -> p kc c", p=128))
                      load_w(BW0["WPR"], w_pr_d[l][:, 0:128].rearrange("(kc p) c -> p kc c", p=128))
                  q = QT[:, tb * TB:(tb + 1) * TB]
                  EX, RD = EXs[k % 2], RDs[k % 2]
                  P.copy("act", q, ps)
                  run_due()
                  psd = P.bank(5)
                  pso = P.bank(6 + k % 2)

                  def stB():
                      for mt in range(2):
                          pss = P.bank(3 + mt)
                          P.mm(pss, kT[l][:, hh, mt * 128:(mt + 1) * 128], q)
                          P.act(EX[:, mt, :], pss, AF.Exp, scale=float(128 ** -0.5))

                  def stC():
                      for mt in range(2):
                          P.mm(psd, ones_b, EX[:, mt, :], start=(mt == 0), stop=(mt == 1))
                      for mt in range(2):
                          P.mm(pso, Vt[l][:, mt, hh * 128:(hh + 1) * 128], EX[:, mt, :],
                               start=(mt == 0), stop=(mt == 1))

                  def stD():
                      P.act(RD, psd, AF.Ln)
                      P.act(RD, RD, AF.Exp, scale=-1.0)
                      P.tt("dve", AO[:, hh, tb * TB:(tb + 1) * TB], pso, RD, ALU.mult)
                  defer(1, stB)
                  defer(2, stC)
                  defer(3, stD)

              proj(w_in_l, [0, 1, 2, 3, 4, 5, 6, 7], H, pa_consume)
              for _ in range(4):
                  sctr[0] += 1
                  run_due()
              run_due(flush=True)
              P.release(mP1)
              if (l, hf) == (0, 0):
                  tap("po0", PO)
                  tap("ao0", AO)
              phase("po%d%d" % (l, hf))
              phase("ao%d%d" % (l, hf))

              mM = P.mark()
              WO = P.tile([128, 8, D], BF16)

              def load_wo(g4):
                  load_w(WO[:, :, g4 * 256:(g4 + 1) * 256],
                         w_o_d[l][:, g4 * 256:(g4 + 1) * 256].rearrange("(kc p) c -> p kc c", p=128))
              mM2 = P.mark()
              MS = [dict(GT=[P.tile([128, TB], F32) for _ in range(3)], ACCS=[P.tile([128, TB], F32) for _ in range(NTB)],
                         WPP=BW0["WPP"], WPM=BW0["WPM"], WPR=BW0["WPR"]),
                    dict(GT=[P.tile([128, TB], F32) for _ in range(3)], ACCS=[P.tile([128, TB], F32) for _ in range(NTB)],
                         WPP=P.tile([128, 4, 128], BF16), WPM=P.tile([128, 4, 128], BF16), WPR=P.tile([128, 8, 128], BF16))]
              def load_branch_w(j):
                  ms = MS[j % 2]
                  cs_ = slice(j * 128, (j + 1) * 128)
                  load_w(ms["WPP"], w_pp_d[l][:, cs_].rearrange("(kc p) c -> p kc c", p=128))
                  load_w(ms["WPM"], w_pm_d[l][:, cs_].rearrange("(kc p) c -> p kc c", p=128))
                  load_w(ms["WPR"], w_pr_d[l][:, cs_].rearrange("(kc p) c -> p kc c", p=128))

              def gate_consume(ci, chunk, tb, ps):
                  b = (chunk - 34) // 8
                  j = (chunk - 34) % 8
                  GT, ACCS, WPP, WPM, WPR = (MS[j % 2][k] for k in ("GT", "ACCS", "WPP", "WPM", "WPR"))
                  if b == 0 and tb == 0 and j + 1 < 8:
                      load_branch_w(j + 1)
                  if b == 1 and tb == 0 and 2 <= j < 6:
                      load_wo(j - 2)
                  P.act(GT[b], ps, AF.Sigmoid, bias=vcol(l, "gate_b", b * 8 + j))
                  src, w, nk = ((PO, WPP, 4), (YR, WPR, 8), (AO, WPM, 4))[b]
                  pv = P.bank(3 + b)
                  for kc in range(nk):
                      P.mm(pv, w[:, kc, :], src[:, kc, tb * TB:(tb + 1) * TB], start=(kc == 0), stop=(kc == nk - 1))
                  if b == 0:
                      P.tt("dve", ACCS[tb], pv, GT[b], ALU.mult)
                  else:
                      P.tt("dve", GT[b], pv, GT[b], ALU.mult)
                      if b == 1:
                          P.tt("dve", ACCS[tb], ACCS[tb], GT[b], ALU.add)
                      else:
                          P.tt("dve", M[:, j, tb * TB:(tb + 1) * TB], ACCS[tb], GT[b], ALU.add)

              order = []
              for j in range(8):
                  order += [34 + j, 42 + j, 50 + j]
              proj(w_in_l, order, H, gate_consume)
              P.release(mM2)
              if (l, hf) == (0, 0):
                  tap("merged0", M)
              phase("merged%d%d" % (l, hf))

              mO = P.mark()
              g_post = load_gain(1 + l * 4 + 1)
              g_fpre = load_gain(1 + l * 4 + 2)
              def wo_mm(tt, hh, ps):
                  for kc in range(8):
                      P.mm(ps, M[:, kc, tt * 128:(tt + 1) * 128], WO[:, kc, hh * 512:(hh + 1) * 512],
                           start=(kc == 0), stop=(kc == 7))

              residual_stage(t_half, x_res, xm_d, wo_mm, g_post, g_fpre, H2)
              P.release(base)
              if (l, hf) == (0, 0):
                  tap("h2_0", H2)
              phase("h2%d%d" % (l, hf))

              mF = P.mark()
              FS = [dict(UG=P.tile([128, 2 + HS], F32), UV=P.tile([128, 2 + HS], F32), CG=P.tile([128, HS], F32),
                         CV=P.tile([128, HS], F32)) for _ in range(2)]
              w_fu_l = w_fu_d[l]
              wd_off = P.mark()
              WD = P.tile([128, NFF, D], BF16)

              def load_wd(kc):
                  load_w(WD[:, kc:kc + 2, :],
                         w_fd_d[l][kc * 128:(kc + 2) * 128, :].rearrange("(kc p) c -> p kc c", p=128))

              def conv(U, c, out):
                  cw = [vcol(l, "conv_w", k * 2 * NFF + c) for k in range(3)]
                  P.act(out, U[:, 0:HS], AF.Identity, bias=vcol(l, "conv_b", c), scale=cw[0])
                  P.stt("dve", out, U[:, 1:HS + 1], cw[1], out, ALU.mult, ALU.add)
                  P.stt("dve", out, U[:, 2:HS + 2], cw[2], out, ALU.mult, ALU.add)

              def up_consume(ci, chunk, tb, ps):
                  c = chunk % NFF
                  UG, UV, CG, CV = (FS[c % 2][k] for k in ("UG", "UV", "CG", "CV"))
                  U = UG if chunk < NFF else UV
                  if tb == 0:
                      P.copy("dve", U[:, 0:2], cc[l][:, chunk, :])
                  P.copy("act", U[:, 2 + tb * TB:2 + (tb + 1) * TB], ps)
                  if tb == NTB - 1:
                      P.copy("dve", cc[l][:, chunk, :], U[:, HS:HS + 2])
                      if chunk >= NFF:
                          if c % 2 == 0:
                              load_wd(c)
                          conv(UG, c, CG)
                          conv(UV, NFF + c, CV)

                          def tail(c=c, CG=CG, CV=CV):
                              P.act(CG, CG, AF.Gelu_apprx_tanh)
                              P.tt("dve", ACTB[:, c, :], CG, CV, ALU.mult)
                          up_pending.append(tail)
                      elif up_pending:
                          up_pending.pop(0)()

              order = []
              for c in range(NFF):
                  order += [c, NFF + c]
              up_pending = []
              proj(w_fu_l, order, H2, up_consume)
              while up_pending:
                  up_pending.pop(0)()
              P.release(mF)
              if (l, hf) == (0, 0):
                  tap("act0", ACTB)
              phase("act%d%d" % (l, hf))

              P.release(off_h2)
              g_fpost = load_gain(1 + l * 4 + 3)
              g_next = load_gain(1 + (l + 1) * 4 + 0) if l + 1 < NL else None
              def fd_mm(tt, hh, ps):
                  for kc in range(NFF):
                      P.mm(ps, ACTB[:, kc, tt * 128:(tt + 1) * 128], WD[:, kc, hh * 512:(hh + 1) * 512],
                           start=(kc == 0), stop=(kc == NFF - 1))

              residual_stage(t_half, xm_d, x_lout, fd_mm, g_fpost, g_next, H)
              assert P.mark() <= wd_off, (P.mark(), wd_off)
              P.release(base)

    except _Stop:
        pass
    P.rwkv_top = rwkv_top[0]
    P.emit()
    return nc, P


_CACHE = {}


def host_tables(inp):
    cols = []
    for l in range(NL):
        rows = []
        rows.append(inp["mu_shift"][l].reshape(26, 128))
        rows.append(inp["pool_b"][l].reshape(4, 128))
        rows.append(inp["pool_scale"][l].reshape(4, 128))
        rows.append(inp["w0"][l].reshape(8, 128))
        rows.append(inp["a0"][l].reshape(8, 128))
        rows.append(inp["k_k"][l].reshape(8, 128))
        rows.append(inp["k_a"][l].reshape(8, 128))
        rows.append(inp["r_k"][l].reshape(8, 128))
        rows.append(inp["ln_x_w"][l].reshape(8, 128))
        rows.append(inp["ln_x_b"][l].reshape(8, 128))
        rows.append(inp["v0"][l - 1].reshape(8, 128) if l > 0 else np.zeros((8, 128), np.float32))
        rows.append(inp["gate_b"][l].reshape(24, 128))
        rows.append(inp["conv_w"][l].reshape(3 * 44, 128))
        rows.append(inp["conv_b"][l].reshape(44, 128))
        cols.append(np.concatenate(rows, 0))
    vt = np.ascontiguousarray(np.concatenate(cols, 0).T.astype(np.float32))
    gains = [inp["mem_norm"].reshape(1, D)]
    for l in range(NL):
        gains += [inp["norm_mix_pre"][l:l + 1], inp["norm_mix_post"][l:l + 1],
                  inp["norm_ffn_pre"][l:l + 1], inp["norm_ffn_post"][l:l + 1]]
    gains = np.ascontiguousarray(np.concatenate(gains, 0).astype(np.float32))
    return vt, gains


def make_in_maps(inp, n_cores, S):
    inp = {k: np.asarray(v) for k, v in inp.items()}
    vt, gains = host_tables(inp)
    cst = make_consts()
    shared = dict(cst=cst, vt=vt, gains=gains)
    for k in ("w_in", "pool_w", "w_proj_pool", "w_mem_kv", "w_proj_mem", "w_up_decay", "w_up_a", "w_up_g",
              "w_down_v", "w_up_v", "w_proj_rwkv", "w_o", "w_ffn_up", "w_ffn_down"):
        shared[k] = np.ascontiguousarray(inp[k], dtype=np.float32)
    maps = []
    for b in range(n_cores):
        m = dict(shared)
        m["x"] = np.ascontiguousarray(inp["x"][b, :S], dtype=np.float32)
        m["mem"] = np.ascontiguousarray(inp["mem"][b], dtype=np.float32)
        maps.append(m)
    return maps


def kernel(**inputs):
    S = 2048
    if "nc" not in _CACHE:
        _CACHE["nc"] = build(S)
    nc, P = _CACHE["nc"]
    maps = make_in_maps(inputs, 8, S)
    res = run_bass_kernel_spmd(nc, maps, core_ids=list(range(8)))
    out = np.stack([np.asarray(r["out"], dtype=np.float32) for r in res.results], 0)
    return out
```
